# Optimizing a Trainium2 kernel written in Bass

```python
import jax
import jax.numpy as jnp
from jax import lax
import numpy as np

D_MODEL = 1024
BATCH = 8
SEQ = 4096
DEPTH = 2

CTX_LEN = 256
GRID_W = 64
N_EVEN = (DEPTH + 1) // 2
N_ODD = DEPTH // 2
EPS = 1e-6

MLA_HEADS = 8
QK_NOPE = 64
QK_ROPE = 32
QK_HEAD = QK_NOPE + QK_ROPE
V_HEAD = 64
Q_LORA = 384
KV_LORA = 256
ROPE_BASE = 10000.0
Q_BLOCK = 128

FNET_GROUPS = 4
FNET_GROUP_W = 128
FNET_W = FNET_GROUPS * FNET_GROUP_W

CONF_GROUPS = 4
CONF_W = 512
CONF_WIDTH = 31
SC_W = 512
SC_WIDTH = 3

D_FF = -(-8 * D_MODEL // (3 * 256)) * 256

EVEN_KV_END = Q_LORA + KV_LORA + QK_ROPE
EVEN_IN = EVEN_KV_END + FNET_W
EVEN_MIX = MLA_HEADS * V_HEAD + FNET_W
ODD_IN = 2 * CONF_W + 3 * SC_W
ODD_MIX = CONF_W + SC_W

kernel_name = 'hybrid_mla_fnet_conformer_shortconv_dit'


def rms_norm(x, g):
    xf = x.astype(jnp.float32)
    y = xf * lax.rsqrt(jnp.mean(jnp.square(xf), axis=-1, keepdims=True) + EPS)
    return (y * g.astype(jnp.float32)).astype(x.dtype)


def group_layer_norm(x, g, b, groups):
    bsz, n, w = x.shape
    xf = x.astype(jnp.float32).reshape(bsz, n, groups, w // groups)
    mu = jnp.mean(xf, axis=-1, keepdims=True)
    var = jnp.mean(jnp.square(xf - mu), axis=-1, keepdims=True)
    y = ((xf - mu) * lax.rsqrt(var + EPS)).reshape(bsz, n, w)
    return (y * g.astype(jnp.float32) + b.astype(jnp.float32)).astype(x.dtype)


def modulate(h, shift, scale):
    return h * (1.0 + scale) + shift


def ada_mod(cvec, w, b):
    return jnp.split(jax.nn.silu(cvec) @ w + b, 6, axis=-1)


def axial_rope_tables(n):
    rows = n // GRID_W
    row = jnp.broadcast_to(jnp.arange(rows, dtype=jnp.float32)[:, None], (rows, GRID_W)).reshape(n)
    col = jnp.broadcast_to(jnp.arange(GRID_W, dtype=jnp.float32)[None, :], (rows, GRID_W)).reshape(n)
    per_axis = QK_ROPE // 4
    inv_freq = ROPE_BASE ** (-jnp.arange(per_axis, dtype=jnp.float32) / per_axis)
    ang = jnp.concatenate([row[:, None] * inv_freq, col[:, None] * inv_freq], axis=-1)
    return jnp.cos(ang), jnp.sin(ang)


def rope_tail(t, rope):
    if rope is None:
        return t
    cos, sin = rope
    half = QK_ROPE // 2
    r = t[..., QK_NOPE:].astype(jnp.float32)
    r1, r2 = r[..., :half], r[..., half:]
    rot = jnp.concatenate([r1 * cos - r2 * sin, r2 * cos + r1 * sin], axis=-1).astype(t.dtype)
    return jnp.concatenate([t[..., :QK_NOPE], rot], axis=-1)


def mla_queries(cq, q_ln_g, w_uq, q_norm_g, rope):
    q = jnp.einsum('bsr,rhd->bshd', rms_norm(cq, q_ln_g), w_uq)
    return rope_tail(rms_norm(q, q_norm_g), rope)


def mla_keys_values(ckv, kr, kv_ln_g, w_uk, w_uv, k_norm_g, rope):
    ckv = rms_norm(ckv, kv_ln_g)
    k_nope = jnp.einsum('bsr,rhd->bshd', ckv, w_uk)
    v = jnp.einsum('bsr,rhd->bshd', ckv, w_uv)
    k_rope = jnp.broadcast_to(kr[:, :, None, :], k_nope.shape[:-1] + (QK_ROPE,))
    k = rope_tail(rms_norm(jnp.concatenate([k_nope, k_rope], axis=-1), k_norm_g), rope)
    return k, v


def latent_attention(q, k, v, k_ctx, v_ctx):
    bsz, n, heads, dq = q.shape
    kt = jnp.concatenate([k_ctx, k], axis=1).transpose(0, 2, 1, 3)
    vt = jnp.concatenate([v_ctx, v], axis=1).transpose(0, 2, 1, 3)
    qb = q.reshape(bsz, n // Q_BLOCK, Q_BLOCK, heads, dq).transpose(1, 0, 2, 3, 4)
    scale = dq ** -0.5

    def block(qi):
        s = jnp.einsum('bqhd,bhkd->bhqk', qi, kt).astype(jnp.float32) * scale
        p = jax.nn.softmax(s, axis=-1).astype(vt.dtype)
        return jnp.einsum('bhqk,bhkd->bqhd', p, vt)

    o = lax.map(block, qb)
    return o.transpose(1, 0, 2, 3, 4).reshape(bsz, n, heads * V_HEAD)


def context_attention(q, k, v):
    bsz, n, heads, dq = q.shape
    s = jnp.einsum('bqhd,bkhd->bhqk', q, k).astype(jnp.float32) * (dq ** -0.5)
    p = jax.nn.softmax(s, axis=-1).astype(v.dtype)
    return jnp.einsum('bhqk,bkhd->bqhd', p, v).reshape(bsz, n, heads * V_HEAD)


def fourier_mix(u):
    bsz, n, _ = u.shape
    ug = u.astype(jnp.float32).reshape(bsz, n, FNET_GROUPS, FNET_GROUP_W)
    y = jnp.fft.fft2(ug, axes=(1, 3), norm='ortho').real
    return y.reshape(bsz, n, FNET_W).astype(u.dtype)


def depthwise_conv(u, w):
    width, ch = w.shape
    pad = (width - 1) // 2
    return lax.conv_general_dilated(
        u, w[:, None, :].astype(u.dtype), window_strides=(1,), padding=[(pad, pad)],
        dimension_numbers=('NWC', 'WIO', 'NWC'), feature_group_count=ch)


def odd_mixer(u, conf_dw, conf_dw_b, conf_ln_g, conf_ln_b, sc_dw):
    a, gate, sb, scc, sx = jnp.split(
        u, [CONF_W, 2 * CONF_W, 2 * CONF_W + SC_W, 2 * CONF_W + 2 * SC_W], axis=-1)
    conf = a * jax.nn.sigmoid(gate)
    conf = depthwise_conv(conf, conf_dw) + conf_dw_b
    conf = jax.nn.silu(group_layer_norm(conf, conf_ln_g, conf_ln_b, CONF_GROUPS))
    sc = sb * depthwise_conv(scc * sx, sc_dw)
    return jnp.concatenate([conf, sc], axis=-1)


def swiglu(h, w1, w3, w2):
    return (jax.nn.silu(h @ w1) * (h @ w3)) @ w2


def setup_inputs(seed: int = 0) -> dict:
    key = jax.random.key(seed)
    keys = iter(jax.random.split(key, 40))

    def normal(shape, scale):
        return jax.random.normal(next(keys), shape, jnp.float32) * scale

    def gain(shape):
        return 1.0 + 0.05 * jax.random.normal(next(keys), shape, jnp.float32)

    E, O, L, D = N_EVEN, N_ODD, DEPTH, D_MODEL
    return {
        'x': normal((BATCH, SEQ, D), 1.0),
        'c': normal((BATCH, D), 1.0),
        'ctx': normal((BATCH, CTX_LEN, D), 1.0),
        'c_ctx': normal((D,), 1.0),
        'ada_w': normal((L, D, 6 * D), 0.5 * D ** -0.5),
        'ada_b': normal((L, 6 * D), 0.02),
        'norm1_g': gain((L, D)),
        'norm2_g': gain((L, D)),
        'ffn_w1': normal((L, D, D_FF), D ** -0.5),
        'ffn_w3': normal((L, D, D_FF), D ** -0.5),
        'ffn_w2': normal((L, D_FF, D), D_FF ** -0.5),
        'a_w_in': normal((E, D, EVEN_IN), D ** -0.5),
        'a_q_ln_g': gain((E, Q_LORA)),
        'a_kv_ln_g': gain((E, KV_LORA)),
        'a_w_uq': normal((E, Q_LORA, MLA_HEADS, QK_HEAD), Q_LORA ** -0.5),
        'a_w_uk': normal((E, KV_LORA, MLA_HEADS, QK_NOPE), KV_LORA ** -0.5),
        'a_w_uv': normal((E, KV_LORA, MLA_HEADS, V_HEAD), KV_LORA ** -0.5),
        'a_q_norm_g': gain((E, QK_HEAD)),
        'a_k_norm_g': gain((E, QK_HEAD)),
        'a_w_out': normal((E, EVEN_MIX, D), EVEN_MIX ** -0.5),
        'b_w_in': normal((O, D, ODD_IN), D ** -0.5),
        'b_conf_dw': normal((O, CONF_WIDTH, CONF_W), CONF_WIDTH ** -0.5),
        'b_conf_dw_b': normal((O, CONF_W), 0.02),
        'b_conf_ln_g': gain((O, CONF_W)),
        'b_conf_ln_b': normal((O, CONF_W), 0.02),
        'b_sc_dw': normal((O, SC_WIDTH, SC_W), SC_WIDTH ** -0.5),
        'b_w_out': normal((O, ODD_MIX, D), ODD_MIX ** -0.5),
    }


def reference(x, c, ctx, c_ctx, ada_w, ada_b, norm1_g, norm2_g, ffn_w1, ffn_w3, ffn_w2,
              a_w_in, a_q_ln_g, a_kv_ln_g, a_w_uq, a_w_uk, a_w_uv, a_q_norm_g, a_k_norm_g, a_w_out,
              b_w_in, b_conf_dw, b_conf_dw_b, b_conf_ln_g, b_conf_ln_b, b_sc_dw, b_w_out):
    n = x.shape[1]
    cos, sin = axial_rope_tables(n)
    rope = (cos[:, None, :], sin[:, None, :])
    for i in range(DEPTH):
        ctx_needed = any(l % 2 == 0 for l in range(i + 1, DEPTH))
        j = i // 2
        sh1, sc1, g1, sh2, sc2, g2 = [t[:, None, :] for t in ada_mod(c, ada_w[i], ada_b[i])]
        h = modulate(rms_norm(x, norm1_g[i]), sh1, sc1)
        if i % 2 == 0 or ctx_needed:
            csh1, csc1, cg1, csh2, csc2, cg2 = ada_mod(c_ctx, ada_w[i], ada_b[i])
            hc = modulate(rms_norm(ctx, norm1_g[i]), csh1, csc1)
        if i % 2 == 0:
            w_in = a_w_in[j]
            u = h @ w_in
            ukc = hc @ w_in[:, Q_LORA:EVEN_KV_END]
            k_c, v_c = mla_keys_values(ukc[..., :KV_LORA], ukc[..., KV_LORA:], a_kv_ln_g[j],
                                       a_w_uk[j], a_w_uv[j], a_k_norm_g[j], None)
            k, v = mla_keys_values(u[..., Q_LORA:Q_LORA + KV_LORA], u[..., Q_LORA + KV_LORA:EVEN_KV_END],
                                   a_kv_ln_g[j], a_w_uk[j], a_w_uv[j], a_k_norm_g[j], rope)
            q = mla_queries(u[..., :Q_LORA], a_q_ln_g[j], a_w_uq[j], a_q_norm_g[j], rope)
            o = jnp.concatenate([latent_attention(q, k, v, k_c, v_c), fourier_mix(u[..., EVEN_KV_END:])], axis=-1)
            mix = o @ a_w_out[j]
            if ctx_needed:
                q_c = mla_queries(hc @ w_in[:, :Q_LORA], a_q_ln_g[j], a_w_uq[j], a_q_norm_g[j], None)
                oc = jnp.concatenate([context_attention(q_c, k_c, v_c),
                                      fourier_mix(hc @ w_in[:, EVEN_KV_END:])], axis=-1)
                mix_c = oc @ a_w_out[j]
        else:
            mix = odd_mixer(h @ b_w_in[j], b_conf_dw[j], b_conf_dw_b[j], b_conf_ln_g[j],
                            b_conf_ln_b[j], b_sc_dw[j]) @ b_w_out[j]
            if ctx_needed:
                mix_c = odd_mixer(hc @ b_w_in[j], b_conf_dw[j], b_conf_dw_b[j], b_conf_ln_g[j],
                                  b_conf_ln_b[j], b_sc_dw[j]) @ b_w_out[j]
        x = x + g1 * mix
        x = x + g2 * swiglu(modulate(rms_norm(x, norm2_g[i]), sh2, sc2), ffn_w1[i], ffn_w3[i], ffn_w2[i])
        if ctx_needed:
            ctx = ctx + cg1 * mix_c
            ctx = ctx + cg2 * swiglu(modulate(rms_norm(ctx, norm2_g[i]), csh2, csc2),
                                     ffn_w1[i], ffn_w3[i], ffn_w2[i])
    return x
```

```python
import contextlib
import numpy as np
import concourse.bass as bass
import concourse.mybir as mybir
from concourse.bass_utils import run_bass_kernel_spmd

F32 = mybir.dt.float32
BF16 = mybir.dt.bfloat16
AF = mybir.ActivationFunctionType
ALU = mybir.AluOpType
AX = mybir.AxisListType

N_DMA_SEMS = 40
T = 4096
D = 1024
CTX = 256
NK = CTX + T
H = 8
DFF = 2816
NHC = DFF // 128
EPS = 1e-6
EVEN_IN = 1184
VS = 80
ODD_IN = 2560


class Buf:
    __slots__ = ("name", "w", "r", "excl")

    def __init__(self, name, excl=False):
        self.name = name
        self.w = None
        self.r = {}
        self.excl = excl


class Sched:
    ENGS = ("pe", "act", "dve", "pool", "sp")

    def __init__(self, nc, sems, dsems):
        self.nc = nc
        self.sems = sems
        self.dsems = dsems
        self.q = {e: [] for e in self.ENGS}
        self.cnt = {e: 0 for e in self.ENGS}
        self.seen = {e: {} for e in self.ENGS}
        self.dcnt = [0] * N_DMA_SEMS
        self.dnext = 0
        self.n_ops = 0

    def _deps(self, e, reads, writes, extra=()):
        deps = {}

        def need(ev, same_ok):
            if ev is None:
                return
            k, v = ev
            if k == e and not same_ok:
                return
            if deps.get(k, 0) < v:
                deps[k] = v

        same = e != "pe"
        for b in reads:
            need(b.w, same)
            if b.excl:
                for k, v in b.r.items():
                    need((k, v), False)
        for b in writes:
            need(b.w, same)
            for k, v in b.r.items():
                need((k, v), False)
        for ev in extra:
            need(ev, True)
        waits = []
        sn = self.seen[e]
        for k, v in deps.items():
            if sn.get(k, 0) < v:
                sn[k] = v
                waits.append((k, v))
        return waits

    def op(self, e, fn, reads=(), writes=()):
        waits = self._deps(e, reads, writes)
        self.cnt[e] += 1
        ev = (e, self.cnt[e])
        self.q[e].append((waits, fn, ev))
        for b in reads:
            if b.r.get(e, 0) < ev[1]:
                b.r[e] = ev[1]
        for b in writes:
            b.w = ev
            b.r = {}
        self.n_ops += 1
        return ev

    def dma(self, e, out_ap, in_ap, reads=(), writes=()):
        i = self.dnext
        self.dnext = (self.dnext + 1) % N_DMA_SEMS
        k = ("d", i)
        extra = []
        if self.dcnt[i] > 0:
            extra.append((k, self.dcnt[i]))
        waits = self._deps(e, reads, writes, extra)
        self.dcnt[i] += 16
        ev = (k, self.dcnt[i])

        def fn(eng, out_ap=out_ap, in_ap=in_ap):
            return eng.dma_start(out=out_ap, in_=in_ap)

        self.q[e].append((waits, fn, ev))
        for b in reads:
            b.r[k] = ev[1]
        for b in writes:
            b.w = ev
            b.r = {}
        self.n_ops += 1
        return ev

    def barrier(self):
        for e in self.ENGS:
            waits = []
            sn = self.seen[e]
            for k in self.ENGS:
                if k != e and self.cnt[k] > sn.get(k, 0):
                    sn[k] = self.cnt[k]
                    waits.append((k, self.cnt[k]))
            for i in range(N_DMA_SEMS):
                k = ("d", i)
                if self.dcnt[i] > sn.get(k, 0):
                    sn[k] = self.dcnt[i]
                    waits.append((k, self.dcnt[i]))
            self.q[e].append((waits, None, None))

    def emit(self):
        nc, sems, dsems = self.nc, self.sems, self.dsems

        def semof(k):
            return dsems[k[1]] if isinstance(k, tuple) else sems[k]

        def run(e):
            def body(eng):
                for waits, fn, ev in self.q[e]:
                    for k, v in waits:
                        eng.wait_ge(semof(k), v)
                    if fn is None:
                        continue
                    ins = fn(eng)
                    if isinstance(ev[0], tuple):
                        ins.then_inc(dsems[ev[0][1]], 16)
                    else:
                        ins.then_inc(sems[e], 1)
            return body

        with nc.Block() as block:
            block.tensor(run("pe"))
            block.scalar(run("act"))
            block.vector(run("dve"))
            block.gpsimd(run("pool"))
            block.sync(run("sp"))
        self.q = {e: [] for e in self.ENGS}


class Rot:
    def __init__(self, ph, name, shape, dt, n=2):
        self.items = [ph.sb("%s%d" % (name, i), shape, dt) for i in range(n)]
        self.i = 0

    def next(self):
        it = self.items[self.i % len(self.items)]
        self.i += 1
        return it


class Phase:
    def __init__(self, K, name):
        self.K = K
        self.name = name
        self.st = contextlib.ExitStack()
        self.n = 0

    def sb(self, name, shape, dt):
        self.n += 1
        nm = "%s_%s" % (self.name, name)
        t = self.st.enter_context(self.K.nc.sbuf_tensor(nm, list(shape), dt))
        return t, Buf(nm)

    def close(self):
        self.K.S.barrier()
        self.K.S.emit()
        self.st.close()


class Ctx:
    pass


def mm(K, out_ap, out_b, lhsT, rhs, start, stop, reads):
    K.S.op("pe", lambda e: e.matmul(out_ap, lhsT=lhsT, rhs=rhs, start=start, stop=stop),
           reads=reads, writes=[out_b])


def act(K, out_ap, in_ap, func, reads, writes, **kw):
    K.S.op("act", lambda e: e.activation(out=out_ap, in_=in_ap, func=func, **kw), reads=reads, writes=writes)


def tt(K, eng, out_ap, in0, in1, op, reads, writes):
    K.S.op(eng, lambda e: e.tensor_tensor(out=out_ap, in0=in0, in1=in1, op=op), reads=reads, writes=writes)


def stt(K, eng, out_ap, in0, scalar, in1, op0, op1, reads, writes):
    K.S.op(eng, lambda e: e.scalar_tensor_tensor(out=out_ap, in0=in0, scalar=scalar, in1=in1, op0=op0, op1=op1),
           reads=reads, writes=writes)


def ts(K, eng, out_ap, in0, s1, s2, op0, op1, reads, writes):
    if s2 is None:
        K.S.op(eng, lambda e: e.tensor_scalar(out=out_ap, in0=in0, scalar1=s1, scalar2=None, op0=op0),
               reads=reads, writes=writes)
    else:
        K.S.op(eng, lambda e: e.tensor_scalar(out=out_ap, in0=in0, scalar1=s1, scalar2=s2, op0=op0, op1=op1),
               reads=reads, writes=writes)


def cp(K, eng, out_ap, in_ap, reads, writes):
    if eng == "act":
        K.S.op("act", lambda e: e.copy(out=out_ap, in_=in_ap), reads=reads, writes=writes)
    else:
        K.S.op(eng, lambda e: e.tensor_copy(out=out_ap, in_=in_ap), reads=reads, writes=writes)


def recip(K, ap, b):
    K.S.op("dve", lambda e: e.reciprocal(out=ap, in_=ap), reads=[b], writes=[b])


def load_consts(K, ph):
    S = K.S
    ident, b_ident = ph.sb("ident", [128, 128], BF16)
    S.dma("pool", ident[:], K.c_ident, writes=[b_ident])
    ones, b_ones = ph.sb("ones", [128, 128], BF16)
    S.op("dve", lambda e: e.memset(ones[:], 1.0), writes=[b_ones])
    return ident, b_ident, ones, b_ones


def load_mod(K, ph, name, row, lo, gain=None, tmp=None):
    t, b = ph.sb(name, [128, D], F32)
    reload_mod(K, t, b, row, lo, gain, tmp)
    return t, b


def reload_mod(K, t, b, row, lo, gain=None, tmp=None):
    S = K.S
    S.dma("sp", t[:], K.mod_d[row, lo:lo + D].partition_broadcast(128), reads=[K.b_mod], writes=[b])
    if gain is not None:
        g, bg = tmp
        S.dma("sp", g[:], gain.partition_broadcast(128), writes=[bg])
        stt(K, "dve", t[:], t[:], 1.0, g[:], ALU.add, ALU.mult, [b, bg], [b])


class Normer:
    def __init__(self, K, ph, ident, b_ident, nhb=4):
        self.K = K
        self.ident, self.b_ident = ident, b_ident
        self.xr = Rot(ph, "nx", [128, D], F32, 2)
        self.hb = Rot(ph, "nhb", [128, D], BF16, nhb)
        self.ss = Rot(ph, "nss", [128, 1], F32, nhb)
        self.ptn = 0

    def part1(self, src_rows, A, bA, SH, bSH, rd=()):
        K, S = self.K, self.K.S
        xt, bx = self.xr.next()
        S.dma("sp", xt[:], src_rows, reads=list(rd), writes=[bx])
        hb, bhb = self.hb.next()
        ss, bs = self.ss.next()
        act(K, hb[:], xt[:], AF.Square, [bx], [bhb, bs], accum_out=ss[:])
        act(K, ss[:], ss[:], AF.Sqrt, [bs], [bs], scale=1.0 / D, bias=EPS)
        recip(K, ss[:], bs)
        stt(K, "dve", xt[:], xt[:], ss[:, 0:1], A[:], ALU.mult, ALU.mult, [bx, bs, bA], [bx])
        tt(K, "pool", hb[:], xt[:], SH[:], ALU.add, [bx, bSH], [bhb])
        return hb, bhb

    def part2(self, hb, bhb, hT_dst, b_hT):
        K, S = self.K, self.K.S
        pt, bpt = K.pt[self.ptn % 2]
        self.ptn += 1
        for c in range(8):
            S.op("pe", lambda e, c=c, pt=pt, hb=hb: e.transpose(out=pt[:, c * 128:(c + 1) * 128],
                                                                in_=hb[:, c * 128:(c + 1) * 128],
                                                                identity=self.ident[:]),
                 reads=[bhb, self.b_ident], writes=[bpt])
        cp(K, "act", hT_dst, pt[:].rearrange("p (c n) -> p c n", c=8), [bpt], [b_hT])

    def block(self, src_rows, A, bA, SH, bSH, hT_dst, b_hT, rd=()):
        hb, bhb = self.part1(src_rows, A, bA, SH, bSH, rd)
        self.part2(hb, bhb, hT_dst, b_hT)


def phase_ada(K):
    S = K.S
    ph = Phase(K, "ada")
    cc, b_cc = ph.sb("cc", [128, 8, 2], F32)
    sl, b_sl = ph.sb("sl", [128, 8, 2], BF16)
    S.dma("sp", cc[:], K.cc, writes=[b_cc])
    act(K, sl[:], cc[:], AF.Silu, [b_cc], [b_sl])
    wr = Rot(ph, "w", [128, 8, 512], BF16, 3)
    for L in range(2):
        bias, b_bias = ph.sb("bias%d" % L, [2, 6144], F32)
        S.dma("sp", bias[:], K.ada_b[L, :].partition_broadcast(2), writes=[b_bias])
        rows, b_rows = ph.sb("rows%d" % L, [2, 6144], F32)
        for j in range(12):
            w, bw = wr.next()
            S.dma("pool", w[:], K.ada_w[L, :, j * 512:(j + 1) * 512].rearrange("(k p) n -> p k n", p=128),
                  writes=[bw])
            ps, bps = K.ps()
            for k in range(8):
                mm(K, ps[0:2, :], bps, sl[:, k, :], w[:, k, :], k == 0, k == 7, [b_sl, bw])
            tt(K, "dve", rows[:, j * 512:(j + 1) * 512], ps[0:2, :], bias[:, j * 512:(j + 1) * 512], ALU.add,
               [bps, b_bias], [b_rows])
        S.dma("sp", K.mod_d[L:L + 1, :], rows[0:1, :], reads=[b_rows], writes=[K.b_mod])
        if L == 0:
            S.dma("sp", K.mod_d[2:3, :], rows[1:2, :], reads=[b_rows], writes=[K.b_mod])
    ph.close()


def phase_pre(K):
    S = K.S
    ph = Phase(K, "pre")
    ident, b_ident, ones, b_ones = load_consts(K, ph)
    cv, b_cv = ph.sb("cv", [128, K.NCV], F32)
    S.dma("sp", cv[:], K.colvecs, writes=[b_cv])
    win, b_win = ph.sb("win", [128, 8, EVEN_IN], BF16)
    for k in range(8):
        S.dma("pool", win[:, k, :], K.a_w_in[k * 128:(k + 1) * 128, :], writes=[b_win])
    wtmp, b_wtmp = ph.sb("wtmp", [128, 3, 768], F32)
    wuq, b_wuq = ph.sb("wuq", [128, 3, 768], BF16)
    S.dma("sp", wtmp[:], K.a_w_uq.rearrange("(c p) n -> p c n", p=128), writes=[b_wtmp])
    for c in range(3):
        ts(K, "dve", wuq[:, c, :], wtmp[:, c, :], cv[:, K.CV_QLN + c:K.CV_QLN + c + 1], None, ALU.mult, None,
           [b_wtmp, b_cv], [b_wuq])
    wk, b_wk = ph.sb("wk", [128, 2, 8, 96], BF16)
    S.op("dve", lambda e: e.memset(wk[:], 0.0), writes=[b_wk])
    S.dma("sp", wtmp[:, 0:2, 0:512], K.a_w_uk.rearrange("(c p) n -> p c n", p=128), reads=[], writes=[b_wtmp])
    for c in range(2):
        ts(K, "dve", wk[:, c, :, 0:64], wtmp[:, c, 0:512].rearrange("p (h d) -> p h d", h=8),
           cv[:, K.CV_KVLN + c:K.CV_KVLN + c + 1], None, ALU.mult, None, [b_wtmp, b_cv], [b_wk])
    wuv, b_wuv = ph.sb("wuv", [128, 2, 512], BF16)
    S.dma("sp", wtmp[:, 0:2, 0:512], K.a_w_uv.rearrange("(c p) n -> p c n", p=128), reads=[], writes=[b_wtmp])
    for c in range(2):
        ts(K, "dve", wuv[:, c, :], wtmp[:, c, 0:512], cv[:, K.CV_KVLN + c:K.CV_KVLN + c + 1], None, ALU.mult, None,
           [b_wtmp, b_cv], [b_wuv])
    e32, b_e32 = ph.sb("e32", [128, 96], BF16)
    S.dma("pool", e32[:], K.c_e32, writes=[b_e32])
    rm, b_rm = ph.sb("rm", [96, 96], BF16)
    S.dma("pool", rm[:], K.c_rm, writes=[b_rm])
    csm, b_csm = ph.sb("csm", [128, 256], BF16)
    S.dma("pool", csm[:], K.c_csm, writes=[b_csm])

    A, bA = ph.sb("A", [128, D], F32)
    SH, bSH = ph.sb("SH", [128, D], F32)
    nrm = Normer(K, ph, ident, b_ident)
    hT, b_hT = ph.sb("hT", [128, 8, 512], BF16)
    cfk, b_cfk = ph.sb("cfk", [128, 2, 512], F32)
    cfq, b_cfq = ph.sb("cfq", [128, 3, 512], F32)
    sqk, b_sqk = ph.sb("sqk", [128, 2, 512], BF16)
    sqq, b_sqq = ph.sb("sqq", [128, 3, 512], BF16)
    rsk, b_rsk = ph.sb("rsk", [128, 512], F32)
    rsq, b_rsq = ph.sb("rsq", [128, 512], F32)
    cnk, b_cnk = ph.sb("cnk", [128, 2, 512], BF16)
    cnq, b_cnq = ph.sb("cnq", [128, 3, 512], BF16)
    kr, b_kr = ph.sb("kr", [128, 512], BF16)
    S.op("dve", lambda e: e.memset(kr[:], 0.0), writes=[b_kr])
    vt, b_vt = ph.sb("vt", [128, 4, 8, VS], BF16)
    S.op("dve", lambda e: e.memset(vt[:], 1.0), writes=[b_vt])
    ktt = Rot(ph, "ktt", [96, 8, 512], BF16, 1)
    qtt = Rot(ph, "qtt", [96, 8, 512], BF16, 1)
    ropec = Rot(ph, "ropec", [96, 512], F32, 1)
    ropes = Rot(ph, "ropes", [96, 512], F32, 1)
    sqh = Rot(ph, "sqh", [96, 512], BF16, 3)
    rsh = Rot(ph, "rsh", [96, 512], F32, 2)
    qnb = Rot(ph, "qnb", [96, 512], BF16, 3)
    t1r = Rot(ph, "t1r", [96, 512], F32, 2)
    t2r = Rot(ph, "t2r", [96, 512], F32, 2)
    uf, b_uf = ph.sb("uf", [128, 4, 512], BF16)
    xrt, b_xrt = ph.sb("xrt", [128, 4, 512], BF16)
    xit, b_xit = ph.sb("xit", [128, 4, 512], BF16)

    def proj_chunks(n, col0, nch, cf, b_cf, sq, b_sq):
        for c in range(nch):
            ps, bps = K.ps()
            for k in range(8):
                mm(K, ps[:, 0:n], bps, win[:, k, col0 + c * 128:col0 + (c + 1) * 128], hT[:, k, 0:n], k == 0, k == 7,
                   [b_win, b_hT])
            cp(K, "act", cf[:, c, 0:n], ps[:, 0:n], [bps], [b_cf])
            act(K, sq[:, c, 0:n], ps[:, 0:n], AF.Square, [bps], [b_sq])

    def lat_finish(n, nch, dim, cf, b_cf, sq, b_sq, rs, b_rs, cn, b_cn):
        ps, bps = K.ps()
        for c in range(nch):
            mm(K, ps[:, 0:n], bps, ones[:, :], sq[:, c, 0:n], c == 0, c == nch - 1, [b_ones, b_sq])
        act(K, rs[:, 0:n], ps[:, 0:n], AF.Sqrt, [bps], [b_rs], scale=1.0 / dim, bias=EPS)
        recip(K, rs[:, 0:n], b_rs)
        for c in range(nch):
            tt(K, "dve" if c % 2 == 0 else "pool", cn[:, c, 0:n], cf[:, c, 0:n], rs[:, 0:n], ALU.mult, [b_cf, b_rs],
               [b_cn])

    groups = [("ctx", K.ctx, 0, CTX, 0)] + [("lat", K.x, g * 512, 512, CTX + g * 512) for g in range(8)]

    def p1(gi):
        kind, src, t0, n, koff = groups[gi]
        is_ctx = kind == "ctx"
        if gi <= 1:
            row = 2 if is_ctx else 0
            reload_mod(K, A, bA, row, 1 * D, gain=K.norm1_g[0, :], tmp=nrm.xr.items[0])
            reload_mod(K, SH, bSH, row, 0)
        return [nrm.part1(src[t0 + tb * 128:t0 + (tb + 1) * 128, :], A, bA, SH, bSH) for tb in range(n // 128)]

    pre = p1(0)
    for gi, (kind, src, t0, n, koff) in enumerate(groups):
        nb = n // 128
        is_ctx = kind == "ctx"
        for tb in range(nb):
            nrm.part2(pre[tb][0], pre[tb][1], hT[:, :, tb * 128:(tb + 1) * 128], b_hT)
        proj_chunks(n, 384, 2, cfk, b_cfk, sqk, b_sqk)
        ps, bps = K.ps()
        for k in range(8):
            mm(K, ps[0:32, 0:n], bps, win[:, k, 640:672], hT[:, k, 0:n], k == 0, k == 7, [b_win, b_hT])
        cp(K, "act", kr[0:32, 0:n], ps[0:32, 0:n], [bps], [b_kr])
        if not is_ctx:
            proj_chunks(n, 0, 3, cfq, b_cfq, sqq, b_sqq)
            for g in range(4):
                ps, bps = K.ps()
                for k in range(8):
                    mm(K, ps[:, :], bps, win[:, k, 672 + g * 128:672 + (g + 1) * 128], hT[:, k, :], k == 0, k == 7,
                       [b_win, b_hT])
                cp(K, "act", uf[:, g, :], ps[:, :], [bps], [b_uf])
        if gi + 1 < len(groups):
            pre = p1(gi + 1)
        lat_finish(n, 2, 256, cfk, b_cfk, sqk, b_sqk, rsk, b_rsk, cnk, b_cnk)
        if not is_ctx:
            lat_finish(n, 3, 384, cfq, b_cfq, sqq, b_sqq, rsq, b_rsq, cnq, b_cnq)
            for tb in range(4):
                for gp in range(2):
                    ps, bps = K.ps()
                    for g2 in range(2):
                        g = gp * 2 + g2
                        mm(K, ps[:, g2 * 256:(g2 + 1) * 256], bps, uf[:, g, tb * 128:(tb + 1) * 128], csm[:, :], True,
                           True, [b_uf, b_csm])
                    pv = ps[:, :].rearrange("p (g r c) -> p g r c", g=2, r=2)
                    cp(K, "act", xrt[:, tb, gp * 256:(gp + 1) * 256].rearrange("p (g c) -> p g c", g=2),
                       pv[:, :, 0, :], [bps], [b_xrt])
                    cp(K, "dve", xit[:, tb, gp * 256:(gp + 1) * 256].rearrange("p (g c) -> p g c", g=2),
                       pv[:, :, 1, :], [bps], [b_xit])
            S.dma("pool", K.XR_d[t0:t0 + 512, :].rearrange("(b p) f -> p b f", p=128), xrt[:, :, :], reads=[b_xrt],
                  writes=[K.b_XR])
            S.dma("pool", K.XI_d[t0:t0 + 512, :].rearrange("(b p) f -> p b f", p=128), xit[:, :, :], reads=[b_xit],
                  writes=[K.b_XI])
        for tb in range(nb):
            ps, bps = K.ps()
            for c in range(2):
                mm(K, ps[:, :], bps, cnk[:, c, tb * 128:(tb + 1) * 128], wuv[:, c, :], c == 0, c == 1, [b_cnk, b_wuv])
            cp(K, "act", vt[:, tb, :, 0:64], ps[:, :].rearrange("p (h d) -> p h d", h=8), [bps], [b_vt])
        S.dma("pool", K.V_d[koff:koff + n, :].rearrange("(b p) f -> p b f", p=128),
              vt[:, 0:nb, :, :].rearrange("p b h d -> p b (h d)"), reads=[b_vt], writes=[K.b_V])
        if not is_ctx:
            rc, brc = ropec.next()
            S.dma("sp", rc[:], K.c_ropec[:, t0:t0 + 512], writes=[brc])
            rsn, brsn = ropes.next()
            S.dma("sp", rsn[:], K.c_ropes[:, t0:t0 + 512], writes=[brsn])
        else:
            rc = brc = rsn = brsn = None
        kt, bkt = ktt.next()
        jobs = [("k", h) for h in range(H)]
        if not is_ctx:
            qt, bqt = qtt.next()
            jobs += [("q", h) for h in range(H)]
        st_ = {}

        def stage_a(j):
            kind_, h = jobs[j]
            ps, bps = K.ps()
            if kind_ == "k":
                for c in range(2):
                    mm(K, ps[0:96, 0:n], bps, wk[:, c, h, :], cnk[:, c, 0:n], c == 0, False, [b_wk, b_cnk])
                mm(K, ps[0:96, 0:n], bps, e32[:, :], kr[:, 0:n], False, True, [b_e32, b_kr])
            else:
                for c in range(3):
                    mm(K, ps[0:96, 0:n], bps, wuq[:, c, h * 96:(h + 1) * 96], cnq[:, c, 0:n], c == 0, c == 2,
                       [b_wuq, b_cnq])
            s1, bs1 = sqh.next()
            act(K, s1[:, 0:n], ps[0:96, 0:n], AF.Square, [bps], [bs1])
            st_[j] = dict(ps=ps, bps=bps, s1=s1, bs1=bs1)

        def stage_b(j):
            kind_, h = jobs[j]
            d_ = st_[j]
            gcol = K.CV_KN if kind_ == "k" else K.CV_QN
            dst, b_dst = (kt[:, h, 0:n], bkt) if kind_ == "k" else (qt[:, h, 0:n], bqt)
            psN, bN = K.ps()
            mm(K, psN[0:96, 0:n], bN, ones[0:96, 0:96], d_["s1"][:, 0:n], True, True, [b_ones, d_["bs1"]])
            r1, br1 = rsh.next()
            act(K, r1[:, 0:n], psN[0:96, 0:n], AF.Sqrt, [bN], [br1], scale=1.0 / 96, bias=EPS)
            recip(K, r1[:, 0:n], br1)
            if rc is None:
                stt(K, "dve", dst, d_["ps"][0:96, 0:n], cv[0:96, gcol:gcol + 1], r1[:, 0:n], ALU.mult, ALU.mult,
                    [d_["bps"], b_cv, br1], [b_dst])
                return
            qn, bqn = qnb.next()
            stt(K, "dve", qn[:, 0:n], d_["ps"][0:96, 0:n], cv[0:96, gcol:gcol + 1], r1[:, 0:n], ALU.mult, ALU.mult,
                [d_["bps"], b_cv, br1], [bqn])
            d_["qn"], d_["bqn"] = qn, bqn

        def stage_c(j):
            kind_, h = jobs[j]
            d_ = st_.pop(j)
            if rc is None:
                return
            dst, b_dst = (kt[:, h, 0:n], bkt) if kind_ == "k" else (qt[:, h, 0:n], bqt)
            qn, bqn = d_["qn"], d_["bqn"]
            psR, bR = K.ps()
            mm(K, psR[0:96, 0:n], bR, rm[:, :], qn[:, 0:n], True, True, [b_rm, bqn])
            t1, bt1 = t1r.next()
            tt(K, "pool", t1[:, 0:n], qn[:, 0:n], rc[:, 0:n], ALU.mult, [bqn, brc], [bt1])
            t2, bt2 = t2r.next()
            tt(K, "dve", t2[:, 0:n], psR[0:96, 0:n], rsn[:, 0:n], ALU.mult, [bR, brsn], [bt2])
            tt(K, "pool", dst, t1[:, 0:n], t2[:, 0:n], ALU.add, [bt1, bt2], [b_dst])

        nj = len(jobs)
        for step in range(nj + 2):
            if step < nj:
                stage_a(step)
            if 0 <= step - 1 < nj:
                stage_b(step - 1)
            if 0 <= step - 2 < nj:
                stage_c(step - 2)
        S.dma("pool", K.KT_d[:, :, koff:koff + n], kt[:, :, 0:n], reads=[bkt], writes=[K.b_KT])
        if not is_ctx:
            S.dma("pool", K.QT_d[:, :, t0:t0 + n], qt[:, :, :], reads=[bqt], writes=[K.b_QT])
    ph.close()


def phase_fft(K):
    S = K.S
    ph = Phase(K, "fft")
    wa, b_wa = ph.sb("wa", [128, 64, 128], BF16)
    for q in range(4):
        S.dma("pool", wa[:, q * 16:(q + 1) * 16, :], K.c_wa[q * 16:(q + 1) * 16].rearrange("n k m -> k n m"),
              writes=[b_wa])
    fb, b_fb = ph.sb("fb", [128, 64], BF16)
    S.dma("pool", fb[:], K.c_fb, writes=[b_fb])
    xg, b_xg = ph.sb("xg", [128, 64, 256], BF16)
    gp_, b_gp = ph.sb("gp", [128, 64, 256], BF16)
    yt, b_yt = ph.sb("yt", [128, 2, 4096], BF16)
    for hf in range(2):
        cs = slice(hf * 256, (hf + 1) * 256)
        S.dma("sp", xg[0:64, :, :], K.XR_d[:, cs].rearrange("(a b) c -> a b c", a=64), reads=[K.b_XR], writes=[b_xg])
        S.dma("sp", xg[64:128, :, :], K.XI_d[:, cs].rearrange("(a b) c -> a b c", a=64), reads=[K.b_XI],
              writes=[b_xg])
        for n2 in range(0, 64, 2):
            ps, bps = K.ps()
            for j in range(2):
                mm(K, ps[:, j * 256:(j + 1) * 256], bps, wa[:, n2 + j, :], xg[:, n2 + j, :], True, True, [b_wa, b_xg])
            cp(K, "act" if (n2 // 2) % 2 == 0 else "dve", gp_[:, n2:n2 + 2, :],
               ps[:, :].rearrange("p (j c) -> p j c", j=2), [bps], [b_gp])
        S.dma("sp", K.GD_d[hf].rearrange("r k n c -> (r k) n c"), gp_[:, :, :], reads=[b_gp], writes=[K.b_GD])
        for r in range(2):
            S.dma("sp", xg[r * 64:(r + 1) * 64, :, :], K.GD_d[hf, r].rearrange("k n c -> n k c"), reads=[K.b_GD],
                  writes=[b_xg])
        for gl in range(2):
            for kb in range(8):
                ps, bps = K.ps()
                for j in range(8):
                    k1 = kb * 8 + j
                    mm(K, ps[:, j * 64:(j + 1) * 64], bps, xg[:, k1, gl * 128:(gl + 1) * 128], fb[:, :], True, True,
                       [b_xg, b_fb])
                dst = yt[:, gl, :].rearrange("p (a b) -> p a b", b=64)[:, :, kb * 8:(kb + 1) * 8]
                cp(K, "act" if kb % 2 == 0 else "dve", dst, ps[:, :].rearrange("p (j a) -> p a j", j=8), [bps],
                   [b_yt])
        S.dma("sp", K.YT_d[:, hf * 2:(hf + 1) * 2, :], yt[:, :, :], reads=[b_yt], writes=[K.b_YT])
    ph.close()


def phase_att(K):
    S = K.S
    ph = Phase(K, "att")
    G1, bG1 = load_mod(K, ph, "G1", 0, 2 * D)
    ktall, b_kt = ph.sb("kt", [96, 8, NK], BF16)
    for h in range(H):
        S.dma("sp", ktall[:, h, :], K.KT_d[:, h, :], reads=[K.b_KT], writes=[b_kt])
    vall, b_v = ph.sb("v", [128, 34, 8 * VS], BF16)
    for q in range(2):
        S.dma("sp", vall[:, q * 17:(q + 1) * 17, :],
              K.V_d[q * 17 * 128:(q + 1) * 17 * 128, :].rearrange("(b p) f -> p b f", p=128), reads=[K.b_V],
              writes=[b_v])
    wo_a, b_woa = ph.sb("woa", [128, 4, D], BF16)
    S.dma("pool", wo_a[:], K.a_w_out[0:512, :].rearrange("(j p) n -> p j n", p=128), writes=[b_woa])
    wo_f, b_wof = ph.sb("wof", [128, 4, D], BF16)
    S.dma("pool", wo_f[:], K.a_w_out[512:1024, :].rearrange("(g p) n -> p g n", p=128), writes=[b_wof])
    onesb, b_onesb = ph.sb("onesb", [128, 64], BF16)
    S.op("dve", lambda e: e.memset(onesb[:], 1.0), writes=[b_onesb])
    hlr = Rot(ph, "hl", [65, 1024], BF16, 2)
    qtr = Rot(ph, "qt", [96, 8, 512], BF16, 2)
    ytr = Rot(ph, "ytq", [128, 4, 512], BF16, 2)
    ptr = Rot(ph, "pT", [128, 512], BF16, 3)
    oT, b_oT = ph.sb("oT", [128, 4, 512], BF16)
    oddr = Rot(ph, "oodd", [64, 512], BF16, 2)
    our = Rot(ph, "ou", [64, 512], F32, 2)
    rcp = Rot(ph, "rcp", [65, 512], F32, 2)
    xr = Rot(ph, "x", [128, D], F32, 2)
    tmr = Rot(ph, "tm", [128, 512], F32, 2)
    scale = float(96 ** -0.5)
    nps = [0, 0]

    def ps_o():
        it = K.psl[nps[0] % 2]
        nps[0] += 1
        return it

    def ps_s():
        it = K.psl[2 + nps[1] % 3]
        nps[1] += 1
        return it

    ps_x = K.psl[5]
    LA = 2
    state = {}

    def load_q(qg):
        t0 = qg * 512
        qt, bqt = qtr.next()
        S.dma("sp", qt[:], K.QT_d[:, :, t0:t0 + 512], reads=[K.b_QT], writes=[bqt])
        ytq, bytq = ytr.next()
        S.dma("sp", ytq[:], K.YT_d[:, :, t0:t0 + 512], reads=[K.b_YT], writes=[bytq])
        state[qg] = (qt, bqt, ytq, bytq)

    items = [(qg, h, kc) for qg in range(8) for h in range(H) for kc in range(34)]
    pos = {}
    spend = {}

    def emit_s(i):
        qg, h, kc = items[i]
        if (qg, h) not in pos:
            pos[(qg, h)] = ps_o()
        qt, bqt = state[qg][0], state[qg][1]
        pss, bpss = ps_s()
        mm(K, pss[:, :], bpss, ktall[:, h, kc * 128:(kc + 1) * 128], qt[:, h, :], True, True, [b_kt, bqt])
        spend[i] = (pss, bpss)

    def finalize(qg, h):
        po, bpo = pos.pop((qg, h))
        ou, bou = our.next()
        cp(K, "act", ou[:], po[0:64, :], [bpo], [bou])
        rc, brc = rcp.next()
        S.op("dve", lambda e, rc=rc, po=po: e.reciprocal(out=rc[64:65, :], in_=po[64:65, :]), reads=[bpo],
             writes=[brc])
        hl, bhl = hlr.next()
        cp(K, "dve", hl[64:65, 0:512], rc[64:65, :], [brc], [bhl])
        tt(K, "dve", hl[64:65, 512:1024], rc[64:65, :], hl[64:65, 0:512], ALU.subtract, [brc, bhl], [bhl])
        pb, bpb = ps_x
        mm(K, pb[0:64, :], bpb, onesb[64:65, 0:64], hl[64:65, 0:512], True, False, [b_onesb, bhl])
        mm(K, pb[0:64, :], bpb, onesb[64:65, 0:64], hl[64:65, 512:1024], False, True, [b_onesb, bhl])
        if h % 2 == 0:
            tt(K, "dve", oT[0:64, h // 2, :], ou[:], pb[0:64, :], ALU.mult, [bou, bpb], [b_oT])
        else:
            od, bod = oddr.next()
            tt(K, "dve", od[:], ou[:], pb[0:64, :], ALU.mult, [bou, bpb], [bod])
            S.dma("sp", oT[64:128, h // 2, :], od[:], reads=[bod], writes=[b_oT])

    def outproj(qg):
        t0 = qg * 512
        ytq, bytq = state[qg][2], state[qg][3]
        for tb in range(4):
            xt, bx = xr.next()
            S.dma("sp", xt[:], K.x[t0 + tb * 128:t0 + (tb + 1) * 128, :], writes=[bx])
            for dh in range(2):
                ps, bps = ps_x
                for j in range(4):
                    mm(K, ps[:, :], bps, oT[:, j, tb * 128:(tb + 1) * 128], wo_a[:, j, dh * 512:(dh + 1) * 512],
                       j == 0, False, [b_oT, b_woa])
                for g in range(4):
                    mm(K, ps[:, :], bps, ytq[:, g, tb * 128:(tb + 1) * 128], wo_f[:, g, dh * 512:(dh + 1) * 512],
                       False, g == 3, [bytq, b_wof])
                sl = slice(dh * 512, (dh + 1) * 512)
                tm, btm = tmr.next()
                tt(K, "dve", tm[:], ps[:, :], G1[:, sl], ALU.mult, [bps, bG1], [btm])
                tt(K, "pool", xt[:, sl], xt[:, sl], tm[:], ALU.add, [bx, btm], [bx])
            S.dma("pool", K.X1_d[t0 + tb * 128:t0 + (tb + 1) * 128, :], xt[:], reads=[bx], writes=[K.b_X1])

    load_q(0)
    pending = []
    n_items = len(items)
    for i in range(min(LA, n_items)):
        emit_s(i)
    for i in range(n_items):
        qg, h, kc = items[i]
        if i + LA < n_items:
            q2, h2, k2 = items[i + LA]
            if q2 not in state:
                load_q(q2)
            emit_s(i + LA)
        pss, bpss = spend.pop(i)
        po, bpo = pos[(qg, h)]
        pT, bpT = ptr.next()
        act(K, pT[:], pss[:, :], AF.Exp, [bpss], [bpT], scale=scale)
        mm(K, po[0:65, :], bpo, vall[:, kc, h * VS:h * VS + 65], pT[:], kc == 0, kc == 33, [b_v, bpT])
        if kc == 33:
            pending.append((qg, h))
        if kc == 8 and pending:
            for (pq, phh) in pending:
                finalize(pq, phh)
                if phh == H - 1:
                    outproj(pq)
            pending = []
    for (pq, phh) in pending:
        finalize(pq, phh)
        if phh == H - 1:
            outproj(pq)
    ph.close()


def phase_ffn(K, L, src, b_src, dst, b_dst):
    S = K.S
    ph = Phase(K, "ffn%d" % L)
    ident, b_ident, ones, b_ones = load_consts(K, ph)
    nrm = Normer(K, ph, ident, b_ident)
    A, bA = load_mod(K, ph, "A", L, 4 * D, gain=K.norm2_g[L, :], tmp=nrm.xr.items[0])
    SH, bSH = load_mod(K, ph, "SH", L, 3 * D)
    G2, bG2 = load_mod(K, ph, "G2", L, 5 * D)
    hT, b_hT = ph.sb("hT", [128, 8, 512], BF16)
    gT, b_gT = ph.sb("gT", [128, NHC, 512], BF16)
    sar = Rot(ph, "sa", [128, 512], F32, 2)
    tmr = Rot(ph, "tm", [128, 512], F32, 2)
    def p1(tg):
        t0 = tg * 512
        return [nrm.part1(src[t0 + tb * 128:t0 + (tb + 1) * 128, :], A, bA, SH, bSH, rd=[b_src]) for tb in range(4)]

    pre = p1(0)
    w1, b_w1 = ph.sb("w1", [128, 8, DFF], BF16)
    w3, b_w3 = ph.sb("w3", [128, 8, DFF], BF16)
    w2, b_w2 = ph.sb("w2", [128, NHC, D], BF16)
    b_w1s = [Buf("w1s%d" % i) for i in range(4)]
    b_w3s = [Buf("w3s%d" % i) for i in range(4)]
    b_w2s = [Buf("w2s%d" % i) for i in range(NHC)]
    for q in range(4):
        cs = slice(q * 704, (q + 1) * 704)
        for k in range(8):
            S.dma("pool", w1[:, k, cs], K.ffn_w1[L, k * 128:(k + 1) * 128, cs], writes=[b_w1s[q]])
            S.dma("pool", w3[:, k, cs], K.ffn_w3[L, k * 128:(k + 1) * 128, cs], writes=[b_w3s[q]])
    for hc in range(NHC):
        S.dma("pool", w2[:, hc, :], K.ffn_w2[L, hc * 128:(hc + 1) * 128, :], writes=[b_w2s[hc]])
    for tg in range(8):
        t0 = tg * 512
        for tb in range(4):
            nrm.part2(pre[tb][0], pre[tb][1], hT[:, :, tb * 128:(tb + 1) * 128], b_hT)
        for hc in range(NHC):
            q = (hc * 128) // 704
            q2 = (hc * 128 + 127) // 704
            pa, bpa = K.ps()
            for k in range(8):
                mm(K, pa[:, :], bpa, w1[:, k, hc * 128:(hc + 1) * 128], hT[:, k, :], k == 0, k == 7,
                   [b_w1s[q], b_w1s[q2], b_hT])
            pb, bpb = K.ps()
            for k in range(8):
                mm(K, pb[:, :], bpb, w3[:, k, hc * 128:(hc + 1) * 128], hT[:, k, :], k == 0, k == 7,
                   [b_w3s[q], b_w3s[q2], b_hT])
            sa, bsa = sar.next()
            act(K, sa[:], pa[:, :], AF.Silu, [bpa], [bsa])
            tt(K, "dve", gT[:, hc, :], sa[:], pb[:, :], ALU.mult, [bsa, bpb], [b_gT])
        if tg + 1 < 8:
            pre = p1(tg + 1)
        for tb in range(4):
            xt, bx = nrm.xr.next()
            S.dma("sp", xt[:], src[t0 + tb * 128:t0 + (tb + 1) * 128, :], reads=[b_src], writes=[bx])
            for dh in range(2):
                ps, bps = K.ps()
                for hc in range(NHC):
                    mm(K, ps[:, :], bps, gT[:, hc, tb * 128:(tb + 1) * 128], w2[:, hc, dh * 512:(dh + 1) * 512],
                       hc == 0, hc == NHC - 1, [b_gT, b_w2s[hc]])
                sl = slice(dh * 512, (dh + 1) * 512)
                tm, btm = tmr.next()
                tt(K, "dve", tm[:], ps[:, :], G2[:, sl], ALU.mult, [bps, bG2], [btm])
                tt(K, "pool", xt[:, sl], xt[:, sl], tm[:], ALU.add, [bx, btm], [bx])
            S.dma("pool", dst[t0 + tb * 128:t0 + (tb + 1) * 128, :], xt[:], reads=[bx], writes=[b_dst])
    ph.close()


def phase_mix1(K, src, b_src, dst, b_dst):
    S = K.S
    ph = Phase(K, "mix1")
    ident, b_ident, ones, b_ones = load_consts(K, ph)
    cv, b_cv = ph.sb("cv", [128, K.NCV], F32)
    S.dma("sp", cv[:], K.colvecs, writes=[b_cv])
    conf, b_conf = ph.sb("conf", [128, 4, T + 30], BF16)
    scx, b_scx = ph.sb("scx", [128, 4, T + 2], BF16)
    sbt, b_sbt = ph.sb("sbt", [128, 4, T], BF16)
    S.op("dve", lambda e: e.memset(conf[:, :, 0:15], 0.0), writes=[b_conf])
    S.op("dve", lambda e: e.memset(conf[:, :, T + 15:T + 30], 0.0), writes=[b_conf])
    S.op("dve", lambda e: e.memset(scx[:, :, 0:1], 0.0), writes=[b_scx])
    S.op("dve", lambda e: e.memset(scx[:, :, T + 1:T + 2], 0.0), writes=[b_scx])
    p1 = Phase(K, "mix1a")
    nrm = Normer(K, p1, ident, b_ident)
    A, bA = load_mod(K, p1, "A", 1, 1 * D, gain=K.norm1_g[1, :], tmp=nrm.xr.items[0])
    SH, bSH = load_mod(K, p1, "SH", 1, 0)
    win, b_win = p1.sb("win", [128, 8, ODD_IN], BF16)
    for k in range(8):
        for q in range(2):
            S.dma("pool", win[:, k, q * 1280:(q + 1) * 1280], K.b_w_in[k * 128:(k + 1) * 128, q * 1280:(q + 1) * 1280],
                  writes=[b_win])
    hT, b_hT = p1.sb("hT", [128, 8, 512], BF16)
    sgr = Rot(p1, "sg", [128, 512], F32, 2)
    def pp1(tg):
        t0 = tg * 512
        return [nrm.part1(src[t0 + tb * 128:t0 + (tb + 1) * 128, :], A, bA, SH, bSH, rd=[b_src]) for tb in range(4)]

    pre = pp1(0)
    for tg in range(8):
        t0 = tg * 512
        for tb in range(4):
            nrm.part2(pre[tb][0], pre[tb][1], hT[:, :, tb * 128:(tb + 1) * 128], b_hT)
        if tg + 1 < 8:
            pre = pp1(tg + 1)

        def proj(fc):
            ps, bps = K.ps()
            for k in range(8):
                mm(K, ps[:, :], bps, win[:, k, fc * 128:(fc + 1) * 128], hT[:, k, :], k == 0, k == 7, [b_win, b_hT])
            return ps, bps

        for c in range(4):
            pa, bpa = proj(c)
            pg, bpg = proj(4 + c)
            sg, bsg = sgr.next()
            act(K, sg[:], pg[:, :], AF.Sigmoid, [bpg], [bsg])
            tt(K, "dve", conf[:, c, 15 + t0:15 + t0 + 512], sg[:], pa[:, :], ALU.mult, [bsg, bpa], [b_conf])
            pB, bpB = proj(8 + c)
            cp(K, "act", sbt[:, c, t0:t0 + 512], pB[:, :], [bpB], [b_sbt])
            pC, bpC = proj(12 + c)
            pX, bpX = proj(16 + c)
            sg, bsg = sgr.next()
            cp(K, "act", sg[:], pC[:, :], [bpC], [bsg])
            tt(K, "dve", scx[:, c, 1 + t0:1 + t0 + 512], sg[:], pX[:, :], ALU.mult, [bsg, bpX], [b_scx])
    p1.close()
    p2 = Phase(K, "mix1b")
    G1, bG1 = load_mod(K, p2, "G1", 1, 2 * D)
    identf, b_identf = p2.sb("identf", [128, 128], F32)
    S.dma("sp", identf[:], K.c_ident, writes=[b_identf])
    dg, b_dg = p2.sb("dg", [128, 4, 31, 128], BF16)
    dg2, b_dg2 = p2.sb("dg2", [128, 4, 3, 128], BF16)
    for c in range(4):
        for j in range(31):
            col = K.CV_DW + c * 31 + j
            ts(K, "dve" if j % 2 == 0 else "pool", dg[:, c, j, :], identf[:], cv[:, col:col + 1], None, ALU.mult, None,
               [b_identf, b_cv], [b_dg])
        for j in range(3):
            col = K.CV_SC + c * 3 + j
            ts(K, "dve", dg2[:, c, j, :], identf[:], cv[:, col:col + 1], None, ALU.mult, None, [b_identf, b_cv],
               [b_dg2])
    wo, b_wo = p2.sb("wo", [128, 8, D], BF16)
    for f in range(8):
        S.dma("pool", wo[:, f, :], K.b_w_out[f * 128:(f + 1) * 128, :], writes=[b_wo])
    oTr = Rot(p2, "oT", [128, 8, 512], BF16, 2)
    yr = Rot(p2, "y", [128, 512], F32, 3)
    ybr = Rot(p2, "yb", [128, 512], BF16, 2)
    dr = Rot(p2, "d", [128, 512], F32, 2)
    sqr = Rot(p2, "sq", [128, 512], BF16, 2)
    rsr = Rot(p2, "rs", [128, 512], F32, 2)
    xr = Rot(p2, "x", [128, D], F32, 2)
    tmr = Rot(p2, "tm", [128, 512], F32, 2)
    items = [(tg, c) for tg in range(8) for c in range(4)]
    oTs = {}
    stt_ = {}

    def st_a(i):
        tg, c = items[i]
        t0 = tg * 512
        if tg not in oTs:
            oTs[tg] = oTr.next()
        oT, b_oT = oTs[tg]
        ps, bps = K.ps()
        for j in range(31):
            mm(K, ps[:, :], bps, dg[:, c, j, :], conf[:, c, t0 + j:t0 + j + 512], j == 0, j == 30, [b_dg, b_conf])
        y, by = yr.next()
        act(K, y[:], ps[:, :], AF.Identity, [bps, b_cv], [by], bias=cv[:, K.CV_DWB + c:K.CV_DWB + c + 1])
        yb, byb = ybr.next()
        cp(K, "pool", yb[:], y[:], [by], [byb])
        ps2, bps2 = K.ps()
        for j in range(3):
            mm(K, ps2[:, :], bps2, dg2[:, c, j, :], scx[:, c, t0 + j:t0 + j + 512], j == 0, j == 2, [b_dg2, b_scx])
        tt(K, "dve", oT[:, 4 + c, :], sbt[:, c, t0:t0 + 512], ps2[:, :], ALU.mult, [b_sbt, bps2], [b_oT])
        stt_[i] = dict(y=y, by=by, yb=yb, byb=byb)

    def st_b(i):
        d_ = stt_[i]
        pm, bpm = K.ps()
        mm(K, pm[:, :], bpm, ones[:, :], d_["yb"][:], True, True, [b_ones, d_["byb"]])
        d, bd = dr.next()
        stt(K, "dve", d[:], pm[:, :], -1.0 / 128, d_["y"][:], ALU.mult, ALU.add, [bpm, d_["by"]], [bd])
        sq, bsq = sqr.next()
        act(K, sq[:], d[:], AF.Square, [bd], [bsq])
        d_.update(d=d, bd=bd, sq=sq, bsq=bsq)

    def st_c(i):
        tg, c = items[i]
        oT, b_oT = oTs[tg]
        d_ = stt_.pop(i)
        d, bd = d_["d"], d_["bd"]
        pv, bpv = K.ps()
        mm(K, pv[:, :], bpv, ones[:, :], d_["sq"][:], True, True, [b_ones, d_["bsq"]])
        rs, brs = rsr.next()
        act(K, rs[:], pv[:, :], AF.Sqrt, [bpv], [brs], scale=1.0 / 128, bias=EPS)
        recip(K, rs[:], brs)
        tt(K, "dve", d[:], d[:], rs[:], ALU.mult, [bd, brs], [bd])
        act(K, oT[:, c, :], d[:], AF.Silu, [bd, b_cv], [b_oT], scale=cv[:, K.CV_LNG + c:K.CV_LNG + c + 1],
            bias=cv[:, K.CV_LNB + c:K.CV_LNB + c + 1])

    def outproj(tg):
        t0 = tg * 512
        oT, b_oT = oTs.pop(tg)
        for tb in range(4):
            xt, bx = xr.next()
            S.dma("sp", xt[:], src[t0 + tb * 128:t0 + (tb + 1) * 128, :], reads=[b_src], writes=[bx])
            for dh in range(2):
                ps, bps = K.ps()
                for f in range(8):
                    mm(K, ps[:, :], bps, oT[:, f, tb * 128:(tb + 1) * 128], wo[:, f, dh * 512:(dh + 1) * 512], f == 0,
                       f == 7, [b_oT, b_wo])
                sl = slice(dh * 512, (dh + 1) * 512)
                tm, btm = tmr.next()
                tt(K, "dve", tm[:], ps[:, :], G1[:, sl], ALU.mult, [bps, bG1], [btm])
                tt(K, "pool", xt[:, sl], xt[:, sl], tm[:], ALU.add, [bx, btm], [bx])
            S.dma("pool", dst[t0 + tb * 128:t0 + (tb + 1) * 128, :], xt[:], reads=[bx], writes=[b_dst])

    ni = len(items)
    for step in range(ni + 4):
        if step < ni:
            st_a(step)
        if 0 <= step - 1 < ni:
            st_b(step - 1)
        if 0 <= step - 2 < ni:
            st_c(step - 2)
        j = step - 4
        if 0 <= j < ni and items[j][1] == 3:
            outproj(items[j][0])
    p2.close()
    ph.close()


CV_LAYOUT = {}


def colvec_layout():
    off = 0
    lay = {}
    for name, n in (("QLN", 3), ("KVLN", 2), ("QN", 1), ("KN", 1), ("DW", 4 * 31), ("DWB", 4), ("LNG", 4),
                    ("LNB", 4), ("SC", 4 * 3)):
        lay[name] = off
        off += n
    return lay, off


def build(phases=("ada", "pre", "fft", "att", "ffn0", "mix1", "ffn1"), dbg=False):
    nc = bass.Bass("TRN2", target_bir_lowering=False)
    K = Ctx()
    K.nc = nc

    def din(name, shape):
        return nc.dram_tensor(name, list(shape), F32, kind="ExternalInput").ap()

    def dscr(name, shape, dt):
        kind = "ExternalOutput" if dbg else "Internal"
        return nc.dram_tensor(name, list(shape), dt, kind=kind).ap()

    K.x = din("x", [T, D])
    K.ctx = din("ctx", [CTX, D])
    K.cc = din("cc", [128, 8, 2])
    K.ada_w = din("ada_w", [2, D, 6 * D])
    K.ada_b = din("ada_b", [2, 6 * D])
    K.norm1_g = din("norm1_g", [2, D])
    K.norm2_g = din("norm2_g", [2, D])
    K.ffn_w1 = din("ffn_w1", [2, D, DFF])
    K.ffn_w3 = din("ffn_w3", [2, D, DFF])
    K.ffn_w2 = din("ffn_w2", [2, DFF, D])
    K.a_w_in = din("a_w_in", [D, EVEN_IN])
    K.a_w_uq = din("a_w_uq", [384, 768])
    K.a_w_uk = din("a_w_uk", [256, 512])
    K.a_w_uv = din("a_w_uv", [256, 512])
    K.a_w_out = din("a_w_out", [D, D])
    K.b_w_in = din("b_w_in", [D, ODD_IN])
    K.b_w_out = din("b_w_out", [D, D])
    lay, ncv = colvec_layout()
    K.NCV = ncv
    for k_, v_ in lay.items():
        setattr(K, "CV_" + k_, v_)
    K.colvecs = din("colvecs", [128, ncv])
    K.c_ident = din("c_ident", [128, 128])
    K.c_e32 = din("c_e32", [128, 96])
    K.c_rm = din("c_rm", [96, 96])
    K.c_csm = din("c_csm", [128, 256])
    K.c_ropec = din("c_ropec", [96, T])
    K.c_ropes = din("c_ropes", [96, T])
    K.c_wa = din("c_wa", [64, 128, 128])
    K.c_fb = din("c_fb", [128, 64])
    K.out = nc.dram_tensor("out", [T, D], F32, kind="ExternalOutput").ap()
    K.b_out = Buf("out")

    K.mod_d = dscr("mod_d", [3, 6 * D], F32)
    K.b_mod = Buf("mod_d")
    K.QT_d = dscr("QT_d", [96, 8, T], BF16)
    K.b_QT = Buf("QT_d")
    K.KT_d = dscr("KT_d", [96, 8, NK], BF16)
    K.b_KT = Buf("KT_d")
    K.V_d = dscr("V_d", [NK, 8 * VS], BF16)
    K.b_V = Buf("V_d")
    K.XR_d = dscr("XR_d", [T, 512], BF16)
    K.b_XR = Buf("XR_d")
    K.XI_d = dscr("XI_d", [T, 512], BF16)
    K.b_XI = Buf("XI_d")
    K.GD_d = dscr("GD_d", [2, 2, 64, 64, 256], BF16)
    K.b_GD = Buf("GD_d")
    K.YT_d = dscr("YT_d", [128, 4, T], BF16)
    K.b_YT = Buf("YT_d")
    K.X1_d = dscr("X1_d", [T, D], F32)
    K.b_X1 = Buf("X1_d")
    K.X2_d = dscr("X2_d", [T, D], F32)
    K.b_X2 = Buf("X2_d")
    K.X3_d = dscr("X3_d", [T, D], F32)
    K.b_X3 = Buf("X3_d")

    with contextlib.ExitStack() as st:
        sems = {e: st.enter_context(nc.semaphore("s_" + e)) for e in Sched.ENGS}
        dsems = [st.enter_context(nc.semaphore("d%d" % i)) for i in range(N_DMA_SEMS)]
        K.S = Sched(nc, sems, dsems)
        psl = []
        for i in range(6):
            psl.append((st.enter_context(nc.psum_tensor("ps%d" % i, [128, 512], F32)), Buf("ps%d" % i, True)))
        K.pt = []
        for i in range(2):
            K.pt.append((st.enter_context(nc.psum_tensor("pt%d" % i, [128, 1024], BF16)), Buf("pt%d" % i, True)))
        K.psi = 0

        def ps():
            it = psl[K.psi % 6]
            K.psi += 1
            return it

        K.ps = ps
        K.psl = psl
        if "ada" in phases:
            phase_ada(K)
        if "pre" in phases:
            phase_pre(K)
        if "fft" in phases:
            phase_fft(K)
        if "att" in phases:
            phase_att(K)
        if "ffn0" in phases:
            phase_ffn(K, 0, K.X1_d, K.b_X1, K.X2_d, K.b_X2)
        if "mix1" in phases:
            phase_mix1(K, K.X2_d, K.b_X2, K.X3_d, K.b_X3)
        if "ffn1" in phases:
            phase_ffn(K, 1, K.X3_d, K.b_X3, K.out, K.b_out)
        K.S.barrier()
        K.S.emit()
    return nc


def host_consts():
    c = {}
    c["c_ident"] = np.eye(128, dtype=np.float32)
    e32 = np.zeros((128, 96), np.float32)
    e32[np.arange(32), 64 + np.arange(32)] = 1.0
    c["c_e32"] = e32
    rm = np.zeros((96, 96), np.float32)
    for i in range(16):
        rm[80 + i, 64 + i] = -1.0
        rm[64 + i, 80 + i] = 1.0
    c["c_rm"] = rm
    m = np.arange(128, dtype=np.float64)
    ang = 2 * np.pi * np.outer(m, m) / 128.0
    c["c_csm"] = np.concatenate([np.cos(ang), -np.sin(ang)], axis=1).astype(np.float32)
    rows = (np.arange(T) // 64).astype(np.float32)
    cols = (np.arange(T) % 64).astype(np.float32)
    inv = (np.float32(10000.0) ** (-np.arange(8, dtype=np.float32) / np.float32(8))).astype(np.float32)
    angp = np.concatenate([rows[:, None] * inv, cols[:, None] * inv], axis=-1).astype(np.float32)
    cos = np.cos(angp.astype(np.float64)).T
    sin = np.sin(angp.astype(np.float64)).T
    rc = np.ones((96, T), np.float64)
    rs = np.zeros((96, T), np.float64)
    rc[64:80] = cos
    rc[80:96] = cos
    rs[64:80] = sin
    rs[80:96] = sin
    c["c_ropec"] = rc.astype(np.float32)
    c["c_ropes"] = rs.astype(np.float32)
    n1 = np.arange(64, dtype=np.float64)
    k1 = np.arange(64, dtype=np.float64)
    wa = np.zeros((64, 128, 128), np.float64)
    for n2 in range(64):
        th = 2 * np.pi * (np.outer(n1, k1) / 64.0 + (n2 * k1)[None, :] / 4096.0)
        wa[n2, 0:64, 0:64] = np.cos(th)
        wa[n2, 64:128, 0:64] = np.sin(th)
        wa[n2, 0:64, 64:128] = -np.sin(th)
        wa[n2, 64:128, 64:128] = np.cos(th)
    c["c_wa"] = wa.astype(np.float32)
    n2 = np.arange(64, dtype=np.float64)
    k2 = np.arange(64, dtype=np.float64)
    ph = 2 * np.pi * np.outer(n2, k2) / 64.0
    s = 1.0 / np.sqrt(4096.0 * 128.0)
    c["c_fb"] = (np.concatenate([np.cos(ph), np.sin(ph)], axis=0) * s).astype(np.float32)
    return c


def make_in_maps(inputs):
    f = lambda a: np.ascontiguousarray(np.asarray(a, dtype=np.float32))
    lay, ncv = colvec_layout()
    cvs = np.zeros((128, ncv), np.float32)
    cvs[:, lay["QLN"]:lay["QLN"] + 3] = f(inputs["a_q_ln_g"]).reshape(3, 128).T
    cvs[:, lay["KVLN"]:lay["KVLN"] + 2] = f(inputs["a_kv_ln_g"]).reshape(2, 128).T
    cvs[0:96, lay["QN"]] = f(inputs["a_q_norm_g"]).reshape(96)
    cvs[0:96, lay["KN"]] = f(inputs["a_k_norm_g"]).reshape(96)
    dw = f(inputs["b_conf_dw"]).reshape(31, 4, 128)
    cvs[:, lay["DW"]:lay["DW"] + 124] = dw.transpose(2, 1, 0).reshape(128, 124)
    cvs[:, lay["DWB"]:lay["DWB"] + 4] = f(inputs["b_conf_dw_b"]).reshape(4, 128).T
    cvs[:, lay["LNG"]:lay["LNG"] + 4] = f(inputs["b_conf_ln_g"]).reshape(4, 128).T
    cvs[:, lay["LNB"]:lay["LNB"] + 4] = f(inputs["b_conf_ln_b"]).reshape(4, 128).T
    sc = f(inputs["b_sc_dw"]).reshape(3, 4, 128)
    cvs[:, lay["SC"]:lay["SC"] + 12] = sc.transpose(2, 1, 0).reshape(128, 12)
    shared = {
        "ada_w": f(inputs["ada_w"]), "ada_b": f(inputs["ada_b"]),
        "norm1_g": f(inputs["norm1_g"]), "norm2_g": f(inputs["norm2_g"]),
        "ffn_w1": f(inputs["ffn_w1"]), "ffn_w3": f(inputs["ffn_w3"]), "ffn_w2": f(inputs["ffn_w2"]),
        "a_w_in": f(inputs["a_w_in"]).reshape(D, EVEN_IN),
        "a_w_uq": f(inputs["a_w_uq"]).reshape(384, 768),
        "a_w_uk": f(inputs["a_w_uk"]).reshape(256, 512),
        "a_w_uv": f(inputs["a_w_uv"]).reshape(256, 512),
        "a_w_out": f(inputs["a_w_out"]).reshape(D, D),
        "b_w_in": f(inputs["b_w_in"]).reshape(D, ODD_IN),
        "b_w_out": f(inputs["b_w_out"]).reshape(D, D),
        "colvecs": cvs,
    }
    shared.update(host_consts())
    x = f(inputs["x"])
    ctx = f(inputs["ctx"])
    c = f(inputs["c"])
    c_ctx = f(inputs["c_ctx"]).reshape(D)
    maps = []
    for b in range(x.shape[0]):
        cc = np.stack([c[b].reshape(8, 128).T, c_ctx.reshape(8, 128).T], axis=-1)
        m = dict(shared)
        m["x"] = np.ascontiguousarray(x[b])
        m["ctx"] = np.ascontiguousarray(ctx[b])
        m["cc"] = np.ascontiguousarray(cc)
        maps.append(m)
    return maps


def kernel(**inputs):
    maps = make_in_maps(inputs)
    nc = build()
    res = run_bass_kernel_spmd(nc, maps, core_ids=list(range(len(maps))))
    return np.stack([np.asarray(r["out"], dtype=np.float32) for r in res.results], axis=0)
```

```python
import contextlib
import numpy as np
import concourse.bass as bass
import concourse.mybir as mybir
from concourse.bass_utils import run_bass_kernel_spmd

F32 = mybir.dt.float32
BF16 = mybir.dt.bfloat16
AF = mybir.ActivationFunctionType
ALU = mybir.AluOpType
AX = mybir.AxisListType

N_DMA_SEMS = 40
T = 4096
D = 1024
CTX = 256
NK = CTX + T
H = 8
DFF = 2816
NHC = DFF // 128
EPS = 1e-6
EVEN_IN = 1184
VS = 80
ODD_IN = 2560


class Buf:
    __slots__ = ("name", "w", "r", "excl")

    def __init__(self, name, excl=False):
        self.name = name
        self.w = None
        self.r = {}
        self.excl = excl


class Sched:
    ENGS = ("pe", "act", "dve", "pool", "sp")

    def __init__(self, nc, sems, dsems):
        self.nc = nc
        self.sems = sems
        self.dsems = dsems
        self.q = {e: [] for e in self.ENGS}
        self.cnt = {e: 0 for e in self.ENGS}
        self.seen = {e: {} for e in self.ENGS}
        self.dcnt = [0] * N_DMA_SEMS
        self.dnext = 0
        self.n_ops = 0

    def _deps(self, e, reads, writes, extra=()):
        deps = {}

        def need(ev, same_ok):
            if ev is None:
                return
            k, v = ev
            if k == e and not same_ok:
                return
            if deps.get(k, 0) < v:
                deps[k] = v

        same = e != "pe"
        for b in reads:
            need(b.w, same)
            if b.excl:
                for k, v in b.r.items():
                    need((k, v), False)
        for b in writes:
            need(b.w, same)
            for k, v in b.r.items():
                need((k, v), False)
        for ev in extra:
            need(ev, True)
        waits = []
        sn = self.seen[e]
        for k, v in deps.items():
            if sn.get(k, 0) < v:
                sn[k] = v
                waits.append((k, v))
        return waits

    def op(self, e, fn, reads=(), writes=()):
        waits = self._deps(e, reads, writes)
        self.cnt[e] += 1
        ev = (e, self.cnt[e])
        self.q[e].append((waits, fn, ev))
        for b in reads:
            if b.r.get(e, 0) < ev[1]:
                b.r[e] = ev[1]
        for b in writes:
            b.w = ev
            b.r = {}
        self.n_ops += 1
        return ev

    def dma(self, e, out_ap, in_ap, reads=(), writes=()):
        i = self.dnext
        self.dnext = (self.dnext + 1) % N_DMA_SEMS
        k = ("d", i)
        extra = []
        if self.dcnt[i] > 0:
            extra.append((k, self.dcnt[i]))
        waits = self._deps(e, reads, writes, extra)
        self.dcnt[i] += 16
        ev = (k, self.dcnt[i])

        def fn(eng, out_ap=out_ap, in_ap=in_ap):
            return eng.dma_start(out=out_ap, in_=in_ap)

        self.q[e].append((waits, fn, ev))
        for b in reads:
            b.r[k] = ev[1]
        for b in writes:
            b.w = ev
            b.r = {}
        self.n_ops += 1
        return ev

    def barrier(self):
        for e in self.ENGS:
            waits = []
            sn = self.seen[e]
            for k in self.ENGS:
                if k != e and self.cnt[k] > sn.get(k, 0):
                    sn[k] = self.cnt[k]
                    waits.append((k, self.cnt[k]))
            for i in range(N_DMA_SEMS):
                k = ("d", i)
                if self.dcnt[i] > sn.get(k, 0):
                    sn[k] = self.dcnt[i]
                    waits.append((k, self.dcnt[i]))
            self.q[e].append((waits, None, None))

    def emit(self):
        nc, sems, dsems = self.nc, self.sems, self.dsems

        def semof(k):
            return dsems[k[1]] if isinstance(k, tuple) else sems[k]

        def run(e):
            def body(eng):
                for waits, fn, ev in self.q[e]:
                    for k, v in waits:
                        eng.wait_ge(semof(k), v)
                    if fn is None:
                        continue
                    ins = fn(eng)
                    if isinstance(ev[0], tuple):
                        ins.then_inc(dsems[ev[0][1]], 16)
                    else:
                        ins.then_inc(sems[e], 1)
            return body

        with nc.Block() as block:
            block.tensor(run("pe"))
            block.scalar(run("act"))
            block.vector(run("dve"))
            block.gpsimd(run("pool"))
            block.sync(run("sp"))
        self.q = {e: [] for e in self.ENGS}


class Rot:
    def __init__(self, ph, name, shape, dt, n=2):
        self.items = [ph.sb("%s%d" % (name, i), shape, dt) for i in range(n)]
        self.i = 0

    def next(self):
        it = self.items[self.i % len(self.items)]
        self.i += 1
        return it


class Phase:
    def __init__(self, K, name):
        self.K = K
        self.name = name
        self.st = contextlib.ExitStack()
        self.n = 0

    def sb(self, name, shape, dt):
        self.n += 1
        nm = "%s_%s" % (self.name, name)
        t = self.st.enter_context(self.K.nc.sbuf_tensor(nm, list(shape), dt))
        return t, Buf(nm)

    def close(self):
        self.K.S.barrier()
        self.K.S.emit()
        self.st.close()


class Ctx:
    pass


def mm(K, out_ap, out_b, lhsT, rhs, start, stop, reads):
    K.S.op("pe", lambda e: e.matmul(out_ap, lhsT=lhsT, rhs=rhs, start=start, stop=stop),
           reads=reads, writes=[out_b])


def act(K, out_ap, in_ap, func, reads, writes, **kw):
    K.S.op("act", lambda e: e.activation(out=out_ap, in_=in_ap, func=func, **kw), reads=reads, writes=writes)


def tt(K, eng, out_ap, in0, in1, op, reads, writes):
    K.S.op(eng, lambda e: e.tensor_tensor(out=out_ap, in0=in0, in1=in1, op=op), reads=reads, writes=writes)


def stt(K, eng, out_ap, in0, scalar, in1, op0, op1, reads, writes):
    K.S.op(eng, lambda e: e.scalar_tensor_tensor(out=out_ap, in0=in0, scalar=scalar, in1=in1, op0=op0, op1=op1),
           reads=reads, writes=writes)


def ts(K, eng, out_ap, in0, s1, s2, op0, op1, reads, writes):
    if s2 is None:
        K.S.op(eng, lambda e: e.tensor_scalar(out=out_ap, in0=in0, scalar1=s1, scalar2=None, op0=op0),
               reads=reads, writes=writes)
    else:
        K.S.op(eng, lambda e: e.tensor_scalar(out=out_ap, in0=in0, scalar1=s1, scalar2=s2, op0=op0, op1=op1),
               reads=reads, writes=writes)


def cp(K, eng, out_ap, in_ap, reads, writes):
    if eng == "act":
        K.S.op("act", lambda e: e.copy(out=out_ap, in_=in_ap), reads=reads, writes=writes)
    else:
        K.S.op(eng, lambda e: e.tensor_copy(out=out_ap, in_=in_ap), reads=reads, writes=writes)


def rstd_act(K, out_ap, in_ap, scale, reads, b_out):
    act(K, out_ap, in_ap, AF.Ln, reads, [b_out], scale=scale, bias=EPS)
    act(K, out_ap, out_ap, AF.Exp, [b_out], [b_out], scale=-0.5)


def recip(K, ap, b):
    K.S.op("dve", lambda e: e.reciprocal(out=ap, in_=ap), reads=[b], writes=[b])


def load_consts(K, ph):
    S = K.S
    ident, b_ident = ph.sb("ident", [128, 128], BF16)
    S.dma("pool", ident[:], K.c_ident, writes=[b_ident])
    ones, b_ones = ph.sb("ones", [128, 128], BF16)
    S.op("dve", lambda e: e.memset(ones[:], 1.0), writes=[b_ones])
    return ident, b_ident, ones, b_ones


def load_mod(K, ph, name, row, lo, gain=None, tmp=None):
    t, b = ph.sb(name, [128, D], F32)
    reload_mod(K, t, b, row, lo, gain, tmp)
    return t, b


def reload_mod(K, t, b, row, lo, gain=None, tmp=None):
    S = K.S
    S.dma("sp", t[:], K.mod_d[row, lo:lo + D].partition_broadcast(128), reads=[K.b_mod], writes=[b])
    if gain is not None:
        g, bg = tmp
        S.dma("sp", g[:], gain.partition_broadcast(128), writes=[bg])
        stt(K, "dve", t[:], t[:], 1.0, g[:], ALU.add, ALU.mult, [b, bg], [b])


class Normer:
    def __init__(self, K, ph, ident, b_ident, nhb=4):
        self.K = K
        self.ident, self.b_ident = ident, b_ident
        self.xr = Rot(ph, "nx", [128, D], F32, 2)
        self.hb = Rot(ph, "nhb", [128, D], BF16, nhb)
        self.ss = Rot(ph, "nss", [128, 1], F32, nhb)
        self.ptn = 0

    def part1(self, src_rows, A, bA, SH, bSH, rd=()):
        K, S = self.K, self.K.S
        xt, bx = self.xr.next()
        S.dma("sp", xt[:], src_rows, reads=list(rd), writes=[bx])
        hb, bhb = self.hb.next()
        ss, bs = self.ss.next()
        act(K, hb[:], xt[:], AF.Square, [bx], [bhb, bs], accum_out=ss[:])
        act(K, ss[:], ss[:], AF.Sqrt, [bs], [bs], scale=1.0 / D, bias=EPS)
        recip(K, ss[:], bs)
        stt(K, "dve", xt[:], xt[:], ss[:, 0:1], A[:], ALU.mult, ALU.mult, [bx, bs, bA], [bx])
        tt(K, "pool", hb[:], xt[:], SH[:], ALU.add, [bx, bSH], [bhb])
        return hb, bhb

    def part2(self, hb, bhb, hT_dst, b_hT):
        K, S = self.K, self.K.S
        pt, bpt = K.pt[self.ptn % 2]
        self.ptn += 1
        for c in range(8):
            S.op("pe", lambda e, c=c, pt=pt, hb=hb: e.transpose(out=pt[:, c * 128:(c + 1) * 128],
                                                                in_=hb[:, c * 128:(c + 1) * 128],
                                                                identity=self.ident[:]),
                 reads=[bhb, self.b_ident], writes=[bpt])
        cp(K, "act", hT_dst, pt[:].rearrange("p (c n) -> p c n", c=8), [bpt], [b_hT])

    def block(self, src_rows, A, bA, SH, bSH, hT_dst, b_hT, rd=()):
        hb, bhb = self.part1(src_rows, A, bA, SH, bSH, rd)
        self.part2(hb, bhb, hT_dst, b_hT)


def phase_ada(K):
    S = K.S
    ph = Phase(K, "ada")
    cc, b_cc = ph.sb("cc", [128, 8, 2], F32)
    sl, b_sl = ph.sb("sl", [128, 8, 2], BF16)
    S.dma("sp", cc[:], K.cc, writes=[b_cc])
    act(K, sl[:], cc[:], AF.Silu, [b_cc], [b_sl])
    wr = Rot(ph, "w", [128, 8, 512], BF16, 3)
    for L in range(2):
        bias, b_bias = ph.sb("bias%d" % L, [2, 6144], F32)
        S.dma("sp", bias[:], K.ada_b[L, :].partition_broadcast(2), writes=[b_bias])
        rows, b_rows = ph.sb("rows%d" % L, [2, 6144], F32)
        for j in range(12):
            w, bw = wr.next()
            S.dma("pool", w[:], K.ada_w[L, :, j * 512:(j + 1) * 512].rearrange("(k p) n -> p k n", p=128),
                  writes=[bw])
            ps, bps = K.ps()
            for k in range(8):
                mm(K, ps[0:2, :], bps, sl[:, k, :], w[:, k, :], k == 0, k == 7, [b_sl, bw])
            tt(K, "dve", rows[:, j * 512:(j + 1) * 512], ps[0:2, :], bias[:, j * 512:(j + 1) * 512], ALU.add,
               [bps, b_bias], [b_rows])
        S.dma("sp", K.mod_d[L:L + 1, :], rows[0:1, :], reads=[b_rows], writes=[K.b_mod])
        if L == 0:
            S.dma("sp", K.mod_d[2:3, :], rows[1:2, :], reads=[b_rows], writes=[K.b_mod])
    ph.close()


def phase_pre(K):
    S = K.S
    ph = Phase(K, "pre")
    ident, b_ident, ones, b_ones = load_consts(K, ph)
    cv, b_cv = ph.sb("cv", [128, K.NCV], F32)
    S.dma("sp", cv[:], K.colvecs, writes=[b_cv])
    win, b_win = ph.sb("win", [128, 8, EVEN_IN], BF16)
    for k in range(8):
        S.dma("pool", win[:, k, :], K.a_w_in[k * 128:(k + 1) * 128, :], writes=[b_win])
    wtmp, b_wtmp = ph.sb("wtmp", [128, 3, 768], F32)
    wuq, b_wuq = ph.sb("wuq", [128, 3, 768], BF16)
    S.dma("sp", wtmp[:], K.a_w_uq.rearrange("(c p) n -> p c n", p=128), writes=[b_wtmp])
    for c in range(3):
        ts(K, "dve", wuq[:, c, :], wtmp[:, c, :], cv[:, K.CV_QLN + c:K.CV_QLN + c + 1], None, ALU.mult, None,
           [b_wtmp, b_cv], [b_wuq])
    wk, b_wk = ph.sb("wk", [128, 2, 8, 96], BF16)
    S.op("dve", lambda e: e.memset(wk[:], 0.0), writes=[b_wk])
    S.dma("sp", wtmp[:, 0:2, 0:512], K.a_w_uk.rearrange("(c p) n -> p c n", p=128), reads=[], writes=[b_wtmp])
    for c in range(2):
        ts(K, "dve", wk[:, c, :, 0:64], wtmp[:, c, 0:512].rearrange("p (h d) -> p h d", h=8),
           cv[:, K.CV_KVLN + c:K.CV_KVLN + c + 1], None, ALU.mult, None, [b_wtmp, b_cv], [b_wk])
    wuv, b_wuv = ph.sb("wuv", [128, 2, 512], BF16)
    S.dma("sp", wtmp[:, 0:2, 0:512], K.a_w_uv.rearrange("(c p) n -> p c n", p=128), reads=[], writes=[b_wtmp])
    for c in range(2):
        ts(K, "dve", wuv[:, c, :], wtmp[:, c, 0:512], cv[:, K.CV_KVLN + c:K.CV_KVLN + c + 1], None, ALU.mult, None,
           [b_wtmp, b_cv], [b_wuv])
    e32, b_e32 = ph.sb("e32", [128, 96], BF16)
    S.dma("pool", e32[:], K.c_e32, writes=[b_e32])
    rm, b_rm = ph.sb("rm", [96, 96], BF16)
    S.dma("pool", rm[:], K.c_rm, writes=[b_rm])
    csm, b_csm = ph.sb("csm", [128, 256], BF16)
    S.dma("pool", csm[:], K.c_csm, writes=[b_csm])

    A, bA = ph.sb("A", [128, D], F32)
    SH, bSH = ph.sb("SH", [128, D], F32)
    nrm = Normer(K, ph, ident, b_ident)
    hT, b_hT = ph.sb("hT", [128, 8, 512], BF16)
    cfk, b_cfk = ph.sb("cfk", [128, 2, 512], F32)
    cfq, b_cfq = ph.sb("cfq", [128, 3, 512], F32)
    sqk, b_sqk = ph.sb("sqk", [128, 2, 512], BF16)
    sqq, b_sqq = ph.sb("sqq", [128, 3, 512], BF16)
    rsk, b_rsk = ph.sb("rsk", [128, 512], F32)
    rsq, b_rsq = ph.sb("rsq", [128, 512], F32)
    cnk, b_cnk = ph.sb("cnk", [128, 2, 512], BF16)
    cnq, b_cnq = ph.sb("cnq", [128, 3, 512], BF16)
    kr, b_kr = ph.sb("kr", [128, 512], BF16)
    S.op("dve", lambda e: e.memset(kr[:], 0.0), writes=[b_kr])
    vt, b_vt = ph.sb("vt", [128, 4, 8, VS], BF16)
    S.op("dve", lambda e: e.memset(vt[:], 1.0), writes=[b_vt])
    ktt = Rot(ph, "ktt", [96, 8, 512], BF16, 1)
    qtt = Rot(ph, "qtt", [96, 8, 512], BF16, 1)
    ropec = Rot(ph, "ropec", [96, 512], F32, 1)
    ropes = Rot(ph, "ropes", [96, 512], F32, 1)
    sqh = Rot(ph, "sqh", [96, 512], BF16, 3)
    rsh = Rot(ph, "rsh", [96, 512], F32, 2)
    qnb = Rot(ph, "qnb", [96, 512], BF16, 3)
    t1r = Rot(ph, "t1r", [96, 512], F32, 2)
    t2r = Rot(ph, "t2r", [96, 512], F32, 2)
    uf, b_uf = ph.sb("uf", [128, 4, 512], BF16)
    xrt, b_xrt = ph.sb("xrt", [128, 4, 512], BF16)
    xit, b_xit = ph.sb("xit", [128, 4, 512], BF16)

    def proj_chunks(n, col0, nch, cf, b_cf, sq, b_sq):
        for c in range(nch):
            ps, bps = K.ps()
            for k in range(8):
                mm(K, ps[:, 0:n], bps, win[:, k, col0 + c * 128:col0 + (c + 1) * 128], hT[:, k, 0:n], k == 0, k == 7,
                   [b_win, b_hT])
            cp(K, "act", cf[:, c, 0:n], ps[:, 0:n], [bps], [b_cf])
            act(K, sq[:, c, 0:n], ps[:, 0:n], AF.Square, [bps], [b_sq])

    def lat_finish(n, nch, dim, cf, b_cf, sq, b_sq, rs, b_rs, cn, b_cn):
        ps, bps = K.ps()
        for c in range(nch):
            mm(K, ps[:, 0:n], bps, ones[:, :], sq[:, c, 0:n], c == 0, c == nch - 1, [b_ones, b_sq])
        rstd_act(K, rs[:, 0:n], ps[:, 0:n], 1.0 / dim, [bps], b_rs)
        for c in range(nch):
            tt(K, "dve" if c % 2 == 0 else "pool", cn[:, c, 0:n], cf[:, c, 0:n], rs[:, 0:n], ALU.mult, [b_cf, b_rs],
               [b_cn])

    groups = [("ctx", K.ctx, 0, CTX, 0)] + [("lat", K.x, g * 512, 512, CTX + g * 512) for g in range(8)]

    def p1(gi):
        kind, src, t0, n, koff = groups[gi]
        is_ctx = kind == "ctx"
        if gi <= 1:
            row = 2 if is_ctx else 0
            reload_mod(K, A, bA, row, 1 * D, gain=K.norm1_g[0, :], tmp=nrm.xr.items[0])
            reload_mod(K, SH, bSH, row, 0)
        return [nrm.part1(src[t0 + tb * 128:t0 + (tb + 1) * 128, :], A, bA, SH, bSH) for tb in range(n // 128)]

    pre = p1(0)
    for gi, (kind, src, t0, n, koff) in enumerate(groups):
        nb = n // 128
        is_ctx = kind == "ctx"
        for tb in range(nb):
            nrm.part2(pre[tb][0], pre[tb][1], hT[:, :, tb * 128:(tb + 1) * 128], b_hT)
        proj_chunks(n, 384, 2, cfk, b_cfk, sqk, b_sqk)
        ps, bps = K.ps()
        for k in range(8):
            mm(K, ps[0:32, 0:n], bps, win[:, k, 640:672], hT[:, k, 0:n], k == 0, k == 7, [b_win, b_hT])
        cp(K, "act", kr[0:32, 0:n], ps[0:32, 0:n], [bps], [b_kr])
        if not is_ctx:
            proj_chunks(n, 0, 3, cfq, b_cfq, sqq, b_sqq)
            for g in range(4):
                ps, bps = K.ps()
                for k in range(8):
                    mm(K, ps[:, :], bps, win[:, k, 672 + g * 128:672 + (g + 1) * 128], hT[:, k, :], k == 0, k == 7,
                       [b_win, b_hT])
                cp(K, "act", uf[:, g, :], ps[:, :], [bps], [b_uf])
        if gi + 1 < len(groups):
            pre = p1(gi + 1)
        lat_finish(n, 2, 256, cfk, b_cfk, sqk, b_sqk, rsk, b_rsk, cnk, b_cnk)
        if not is_ctx:
            lat_finish(n, 3, 384, cfq, b_cfq, sqq, b_sqq, rsq, b_rsq, cnq, b_cnq)
            for tb in range(4):
                for gp in range(2):
                    ps, bps = K.ps()
                    for g2 in range(2):
                        g = gp * 2 + g2
                        mm(K, ps[:, g2 * 256:(g2 + 1) * 256], bps, uf[:, g, tb * 128:(tb + 1) * 128], csm[:, :], True,
                           True, [b_uf, b_csm])
                    pv = ps[:, :].rearrange("p (g r c) -> p g r c", g=2, r=2)
                    cp(K, "act", xrt[:, tb, gp * 256:(gp + 1) * 256].rearrange("p (g c) -> p g c", g=2),
                       pv[:, :, 0, :], [bps], [b_xrt])
                    cp(K, "dve", xit[:, tb, gp * 256:(gp + 1) * 256].rearrange("p (g c) -> p g c", g=2),
                       pv[:, :, 1, :], [bps], [b_xit])
            S.dma("pool", K.XR_d[t0:t0 + 512, :].rearrange("(b p) f -> p b f", p=128), xrt[:, :, :], reads=[b_xrt],
                  writes=[K.b_XR])
            S.dma("pool", K.XI_d[t0:t0 + 512, :].rearrange("(b p) f -> p b f", p=128), xit[:, :, :], reads=[b_xit],
                  writes=[K.b_XI])
        for tb in range(nb):
            ps, bps = K.ps()
            for c in range(2):
                mm(K, ps[:, :], bps, cnk[:, c, tb * 128:(tb + 1) * 128], wuv[:, c, :], c == 0, c == 1, [b_cnk, b_wuv])
            cp(K, "act", vt[:, tb, :, 0:64], ps[:, :].rearrange("p (h d) -> p h d", h=8), [bps], [b_vt])
        S.dma("pool", K.V_d[koff:koff + n, :].rearrange("(b p) f -> p b f", p=128),
              vt[:, 0:nb, :, :].rearrange("p b h d -> p b (h d)"), reads=[b_vt], writes=[K.b_V])
        if not is_ctx:
            rc, brc = ropec.next()
            S.dma("sp", rc[:], K.c_ropec[:, t0:t0 + 512], writes=[brc])
            rsn, brsn = ropes.next()
            S.dma("sp", rsn[:], K.c_ropes[:, t0:t0 + 512], writes=[brsn])
        else:
            rc = brc = rsn = brsn = None
        kt, bkt = ktt.next()
        jobs = [("k", h) for h in range(H)]
        if not is_ctx:
            qt, bqt = qtt.next()
            jobs += [("q", h) for h in range(H)]
        st_ = {}

        def stage_a(j):
            kind_, h = jobs[j]
            ps, bps = K.ps()
            if kind_ == "k":
                for c in range(2):
                    mm(K, ps[0:96, 0:n], bps, wk[:, c, h, :], cnk[:, c, 0:n], c == 0, False, [b_wk, b_cnk])
                mm(K, ps[0:96, 0:n], bps, e32[:, :], kr[:, 0:n], False, True, [b_e32, b_kr])
            else:
                for c in range(3):
                    mm(K, ps[0:96, 0:n], bps, wuq[:, c, h * 96:(h + 1) * 96], cnq[:, c, 0:n], c == 0, c == 2,
                       [b_wuq, b_cnq])
            s1, bs1 = sqh.next()
            act(K, s1[:, 0:n], ps[0:96, 0:n], AF.Square, [bps], [bs1])
            st_[j] = dict(ps=ps, bps=bps, s1=s1, bs1=bs1)

        def stage_b(j):
            kind_, h = jobs[j]
            d_ = st_[j]
            gcol = K.CV_KN if kind_ == "k" else K.CV_QN
            dst, b_dst = (kt[:, h, 0:n], bkt) if kind_ == "k" else (qt[:, h, 0:n], bqt)
            psN, bN = K.ps()
            mm(K, psN[0:96, 0:n], bN, ones[0:96, 0:96], d_["s1"][:, 0:n], True, True, [b_ones, d_["bs1"]])
            r1, br1 = rsh.next()
            rstd_act(K, r1[:, 0:n], psN[0:96, 0:n], 1.0 / 96, [bN], br1)
            if rc is None:
                stt(K, "dve", dst, d_["ps"][0:96, 0:n], cv[0:96, gcol:gcol + 1], r1[:, 0:n], ALU.mult, ALU.mult,
                    [d_["bps"], b_cv, br1], [b_dst])
                return
            qn, bqn = qnb.next()
            stt(K, "dve", qn[:, 0:n], d_["ps"][0:96, 0:n], cv[0:96, gcol:gcol + 1], r1[:, 0:n], ALU.mult, ALU.mult,
                [d_["bps"], b_cv, br1], [bqn])
            d_["qn"], d_["bqn"] = qn, bqn

        def stage_c(j):
            kind_, h = jobs[j]
            d_ = st_.pop(j)
            if rc is None:
                return
            dst, b_dst = (kt[:, h, 0:n], bkt) if kind_ == "k" else (qt[:, h, 0:n], bqt)
            qn, bqn = d_["qn"], d_["bqn"]
            psR, bR = K.ps()
            mm(K, psR[0:96, 0:n], bR, rm[:, :], qn[:, 0:n], True, True, [b_rm, bqn])
            t1, bt1 = t1r.next()
            tt(K, "pool", t1[:, 0:n], qn[:, 0:n], rc[:, 0:n], ALU.mult, [bqn, brc], [bt1])
            t2, bt2 = t2r.next()
            tt(K, "dve", t2[:, 0:n], psR[0:96, 0:n], rsn[:, 0:n], ALU.mult, [bR, brsn], [bt2])
            tt(K, "pool", dst, t1[:, 0:n], t2[:, 0:n], ALU.add, [bt1, bt2], [b_dst])

        nj = len(jobs)
        for step in range(nj + 2):
            if step < nj:
                stage_a(step)
            if 0 <= step - 1 < nj:
                stage_b(step - 1)
            if 0 <= step - 2 < nj:
                stage_c(step - 2)
        S.dma("pool", K.KT_d[:, :, koff:koff + n], kt[:, :, 0:n], reads=[bkt], writes=[K.b_KT])
        if not is_ctx:
            S.dma("pool", K.QT_d[:, :, t0:t0 + n], qt[:, :, :], reads=[bqt], writes=[K.b_QT])
    ph.close()


def phase_fft(K):
    S = K.S
    ph = Phase(K, "fft")
    wa, b_wa = ph.sb("wa", [128, 64, 128], BF16)
    for q in range(4):
        S.dma("pool", wa[:, q * 16:(q + 1) * 16, :], K.c_wa[q * 16:(q + 1) * 16].rearrange("n k m -> k n m"),
              writes=[b_wa])
    fb, b_fb = ph.sb("fb", [128, 64], BF16)
    S.dma("pool", fb[:], K.c_fb, writes=[b_fb])
    xg, b_xg = ph.sb("xg", [128, 64, 256], BF16)
    gp_, b_gp = ph.sb("gp", [128, 64, 256], BF16)
    yt, b_yt = ph.sb("yt", [128, 2, 4096], BF16)
    for hf in range(2):
        cs = slice(hf * 256, (hf + 1) * 256)
        S.dma("sp", xg[0:64, :, :], K.XR_d[:, cs].rearrange("(a b) c -> a b c", a=64), reads=[K.b_XR], writes=[b_xg])
        S.dma("sp", xg[64:128, :, :], K.XI_d[:, cs].rearrange("(a b) c -> a b c", a=64), reads=[K.b_XI],
              writes=[b_xg])
        for n2 in range(0, 64, 2):
            ps, bps = K.ps()
            for j in range(2):
                mm(K, ps[:, j * 256:(j + 1) * 256], bps, wa[:, n2 + j, :], xg[:, n2 + j, :], True, True, [b_wa, b_xg])
            cp(K, "act" if (n2 // 2) % 2 == 0 else "dve", gp_[:, n2:n2 + 2, :],
               ps[:, :].rearrange("p (j c) -> p j c", j=2), [bps], [b_gp])
        S.dma("sp", K.GD_d[hf].rearrange("r k n c -> (r k) n c"), gp_[:, :, :], reads=[b_gp], writes=[K.b_GD])
        for r in range(2):
            S.dma("sp", xg[r * 64:(r + 1) * 64, :, :], K.GD_d[hf, r].rearrange("k n c -> n k c"), reads=[K.b_GD],
                  writes=[b_xg])
        for gl in range(2):
            for kb in range(8):
                ps, bps = K.ps()
                for j in range(8):
                    k1 = kb * 8 + j
                    mm(K, ps[:, j * 64:(j + 1) * 64], bps, xg[:, k1, gl * 128:(gl + 1) * 128], fb[:, :], True, True,
                       [b_xg, b_fb])
                dst = yt[:, gl, :].rearrange("p (a b) -> p a b", b=64)[:, :, kb * 8:(kb + 1) * 8]
                cp(K, "act" if kb % 2 == 0 else "dve", dst, ps[:, :].rearrange("p (j a) -> p a j", j=8), [bps],
                   [b_yt])
        S.dma("sp", K.YT_d[:, hf * 2:(hf + 1) * 2, :], yt[:, :, :], reads=[b_yt], writes=[K.b_YT])
    ph.close()


def phase_att(K):
    S = K.S
    ph = Phase(K, "att")
    G1, bG1 = load_mod(K, ph, "G1", 0, 2 * D)
    ktall, b_kt = ph.sb("kt", [96, 8, NK], BF16)
    for h in range(H):
        S.dma("sp", ktall[:, h, :], K.KT_d[:, h, :], reads=[K.b_KT], writes=[b_kt])
    vall, b_v = ph.sb("v", [128, 34, 8 * VS], BF16)
    for q in range(2):
        S.dma("sp", vall[:, q * 17:(q + 1) * 17, :],
              K.V_d[q * 17 * 128:(q + 1) * 17 * 128, :].rearrange("(b p) f -> p b f", p=128), reads=[K.b_V],
              writes=[b_v])
    wo_a, b_woa = ph.sb("woa", [128, 4, D], BF16)
    S.dma("pool", wo_a[:], K.a_w_out[0:512, :].rearrange("(j p) n -> p j n", p=128), writes=[b_woa])
    wo_f, b_wof = ph.sb("wof", [128, 4, D], BF16)
    S.dma("pool", wo_f[:], K.a_w_out[512:1024, :].rearrange("(g p) n -> p g n", p=128), writes=[b_wof])
    onesb, b_onesb = ph.sb("onesb", [128, 64], BF16)
    S.op("dve", lambda e: e.memset(onesb[:], 1.0), writes=[b_onesb])
    hlr = Rot(ph, "hl", [65, 1024], BF16, 2)
    qtr = Rot(ph, "qt", [96, 8, 512], BF16, 2)
    ytr = Rot(ph, "ytq", [128, 4, 512], BF16, 2)
    ptr = Rot(ph, "pT", [128, 512], BF16, 3)
    oT, b_oT = ph.sb("oT", [128, 4, 512], BF16)
    oddr = Rot(ph, "oodd", [64, 512], BF16, 2)
    our = Rot(ph, "ou", [64, 512], F32, 2)
    rcp = Rot(ph, "rcp", [65, 512], F32, 2)
    xr = Rot(ph, "x", [128, D], F32, 2)
    tmr = Rot(ph, "tm", [128, 512], F32, 2)
    scale = float(96 ** -0.5)
    nps = [0, 0]

    def ps_o():
        it = K.psl[nps[0] % 2]
        nps[0] += 1
        return it

    def ps_s():
        it = K.psl[2 + nps[1] % 3]
        nps[1] += 1
        return it

    ps_x = K.psl[5]
    LA = 2
    state = {}

    def load_q(qg):
        t0 = qg * 512
        qt, bqt = qtr.next()
        S.dma("sp", qt[:], K.QT_d[:, :, t0:t0 + 512], reads=[K.b_QT], writes=[bqt])
        ytq, bytq = ytr.next()
        S.dma("sp", ytq[:], K.YT_d[:, :, t0:t0 + 512], reads=[K.b_YT], writes=[bytq])
        state[qg] = (qt, bqt, ytq, bytq)

    items = [(qg, h, kc) for qg in range(8) for h in range(H) for kc in range(34)]
    pos = {}
    spend = {}

    def emit_s(i):
        qg, h, kc = items[i]
        if (qg, h) not in pos:
            pos[(qg, h)] = ps_o()
        qt, bqt = state[qg][0], state[qg][1]
        pss, bpss = ps_s()
        mm(K, pss[:, :], bpss, ktall[:, h, kc * 128:(kc + 1) * 128], qt[:, h, :], True, True, [b_kt, bqt])
        spend[i] = (pss, bpss)

    def finalize(qg, h):
        po, bpo = pos.pop((qg, h))
        ou, bou = our.next()
        cp(K, "act", ou[:], po[0:64, :], [bpo], [bou])
        rc, brc = rcp.next()
        S.op("dve", lambda e, rc=rc, po=po: e.reciprocal(out=rc[64:65, :], in_=po[64:65, :]), reads=[bpo],
             writes=[brc])
        hl, bhl = hlr.next()
        cp(K, "dve", hl[64:65, 0:512], rc[64:65, :], [brc], [bhl])
        tt(K, "dve", hl[64:65, 512:1024], rc[64:65, :], hl[64:65, 0:512], ALU.subtract, [brc, bhl], [bhl])
        pb, bpb = ps_x
        mm(K, pb[0:64, :], bpb, onesb[64:65, 0:64], hl[64:65, 0:512], True, False, [b_onesb, bhl])
        mm(K, pb[0:64, :], bpb, onesb[64:65, 0:64], hl[64:65, 512:1024], False, True, [b_onesb, bhl])
        if h % 2 == 0:
            tt(K, "dve", oT[0:64, h // 2, :], ou[:], pb[0:64, :], ALU.mult, [bou, bpb], [b_oT])
        else:
            od, bod = oddr.next()
            tt(K, "dve", od[:], ou[:], pb[0:64, :], ALU.mult, [bou, bpb], [bod])
            S.dma("sp", oT[64:128, h // 2, :], od[:], reads=[bod], writes=[b_oT])

    def outproj(qg):
        t0 = qg * 512
        ytq, bytq = state[qg][2], state[qg][3]
        for tb in range(4):
            xt, bx = xr.next()
            S.dma("sp", xt[:], K.x[t0 + tb * 128:t0 + (tb + 1) * 128, :], writes=[bx])
            for dh in range(2):
                ps, bps = ps_x
                for j in range(4):
                    mm(K, ps[:, :], bps, oT[:, j, tb * 128:(tb + 1) * 128], wo_a[:, j, dh * 512:(dh + 1) * 512],
                       j == 0, False, [b_oT, b_woa])
                for g in range(4):
                    mm(K, ps[:, :], bps, ytq[:, g, tb * 128:(tb + 1) * 128], wo_f[:, g, dh * 512:(dh + 1) * 512],
                       False, g == 3, [bytq, b_wof])
                sl = slice(dh * 512, (dh + 1) * 512)
                tm, btm = tmr.next()
                tt(K, "dve", tm[:], ps[:, :], G1[:, sl], ALU.mult, [bps, bG1], [btm])
                tt(K, "pool", xt[:, sl], xt[:, sl], tm[:], ALU.add, [bx, btm], [bx])
            S.dma("pool", K.X1_d[t0 + tb * 128:t0 + (tb + 1) * 128, :], xt[:], reads=[bx], writes=[K.b_X1])

    load_q(0)
    pending = []
    n_items = len(items)
    for i in range(min(LA, n_items)):
        emit_s(i)
    for i in range(n_items):
        qg, h, kc = items[i]
        if i + LA < n_items:
            q2, h2, k2 = items[i + LA]
            if q2 not in state:
                load_q(q2)
            emit_s(i + LA)
        pss, bpss = spend.pop(i)
        po, bpo = pos[(qg, h)]
        pT, bpT = ptr.next()
        act(K, pT[:], pss[:, :], AF.Exp, [bpss], [bpT], scale=scale)
        mm(K, po[0:65, :], bpo, vall[:, kc, h * VS:h * VS + 65], pT[:], kc == 0, kc == 33, [b_v, bpT])
        if kc == 33:
            pending.append((qg, h))
        if kc == 20 and pending:
            for (pq, phh) in pending:
                finalize(pq, phh)
                if phh == H - 1:
                    outproj(pq)
            pending = []
    for (pq, phh) in pending:
        finalize(pq, phh)
        if phh == H - 1:
            outproj(pq)
    ph.close()


def phase_ffn(K, L, src, b_src, dst, b_dst):
    S = K.S
    ph = Phase(K, "ffn%d" % L)
    ident, b_ident, ones, b_ones = load_consts(K, ph)
    nrm = Normer(K, ph, ident, b_ident)
    A, bA = load_mod(K, ph, "A", L, 4 * D, gain=K.norm2_g[L, :], tmp=nrm.xr.items[0])
    SH, bSH = load_mod(K, ph, "SH", L, 3 * D)
    G2, bG2 = load_mod(K, ph, "G2", L, 5 * D)
    hT, b_hT = ph.sb("hT", [128, 8, 512], BF16)
    gT, b_gT = ph.sb("gT", [128, NHC, 512], BF16)
    sar = Rot(ph, "sa", [128, 512], F32, 2)
    tmr = Rot(ph, "tm", [128, 512], F32, 2)
    def p1(tg):
        t0 = tg * 512
        return [nrm.part1(src[t0 + tb * 128:t0 + (tb + 1) * 128, :], A, bA, SH, bSH, rd=[b_src]) for tb in range(4)]

    pre = p1(0)
    w1, b_w1 = ph.sb("w1", [128, 8, DFF], BF16)
    w3, b_w3 = ph.sb("w3", [128, 8, DFF], BF16)
    w2, b_w2 = ph.sb("w2", [128, NHC, D], BF16)
    b_w1s = [Buf("w1s%d" % i) for i in range(4)]
    b_w3s = [Buf("w3s%d" % i) for i in range(4)]
    b_w2s = [Buf("w2s%d" % i) for i in range(NHC)]
    for q in range(4):
        cs = slice(q * 704, (q + 1) * 704)
        for k in range(8):
            S.dma("pool", w1[:, k, cs], K.ffn_w1[L, k * 128:(k + 1) * 128, cs], writes=[b_w1s[q]])
            S.dma("pool", w3[:, k, cs], K.ffn_w3[L, k * 128:(k + 1) * 128, cs], writes=[b_w3s[q]])
    for hc in range(NHC):
        S.dma("pool", w2[:, hc, :], K.ffn_w2[L, hc * 128:(hc + 1) * 128, :], writes=[b_w2s[hc]])
    for tg in range(8):
        t0 = tg * 512
        for tb in range(4):
            nrm.part2(pre[tb][0], pre[tb][1], hT[:, :, tb * 128:(tb + 1) * 128], b_hT)
        for hc in range(NHC):
            q = (hc * 128) // 704
            q2 = (hc * 128 + 127) // 704
            pa, bpa = K.ps()
            for k in range(8):
                mm(K, pa[:, :], bpa, w1[:, k, hc * 128:(hc + 1) * 128], hT[:, k, :], k == 0, k == 7,
                   [b_w1s[q], b_w1s[q2], b_hT])
            pb, bpb = K.ps()
            for k in range(8):
                mm(K, pb[:, :], bpb, w3[:, k, hc * 128:(hc + 1) * 128], hT[:, k, :], k == 0, k == 7,
                   [b_w3s[q], b_w3s[q2], b_hT])
            sa, bsa = sar.next()
            act(K, sa[:], pa[:, :], AF.Silu, [bpa], [bsa])
            tt(K, "dve", gT[:, hc, :], sa[:], pb[:, :], ALU.mult, [bsa, bpb], [b_gT])
        if tg + 1 < 8:
            pre = p1(tg + 1)
        for tb in range(4):
            xt, bx = nrm.xr.next()
            S.dma("sp", xt[:], src[t0 + tb * 128:t0 + (tb + 1) * 128, :], reads=[b_src], writes=[bx])
            for dh in range(2):
                ps, bps = K.ps()
                for hc in range(NHC):
                    mm(K, ps[:, :], bps, gT[:, hc, tb * 128:(tb + 1) * 128], w2[:, hc, dh * 512:(dh + 1) * 512],
                       hc == 0, hc == NHC - 1, [b_gT, b_w2s[hc]])
                sl = slice(dh * 512, (dh + 1) * 512)
                tm, btm = tmr.next()
                tt(K, "dve", tm[:], ps[:, :], G2[:, sl], ALU.mult, [bps, bG2], [btm])
                tt(K, "pool", xt[:, sl], xt[:, sl], tm[:], ALU.add, [bx, btm], [bx])
            S.dma("pool", dst[t0 + tb * 128:t0 + (tb + 1) * 128, :], xt[:], reads=[bx], writes=[b_dst])
    ph.close()


def phase_mix1(K, src, b_src, dst, b_dst):
    S = K.S
    ph = Phase(K, "mix1")
    ident, b_ident, ones, b_ones = load_consts(K, ph)
    cv, b_cv = ph.sb("cv", [128, K.NCV], F32)
    S.dma("sp", cv[:], K.colvecs, writes=[b_cv])
    conf, b_conf = ph.sb("conf", [128, 4, T + 30], BF16)
    scx, b_scx = ph.sb("scx", [128, 4, T + 2], BF16)
    sbt, b_sbt = ph.sb("sbt", [128, 4, T], BF16)
    S.op("dve", lambda e: e.memset(conf[:, :, 0:15], 0.0), writes=[b_conf])
    S.op("dve", lambda e: e.memset(conf[:, :, T + 15:T + 30], 0.0), writes=[b_conf])
    S.op("dve", lambda e: e.memset(scx[:, :, 0:1], 0.0), writes=[b_scx])
    S.op("dve", lambda e: e.memset(scx[:, :, T + 1:T + 2], 0.0), writes=[b_scx])
    p1 = Phase(K, "mix1a")
    nrm = Normer(K, p1, ident, b_ident)
    A, bA = load_mod(K, p1, "A", 1, 1 * D, gain=K.norm1_g[1, :], tmp=nrm.xr.items[0])
    SH, bSH = load_mod(K, p1, "SH", 1, 0)
    win, b_win = p1.sb("win", [128, 8, ODD_IN], BF16)
    for k in range(8):
        for q in range(2):
            S.dma("pool", win[:, k, q * 1280:(q + 1) * 1280], K.b_w_in[k * 128:(k + 1) * 128, q * 1280:(q + 1) * 1280],
                  writes=[b_win])
    hT, b_hT = p1.sb("hT", [128, 8, 512], BF16)
    sgr = Rot(p1, "sg", [128, 512], F32, 2)
    def pp1(tg):
        t0 = tg * 512
        return [nrm.part1(src[t0 + tb * 128:t0 + (tb + 1) * 128, :], A, bA, SH, bSH, rd=[b_src]) for tb in range(4)]

    pre = pp1(0)
    for tg in range(8):
        t0 = tg * 512
        for tb in range(4):
            nrm.part2(pre[tb][0], pre[tb][1], hT[:, :, tb * 128:(tb + 1) * 128], b_hT)
        if tg + 1 < 8:
            pre = pp1(tg + 1)

        def proj(fc):
            ps, bps = K.ps()
            for k in range(8):
                mm(K, ps[:, :], bps, win[:, k, fc * 128:(fc + 1) * 128], hT[:, k, :], k == 0, k == 7, [b_win, b_hT])
            return ps, bps

        for c in range(4):
            pa, bpa = proj(c)
            pg, bpg = proj(4 + c)
            sg, bsg = sgr.next()
            act(K, sg[:], pg[:, :], AF.Sigmoid, [bpg], [bsg])
            tt(K, "dve", conf[:, c, 15 + t0:15 + t0 + 512], sg[:], pa[:, :], ALU.mult, [bsg, bpa], [b_conf])
            pB, bpB = proj(8 + c)
            cp(K, "act", sbt[:, c, t0:t0 + 512], pB[:, :], [bpB], [b_sbt])
            pC, bpC = proj(12 + c)
            pX, bpX = proj(16 + c)
            sg, bsg = sgr.next()
            cp(K, "act", sg[:], pC[:, :], [bpC], [bsg])
            tt(K, "dve", scx[:, c, 1 + t0:1 + t0 + 512], sg[:], pX[:, :], ALU.mult, [bsg, bpX], [b_scx])
    p1.close()
    p2 = Phase(K, "mix1b")
    G1, bG1 = load_mod(K, p2, "G1", 1, 2 * D)
    identf, b_identf = p2.sb("identf", [128, 128], F32)
    S.dma("sp", identf[:], K.c_ident, writes=[b_identf])
    dg, b_dg = p2.sb("dg", [128, 4, 31, 128], BF16)
    dg2, b_dg2 = p2.sb("dg2", [128, 4, 3, 128], BF16)
    for c in range(4):
        for j in range(31):
            col = K.CV_DW + c * 31 + j
            ts(K, "dve" if j % 2 == 0 else "pool", dg[:, c, j, :], identf[:], cv[:, col:col + 1], None, ALU.mult, None,
               [b_identf, b_cv], [b_dg])
        for j in range(3):
            col = K.CV_SC + c * 3 + j
            ts(K, "dve", dg2[:, c, j, :], identf[:], cv[:, col:col + 1], None, ALU.mult, None, [b_identf, b_cv],
               [b_dg2])
    wo, b_wo = p2.sb("wo", [128, 8, D], BF16)
    for f in range(8):
        S.dma("pool", wo[:, f, :], K.b_w_out[f * 128:(f + 1) * 128, :], writes=[b_wo])
    oTr = Rot(p2, "oT", [128, 8, 512], BF16, 2)
    yr = Rot(p2, "y", [128, 512], F32, 3)
    ybr = Rot(p2, "yb", [128, 512], BF16, 2)
    dr = Rot(p2, "d", [128, 512], F32, 2)
    sqr = Rot(p2, "sq", [128, 512], BF16, 2)
    rsr = Rot(p2, "rs", [128, 512], F32, 2)
    xr = Rot(p2, "x", [128, D], F32, 2)
    tmr = Rot(p2, "tm", [128, 512], F32, 2)
    items = [(tg, c) for tg in range(8) for c in range(4)]
    oTs = {}
    stt_ = {}

    def st_a(i):
        tg, c = items[i]
        t0 = tg * 512
        if tg not in oTs:
            oTs[tg] = oTr.next()
        oT, b_oT = oTs[tg]
        ps, bps = K.ps()
        for j in range(31):
            mm(K, ps[:, :], bps, dg[:, c, j, :], conf[:, c, t0 + j:t0 + j + 512], j == 0, j == 30, [b_dg, b_conf])
        y, by = yr.next()
        act(K, y[:], ps[:, :], AF.Identity, [bps, b_cv], [by], bias=cv[:, K.CV_DWB + c:K.CV_DWB + c + 1])
        yb, byb = ybr.next()
        cp(K, "pool", yb[:], y[:], [by], [byb])
        ps2, bps2 = K.ps()
        for j in range(3):
            mm(K, ps2[:, :], bps2, dg2[:, c, j, :], scx[:, c, t0 + j:t0 + j + 512], j == 0, j == 2, [b_dg2, b_scx])
        tt(K, "dve", oT[:, 4 + c, :], sbt[:, c, t0:t0 + 512], ps2[:, :], ALU.mult, [b_sbt, bps2], [b_oT])
        stt_[i] = dict(y=y, by=by, yb=yb, byb=byb)

    def st_b(i):
        d_ = stt_[i]
        pm, bpm = K.ps()
        mm(K, pm[:, :], bpm, ones[:, :], d_["yb"][:], True, True, [b_ones, d_["byb"]])
        d, bd = dr.next()
        stt(K, "dve", d[:], pm[:, :], -1.0 / 128, d_["y"][:], ALU.mult, ALU.add, [bpm, d_["by"]], [bd])
        sq, bsq = sqr.next()
        act(K, sq[:], d[:], AF.Square, [bd], [bsq])
        d_.update(d=d, bd=bd, sq=sq, bsq=bsq)

    def st_c(i):
        tg, c = items[i]
        oT, b_oT = oTs[tg]
        d_ = stt_.pop(i)
        d, bd = d_["d"], d_["bd"]
        pv, bpv = K.ps()
        mm(K, pv[:, :], bpv, ones[:, :], d_["sq"][:], True, True, [b_ones, d_["bsq"]])
        rs, brs = rsr.next()
        rstd_act(K, rs[:], pv[:, :], 1.0 / 128, [bpv], brs)
        tt(K, "dve", d[:], d[:], rs[:], ALU.mult, [bd, brs], [bd])
        act(K, oT[:, c, :], d[:], AF.Silu, [bd, b_cv], [b_oT], scale=cv[:, K.CV_LNG + c:K.CV_LNG + c + 1],
            bias=cv[:, K.CV_LNB + c:K.CV_LNB + c + 1])

    def outproj(tg):
        t0 = tg * 512
        oT, b_oT = oTs.pop(tg)
        for tb in range(4):
            xt, bx = xr.next()
            S.dma("sp", xt[:], src[t0 + tb * 128:t0 + (tb + 1) * 128, :], reads=[b_src], writes=[bx])
            for dh in range(2):
                ps, bps = K.ps()
                for f in range(8):
                    mm(K, ps[:, :], bps, oT[:, f, tb * 128:(tb + 1) * 128], wo[:, f, dh * 512:(dh + 1) * 512], f == 0,
                       f == 7, [b_oT, b_wo])
                sl = slice(dh * 512, (dh + 1) * 512)
                tm, btm = tmr.next()
                tt(K, "dve", tm[:], ps[:, :], G1[:, sl], ALU.mult, [bps, bG1], [btm])
                tt(K, "pool", xt[:, sl], xt[:, sl], tm[:], ALU.add, [bx, btm], [bx])
            S.dma("pool", dst[t0 + tb * 128:t0 + (tb + 1) * 128, :], xt[:], reads=[bx], writes=[b_dst])

    ni = len(items)
    for step in range(ni + 4):
        if step < ni:
            st_a(step)
        if 0 <= step - 1 < ni:
            st_b(step - 1)
        if 0 <= step - 2 < ni:
            st_c(step - 2)
        j = step - 4
        if 0 <= j < ni and items[j][1] == 3:
            outproj(items[j][0])
    p2.close()
    ph.close()


CV_LAYOUT = {}


def colvec_layout():
    off = 0
    lay = {}
    for name, n in (("QLN", 3), ("KVLN", 2), ("QN", 1), ("KN", 1), ("DW", 4 * 31), ("DWB", 4), ("LNG", 4),
                    ("LNB", 4), ("SC", 4 * 3)):
        lay[name] = off
        off += n
    return lay, off


def build(phases=("ada", "pre", "fft", "att", "ffn0", "mix1", "ffn1"), dbg=False):
    nc = bass.Bass("TRN2", target_bir_lowering=False)
    K = Ctx()
    K.nc = nc

    def din(name, shape):
        return nc.dram_tensor(name, list(shape), F32, kind="ExternalInput").ap()

    def dscr(name, shape, dt):
        kind = "ExternalOutput" if dbg else "Internal"
        return nc.dram_tensor(name, list(shape), dt, kind=kind).ap()

    K.x = din("x", [T, D])
    K.ctx = din("ctx", [CTX, D])
    K.cc = din("cc", [128, 8, 2])
    K.ada_w = din("ada_w", [2, D, 6 * D])
    K.ada_b = din("ada_b", [2, 6 * D])
    K.norm1_g = din("norm1_g", [2, D])
    K.norm2_g = din("norm2_g", [2, D])
    K.ffn_w1 = din("ffn_w1", [2, D, DFF])
    K.ffn_w3 = din("ffn_w3", [2, D, DFF])
    K.ffn_w2 = din("ffn_w2", [2, DFF, D])
    K.a_w_in = din("a_w_in", [D, EVEN_IN])
    K.a_w_uq = din("a_w_uq", [384, 768])
    K.a_w_uk = din("a_w_uk", [256, 512])
    K.a_w_uv = din("a_w_uv", [256, 512])
    K.a_w_out = din("a_w_out", [D, D])
    K.b_w_in = din("b_w_in", [D, ODD_IN])
    K.b_w_out = din("b_w_out", [D, D])
    lay, ncv = colvec_layout()
    K.NCV = ncv
    for k_, v_ in lay.items():
        setattr(K, "CV_" + k_, v_)
    K.colvecs = din("colvecs", [128, ncv])
    K.c_ident = din("c_ident", [128, 128])
    K.c_e32 = din("c_e32", [128, 96])
    K.c_rm = din("c_rm", [96, 96])
    K.c_csm = din("c_csm", [128, 256])
    K.c_ropec = din("c_ropec", [96, T])
    K.c_ropes = din("c_ropes", [96, T])
    K.c_wa = din("c_wa", [64, 128, 128])
    K.c_fb = din("c_fb", [128, 64])
    K.out = nc.dram_tensor("out", [T, D], F32, kind="ExternalOutput").ap()
    K.b_out = Buf("out")

    K.mod_d = dscr("mod_d", [3, 6 * D], F32)
    K.b_mod = Buf("mod_d")
    K.QT_d = dscr("QT_d", [96, 8, T], BF16)
    K.b_QT = Buf("QT_d")
    K.KT_d = dscr("KT_d", [96, 8, NK], BF16)
    K.b_KT = Buf("KT_d")
    K.V_d = dscr("V_d", [NK, 8 * VS], BF16)
    K.b_V = Buf("V_d")
    K.XR_d = dscr("XR_d", [T, 512], BF16)
    K.b_XR = Buf("XR_d")
    K.XI_d = dscr("XI_d", [T, 512], BF16)
    K.b_XI = Buf("XI_d")
    K.GD_d = dscr("GD_d", [2, 2, 64, 64, 256], BF16)
    K.b_GD = Buf("GD_d")
    K.YT_d = dscr("YT_d", [128, 4, T], BF16)
    K.b_YT = Buf("YT_d")
    K.X1_d = dscr("X1_d", [T, D], F32)
    K.b_X1 = Buf("X1_d")
    K.X2_d = dscr("X2_d", [T, D], F32)
    K.b_X2 = Buf("X2_d")
    K.X3_d = dscr("X3_d", [T, D], F32)
    K.b_X3 = Buf("X3_d")

    with contextlib.ExitStack() as st:
        sems = {e: st.enter_context(nc.semaphore("s_" + e)) for e in Sched.ENGS}
        dsems = [st.enter_context(nc.semaphore("d%d" % i)) for i in range(N_DMA_SEMS)]
        K.S = Sched(nc, sems, dsems)
        psl = []
        for i in range(6):
            psl.append((st.enter_context(nc.psum_tensor("ps%d" % i, [128, 512], F32)), Buf("ps%d" % i, True)))
        K.pt = []
        for i in range(2):
            K.pt.append((st.enter_context(nc.psum_tensor("pt%d" % i, [128, 1024], BF16)), Buf("pt%d" % i, True)))
        K.psi = 0

        def ps():
            it = psl[K.psi % 6]
            K.psi += 1
            return it

        K.ps = ps
        K.psl = psl
        if "ada" in phases:
            phase_ada(K)
        if "pre" in phases:
            phase_pre(K)
        if "fft" in phases:
            phase_fft(K)
        if "att" in phases:
            phase_att(K)
        if "ffn0" in phases:
            phase_ffn(K, 0, K.X1_d, K.b_X1, K.X2_d, K.b_X2)
        if "mix1" in phases:
            phase_mix1(K, K.X2_d, K.b_X2, K.X3_d, K.b_X3)
        if "ffn1" in phases:
            phase_ffn(K, 1, K.X3_d, K.b_X3, K.out, K.b_out)
        K.S.barrier()
        K.S.emit()
    return nc


def host_consts():
    c = {}
    c["c_ident"] = np.eye(128, dtype=np.float32)
    e32 = np.zeros((128, 96), np.float32)
    e32[np.arange(32), 64 + np.arange(32)] = 1.0
    c["c_e32"] = e32
    rm = np.zeros((96, 96), np.float32)
    for i in range(16):
        rm[80 + i, 64 + i] = -1.0
        rm[64 + i, 80 + i] = 1.0
    c["c_rm"] = rm
    m = np.arange(128, dtype=np.float64)
    ang = 2 * np.pi * np.outer(m, m) / 128.0
    c["c_csm"] = np.concatenate([np.cos(ang), -np.sin(ang)], axis=1).astype(np.float32)
    rows = (np.arange(T) // 64).astype(np.float32)
    cols = (np.arange(T) % 64).astype(np.float32)
    inv = (np.float32(10000.0) ** (-np.arange(8, dtype=np.float32) / np.float32(8))).astype(np.float32)
    angp = np.concatenate([rows[:, None] * inv, cols[:, None] * inv], axis=-1).astype(np.float32)
    cos = np.cos(angp.astype(np.float64)).T
    sin = np.sin(angp.astype(np.float64)).T
    rc = np.ones((96, T), np.float64)
    rs = np.zeros((96, T), np.float64)
    rc[64:80] = cos
    rc[80:96] = cos
    rs[64:80] = sin
    rs[80:96] = sin
    c["c_ropec"] = rc.astype(np.float32)
    c["c_ropes"] = rs.astype(np.float32)
    n1 = np.arange(64, dtype=np.float64)
    k1 = np.arange(64, dtype=np.float64)
    wa = np.zeros((64, 128, 128), np.float64)
    for n2 in range(64):
        th = 2 * np.pi * (np.outer(n1, k1) / 64.0 + (n2 * k1)[None, :] / 4096.0)
        wa[n2, 0:64, 0:64] = np.cos(th)
        wa[n2, 64:128, 0:64] = np.sin(th)
        wa[n2, 0:64, 64:128] = -np.sin(th)
        wa[n2, 64:128, 64:128] = np.cos(th)
    c["c_wa"] = wa.astype(np.float32)
    n2 = np.arange(64, dtype=np.float64)
    k2 = np.arange(64, dtype=np.float64)
    ph = 2 * np.pi * np.outer(n2, k2) / 64.0
    s = 1.0 / np.sqrt(4096.0 * 128.0)
    c["c_fb"] = (np.concatenate([np.cos(ph), np.sin(ph)], axis=0) * s).astype(np.float32)
    return c


def make_in_maps(inputs):
    f = lambda a: np.ascontiguousarray(np.asarray(a, dtype=np.float32))
    lay, ncv = colvec_layout()
    cvs = np.zeros((128, ncv), np.float32)
    cvs[:, lay["QLN"]:lay["QLN"] + 3] = f(inputs["a_q_ln_g"]).reshape(3, 128).T
    cvs[:, lay["KVLN"]:lay["KVLN"] + 2] = f(inputs["a_kv_ln_g"]).reshape(2, 128).T
    cvs[0:96, lay["QN"]] = f(inputs["a_q_norm_g"]).reshape(96)
    cvs[0:96, lay["KN"]] = f(inputs["a_k_norm_g"]).reshape(96)
    dw = f(inputs["b_conf_dw"]).reshape(31, 4, 128)
    cvs[:, lay["DW"]:lay["DW"] + 124] = dw.transpose(2, 1, 0).reshape(128, 124)
    cvs[:, lay["DWB"]:lay["DWB"] + 4] = f(inputs["b_conf_dw_b"]).reshape(4, 128).T
    cvs[:, lay["LNG"]:lay["LNG"] + 4] = f(inputs["b_conf_ln_g"]).reshape(4, 128).T
    cvs[:, lay["LNB"]:lay["LNB"] + 4] = f(inputs["b_conf_ln_b"]).reshape(4, 128).T
    sc = f(inputs["b_sc_dw"]).reshape(3, 4, 128)
    cvs[:, lay["SC"]:lay["SC"] + 12] = sc.transpose(2, 1, 0).reshape(128, 12)
    shared = {
        "ada_w": f(inputs["ada_w"]), "ada_b": f(inputs["ada_b"]),
        "norm1_g": f(inputs["norm1_g"]), "norm2_g": f(inputs["norm2_g"]),
        "ffn_w1": f(inputs["ffn_w1"]), "ffn_w3": f(inputs["ffn_w3"]), "ffn_w2": f(inputs["ffn_w2"]),
        "a_w_in": f(inputs["a_w_in"]).reshape(D, EVEN_IN),
        "a_w_uq": f(inputs["a_w_uq"]).reshape(384, 768),
        "a_w_uk": f(inputs["a_w_uk"]).reshape(256, 512),
        "a_w_uv": f(inputs["a_w_uv"]).reshape(256, 512),
        "a_w_out": f(inputs["a_w_out"]).reshape(D, D),
        "b_w_in": f(inputs["b_w_in"]).reshape(D, ODD_IN),
        "b_w_out": f(inputs["b_w_out"]).reshape(D, D),
        "colvecs": cvs,
    }
    shared.update(host_consts())
    x = f(inputs["x"])
    ctx = f(inputs["ctx"])
    c = f(inputs["c"])
    c_ctx = f(inputs["c_ctx"]).reshape(D)
    maps = []
    for b in range(x.shape[0]):
        cc = np.stack([c[b].reshape(8, 128).T, c_ctx.reshape(8, 128).T], axis=-1)
        m = dict(shared)
        m["x"] = np.ascontiguousarray(x[b])
        m["ctx"] = np.ascontiguousarray(ctx[b])
        m["cc"] = np.ascontiguousarray(cc)
        maps.append(m)
    return maps


def kernel(**inputs):
    maps = make_in_maps(inputs)
    nc = build()
    res = run_bass_kernel_spmd(nc, maps, core_ids=list(range(len(maps))))
    return np.stack([np.asarray(r["out"], dtype=np.float32) for r in res.results], axis=0)
```

```python
import contextlib
import numpy as np
import concourse.bass as bass
import concourse.mybir as mybir
from concourse.bass_utils import run_bass_kernel_spmd

F32 = mybir.dt.float32
BF16 = mybir.dt.bfloat16
AF = mybir.ActivationFunctionType
ALU = mybir.AluOpType
AX = mybir.AxisListType

N_DMA_SEMS = 40
T = 4096
D = 1024
CTX = 256
NK = CTX + T
H = 8
DFF = 2816
NHC = DFF // 128
EPS = 1e-6
EVEN_IN = 1184
VS = 80
ODD_IN = 2560


class Buf:
    __slots__ = ("name", "w", "r", "excl")

    def __init__(self, name, excl=False):
        self.name = name
        self.w = None
        self.r = {}
        self.excl = excl


class Sched:
    ENGS = ("pe", "act", "dve", "pool", "sp")

    def __init__(self, nc, sems, dsems):
        self.nc = nc
        self.sems = sems
        self.dsems = dsems
        self.q = {e: [] for e in self.ENGS}
        self.cnt = {e: 0 for e in self.ENGS}
        self.seen = {e: {} for e in self.ENGS}
        self.dcnt = [0] * N_DMA_SEMS
        self.dnext = 0
        self.n_ops = 0

    def _deps(self, e, reads, writes, extra=()):
        deps = {}

        def need(ev, same_ok):
            if ev is None:
                return
            k, v = ev
            if k == e and not same_ok:
                return
            if deps.get(k, 0) < v:
                deps[k] = v

        same = e != "pe"
        for b in reads:
            need(b.w, same)
            if b.excl:
                for k, v in b.r.items():
                    need((k, v), False)
        for b in writes:
            need(b.w, same)
            for k, v in b.r.items():
                need((k, v), False)
        for ev in extra:
            need(ev, True)
        waits = []
        sn = self.seen[e]
        for k, v in deps.items():
            if sn.get(k, 0) < v:
                sn[k] = v
                waits.append((k, v))
        return waits

    def op(self, e, fn, reads=(), writes=()):
        waits = self._deps(e, reads, writes)
        self.cnt[e] += 1
        ev = (e, self.cnt[e])
        self.q[e].append((waits, fn, ev))
        for b in reads:
            if b.r.get(e, 0) < ev[1]:
                b.r[e] = ev[1]
        for b in writes:
            b.w = ev
            b.r = {}
        self.n_ops += 1
        return ev

    def dma(self, e, out_ap, in_ap, reads=(), writes=()):
        i = self.dnext
        self.dnext = (self.dnext + 1) % N_DMA_SEMS
        k = ("d", i)
        extra = []
        if self.dcnt[i] > 0:
            extra.append((k, self.dcnt[i]))
        waits = self._deps(e, reads, writes, extra)
        self.dcnt[i] += 16
        ev = (k, self.dcnt[i])

        def fn(eng, out_ap=out_ap, in_ap=in_ap):
            return eng.dma_start(out=out_ap, in_=in_ap)

        self.q[e].append((waits, fn, ev))
        for b in reads:
            b.r[k] = ev[1]
        for b in writes:
            b.w = ev
            b.r = {}
        self.n_ops += 1
        return ev

    def barrier(self):
        for e in self.ENGS:
            waits = []
            sn = self.seen[e]
            for k in self.ENGS:
                if k != e and self.cnt[k] > sn.get(k, 0):
                    sn[k] = self.cnt[k]
                    waits.append((k, self.cnt[k]))
            for i in range(N_DMA_SEMS):
                k = ("d", i)
                if self.dcnt[i] > sn.get(k, 0):
                    sn[k] = self.dcnt[i]
                    waits.append((k, self.dcnt[i]))
            self.q[e].append((waits, None, None))

    def emit(self):
        nc, sems, dsems = self.nc, self.sems, self.dsems

        def semof(k):
            return dsems[k[1]] if isinstance(k, tuple) else sems[k]

        def run(e):
            def body(eng):
                for waits, fn, ev in self.q[e]:
                    for k, v in waits:
                        eng.wait_ge(semof(k), v)
                    if fn is None:
                        continue
                    ins = fn(eng)
                    if isinstance(ev[0], tuple):
                        ins.then_inc(dsems[ev[0][1]], 16)
                    else:
                        ins.then_inc(sems[e], 1)
            return body

        with nc.Block() as block:
            block.tensor(run("pe"))
            block.scalar(run("act"))
            block.vector(run("dve"))
            block.gpsimd(run("pool"))
            block.sync(run("sp"))
        self.q = {e: [] for e in self.ENGS}


class Rot:
    def __init__(self, ph, name, shape, dt, n=2):
        self.items = [ph.sb("%s%d" % (name, i), shape, dt) for i in range(n)]
        self.i = 0

    def next(self):
        it = self.items[self.i % len(self.items)]
        self.i += 1
        return it


class Phase:
    def __init__(self, K, name):
        self.K = K
        self.name = name
        self.st = contextlib.ExitStack()
        self.n = 0

    def sb(self, name, shape, dt):
        self.n += 1
        nm = "%s_%s" % (self.name, name)
        t = self.st.enter_context(self.K.nc.sbuf_tensor(nm, list(shape), dt))
        return t, Buf(nm)

    def close(self):
        self.K.S.barrier()
        self.K.S.emit()
        self.st.close()


class Ctx:
    pass


def mm(K, out_ap, out_b, lhsT, rhs, start, stop, reads):
    K.S.op("pe", lambda e: e.matmul(out_ap, lhsT=lhsT, rhs=rhs, start=start, stop=stop),
           reads=reads, writes=[out_b])


def act(K, out_ap, in_ap, func, reads, writes, **kw):
    K.S.op("act", lambda e: e.activation(out=out_ap, in_=in_ap, func=func, **kw), reads=reads, writes=writes)


def tt(K, eng, out_ap, in0, in1, op, reads, writes):
    K.S.op(eng, lambda e: e.tensor_tensor(out=out_ap, in0=in0, in1=in1, op=op), reads=reads, writes=writes)


def stt(K, eng, out_ap, in0, scalar, in1, op0, op1, reads, writes):
    K.S.op(eng, lambda e: e.scalar_tensor_tensor(out=out_ap, in0=in0, scalar=scalar, in1=in1, op0=op0, op1=op1),
           reads=reads, writes=writes)


def ts(K, eng, out_ap, in0, s1, s2, op0, op1, reads, writes):
    if s2 is None:
        K.S.op(eng, lambda e: e.tensor_scalar(out=out_ap, in0=in0, scalar1=s1, scalar2=None, op0=op0),
               reads=reads, writes=writes)
    else:
        K.S.op(eng, lambda e: e.tensor_scalar(out=out_ap, in0=in0, scalar1=s1, scalar2=s2, op0=op0, op1=op1),
               reads=reads, writes=writes)


def cp(K, eng, out_ap, in_ap, reads, writes):
    if eng == "act":
        K.S.op("act", lambda e: e.copy(out=out_ap, in_=in_ap), reads=reads, writes=writes)
    else:
        K.S.op(eng, lambda e: e.tensor_copy(out=out_ap, in_=in_ap), reads=reads, writes=writes)


def rstd_act(K, out_ap, in_ap, scale, reads, b_out):
    act(K, out_ap, in_ap, AF.Ln, reads, [b_out], scale=scale, bias=EPS)
    act(K, out_ap, out_ap, AF.Exp, [b_out], [b_out], scale=-0.5)


def recip(K, ap, b):
    K.S.op("dve", lambda e: e.reciprocal(out=ap, in_=ap), reads=[b], writes=[b])


def load_consts(K, ph):
    S = K.S
    ident, b_ident = ph.sb("ident", [128, 128], BF16)
    S.dma("pool", ident[:], K.c_ident, writes=[b_ident])
    ones, b_ones = ph.sb("ones", [128, 128], BF16)
    S.op("dve", lambda e: e.memset(ones[:], 1.0), writes=[b_ones])
    return ident, b_ident, ones, b_ones


def load_mod(K, ph, name, row, lo, gain=None, tmp=None):
    t, b = ph.sb(name, [128, D], F32)
    reload_mod(K, t, b, row, lo, gain, tmp)
    return t, b


def reload_mod(K, t, b, row, lo, gain=None, tmp=None):
    S = K.S
    S.dma("sp", t[:], K.mod_d[row, lo:lo + D].partition_broadcast(128), reads=[K.b_mod], writes=[b])
    if gain is not None:
        g, bg = tmp
        S.dma("sp", g[:], gain.partition_broadcast(128), writes=[bg])
        stt(K, "dve", t[:], t[:], 1.0, g[:], ALU.add, ALU.mult, [b, bg], [b])


class Normer:
    def __init__(self, K, ph, ident, b_ident, nhb=4):
        self.K = K
        self.ident, self.b_ident = ident, b_ident
        self.xr = Rot(ph, "nx", [128, D], F32, 2)
        self.hb = Rot(ph, "nhb", [128, D], BF16, nhb)
        self.ss = Rot(ph, "nss", [128, 1], F32, nhb)
        self.ptn = 0

    def part1(self, src_rows, A, bA, SH, bSH, rd=()):
        K, S = self.K, self.K.S
        xt, bx = self.xr.next()
        S.dma("sp", xt[:], src_rows, reads=list(rd), writes=[bx])
        hb, bhb = self.hb.next()
        ss, bs = self.ss.next()
        act(K, hb[:], xt[:], AF.Square, [bx], [bhb, bs], accum_out=ss[:])
        act(K, ss[:], ss[:], AF.Sqrt, [bs], [bs], scale=1.0 / D, bias=EPS)
        recip(K, ss[:], bs)
        stt(K, "dve", xt[:], xt[:], ss[:, 0:1], A[:], ALU.mult, ALU.mult, [bx, bs, bA], [bx])
        tt(K, "pool", hb[:], xt[:], SH[:], ALU.add, [bx, bSH], [bhb])
        return hb, bhb

    def part2(self, hb, bhb, hT_dst, b_hT):
        K, S = self.K, self.K.S
        pt, bpt = K.pt[self.ptn % 2]
        self.ptn += 1
        for c in range(8):
            S.op("pe", lambda e, c=c, pt=pt, hb=hb: e.transpose(out=pt[:, c * 128:(c + 1) * 128],
                                                                in_=hb[:, c * 128:(c + 1) * 128],
                                                                identity=self.ident[:]),
                 reads=[bhb, self.b_ident], writes=[bpt])
        cp(K, "act", hT_dst, pt[:].rearrange("p (c n) -> p c n", c=8), [bpt], [b_hT])

    def block(self, src_rows, A, bA, SH, bSH, hT_dst, b_hT, rd=()):
        hb, bhb = self.part1(src_rows, A, bA, SH, bSH, rd)
        self.part2(hb, bhb, hT_dst, b_hT)


def phase_ada(K):
    S = K.S
    ph = Phase(K, "ada")
    cc, b_cc = ph.sb("cc", [128, 8, 2], F32)
    sl, b_sl = ph.sb("sl", [128, 8, 2], BF16)
    S.dma("sp", cc[:], K.cc, writes=[b_cc])
    act(K, sl[:], cc[:], AF.Silu, [b_cc], [b_sl])
    wr = Rot(ph, "w", [128, 8, 512], BF16, 3)
    for L in range(2):
        bias, b_bias = ph.sb("bias%d" % L, [2, 6144], F32)
        S.dma("sp", bias[:], K.ada_b[L, :].partition_broadcast(2), writes=[b_bias])
        rows, b_rows = ph.sb("rows%d" % L, [2, 6144], F32)
        for j in range(12):
            w, bw = wr.next()
            S.dma("pool", w[:], K.ada_w[L, :, j * 512:(j + 1) * 512].rearrange("(k p) n -> p k n", p=128),
                  writes=[bw])
            ps, bps = K.ps()
            for k in range(8):
                mm(K, ps[0:2, :], bps, sl[:, k, :], w[:, k, :], k == 0, k == 7, [b_sl, bw])
            tt(K, "dve", rows[:, j * 512:(j + 1) * 512], ps[0:2, :], bias[:, j * 512:(j + 1) * 512], ALU.add,
               [bps, b_bias], [b_rows])
        S.dma("sp", K.mod_d[L:L + 1, :], rows[0:1, :], reads=[b_rows], writes=[K.b_mod])
        if L == 0:
            S.dma("sp", K.mod_d[2:3, :], rows[1:2, :], reads=[b_rows], writes=[K.b_mod])
    ph.close()


def phase_pre(K):
    S = K.S
    ph = Phase(K, "pre")
    ident, b_ident, ones, b_ones = load_consts(K, ph)
    cv, b_cv = ph.sb("cv", [128, K.NCV], F32)
    S.dma("sp", cv[:], K.colvecs, writes=[b_cv])
    win, b_win = ph.sb("win", [128, 8, EVEN_IN], BF16)
    for k in range(8):
        S.dma("pool", win[:, k, :], K.a_w_in[k * 128:(k + 1) * 128, :], writes=[b_win])
    wtmp, b_wtmp = ph.sb("wtmp", [128, 3, 768], F32)
    wuq, b_wuq = ph.sb("wuq", [128, 3, 768], BF16)
    S.dma("sp", wtmp[:], K.a_w_uq.rearrange("(c p) n -> p c n", p=128), writes=[b_wtmp])
    for c in range(3):
        ts(K, "dve", wuq[:, c, :], wtmp[:, c, :], cv[:, K.CV_QLN + c:K.CV_QLN + c + 1], None, ALU.mult, None,
           [b_wtmp, b_cv], [b_wuq])
    wk, b_wk = ph.sb("wk", [128, 2, 8, 96], BF16)
    S.op("dve", lambda e: e.memset(wk[:], 0.0), writes=[b_wk])
    S.dma("sp", wtmp[:, 0:2, 0:512], K.a_w_uk.rearrange("(c p) n -> p c n", p=128), reads=[], writes=[b_wtmp])
    for c in range(2):
        ts(K, "dve", wk[:, c, :, 0:64], wtmp[:, c, 0:512].rearrange("p (h d) -> p h d", h=8),
           cv[:, K.CV_KVLN + c:K.CV_KVLN + c + 1], None, ALU.mult, None, [b_wtmp, b_cv], [b_wk])
    wuv, b_wuv = ph.sb("wuv", [128, 2, 512], BF16)
    S.dma("sp", wtmp[:, 0:2, 0:512], K.a_w_uv.rearrange("(c p) n -> p c n", p=128), reads=[], writes=[b_wtmp])
    for c in range(2):
        ts(K, "dve", wuv[:, c, :], wtmp[:, c, 0:512], cv[:, K.CV_KVLN + c:K.CV_KVLN + c + 1], None, ALU.mult, None,
           [b_wtmp, b_cv], [b_wuv])
    e32, b_e32 = ph.sb("e32", [128, 96], BF16)
    S.dma("pool", e32[:], K.c_e32, writes=[b_e32])
    rm, b_rm = ph.sb("rm", [96, 96], BF16)
    S.dma("pool", rm[:], K.c_rm, writes=[b_rm])
    csm, b_csm = ph.sb("csm", [128, 256], BF16)
    S.dma("pool", csm[:], K.c_csm, writes=[b_csm])

    A, bA = ph.sb("A", [128, D], F32)
    SH, bSH = ph.sb("SH", [128, D], F32)
    nrm = Normer(K, ph, ident, b_ident)
    hT, b_hT = ph.sb("hT", [128, 8, 512], BF16)
    cfk, b_cfk = ph.sb("cfk", [128, 2, 512], F32)
    cfq, b_cfq = ph.sb("cfq", [128, 3, 512], F32)
    sqk, b_sqk = ph.sb("sqk", [128, 2, 512], BF16)
    sqq, b_sqq = ph.sb("sqq", [128, 3, 512], BF16)
    rsk, b_rsk = ph.sb("rsk", [128, 512], F32)
    rsq, b_rsq = ph.sb("rsq", [128, 512], F32)
    cnk, b_cnk = ph.sb("cnk", [128, 2, 512], BF16)
    cnq, b_cnq = ph.sb("cnq", [128, 3, 512], BF16)
    kr, b_kr = ph.sb("kr", [128, 512], BF16)
    S.op("dve", lambda e: e.memset(kr[:], 0.0), writes=[b_kr])
    vt, b_vt = ph.sb("vt", [128, 4, 8, VS], BF16)
    S.op("dve", lambda e: e.memset(vt[:], 1.0), writes=[b_vt])
    ktt = Rot(ph, "ktt", [96, 8, 512], BF16, 1)
    qtt = Rot(ph, "qtt", [96, 8, 512], BF16, 1)
    ropec = Rot(ph, "ropec", [96, 512], F32, 1)
    ropes = Rot(ph, "ropes", [96, 512], F32, 1)
    sqh = Rot(ph, "sqh", [96, 512], BF16, 3)
    rsh = Rot(ph, "rsh", [96, 512], F32, 2)
    qnb = Rot(ph, "qnb", [96, 512], BF16, 3)
    t1r = Rot(ph, "t1r", [96, 512], F32, 2)
    t2r = Rot(ph, "t2r", [96, 512], F32, 2)
    uf, b_uf = ph.sb("uf", [128, 4, 512], BF16)
    xrt, b_xrt = ph.sb("xrt", [128, 4, 512], BF16)
    xit, b_xit = ph.sb("xit", [128, 4, 512], BF16)

    def proj_chunks(n, col0, nch, cf, b_cf, sq, b_sq):
        for c in range(nch):
            ps, bps = K.ps()
            for k in range(8):
                mm(K, ps[:, 0:n], bps, win[:, k, col0 + c * 128:col0 + (c + 1) * 128], hT[:, k, 0:n], k == 0, k == 7,
                   [b_win, b_hT])
            cp(K, "act", cf[:, c, 0:n], ps[:, 0:n], [bps], [b_cf])
            act(K, sq[:, c, 0:n], ps[:, 0:n], AF.Square, [bps], [b_sq])

    def lat_finish(n, nch, dim, cf, b_cf, sq, b_sq, rs, b_rs, cn, b_cn):
        ps, bps = K.ps()
        for c in range(nch):
            mm(K, ps[:, 0:n], bps, ones[:, :], sq[:, c, 0:n], c == 0, c == nch - 1, [b_ones, b_sq])
        rstd_act(K, rs[:, 0:n], ps[:, 0:n], 1.0 / dim, [bps], b_rs)
        for c in range(nch):
            tt(K, "dve" if c % 2 == 0 else "pool", cn[:, c, 0:n], cf[:, c, 0:n], rs[:, 0:n], ALU.mult, [b_cf, b_rs],
               [b_cn])

    groups = [("ctx", K.ctx, 0, CTX, 0)] + [("lat", K.x, g * 512, 512, CTX + g * 512) for g in range(8)]

    def p1(gi):
        kind, src, t0, n, koff = groups[gi]
        is_ctx = kind == "ctx"
        if gi <= 1:
            row = 2 if is_ctx else 0
            reload_mod(K, A, bA, row, 1 * D, gain=K.norm1_g[0, :], tmp=nrm.xr.items[0])
            reload_mod(K, SH, bSH, row, 0)
        return [nrm.part1(src[t0 + tb * 128:t0 + (tb + 1) * 128, :], A, bA, SH, bSH) for tb in range(n // 128)]

    pre = p1(0)
    for gi, (kind, src, t0, n, koff) in enumerate(groups):
        nb = n // 128
        is_ctx = kind == "ctx"
        for tb in range(nb):
            nrm.part2(pre[tb][0], pre[tb][1], hT[:, :, tb * 128:(tb + 1) * 128], b_hT)
        proj_chunks(n, 384, 2, cfk, b_cfk, sqk, b_sqk)
        ps, bps = K.ps()
        for k in range(8):
            mm(K, ps[0:32, 0:n], bps, win[:, k, 640:672], hT[:, k, 0:n], k == 0, k == 7, [b_win, b_hT])
        cp(K, "act", kr[0:32, 0:n], ps[0:32, 0:n], [bps], [b_kr])
        if not is_ctx:
            proj_chunks(n, 0, 3, cfq, b_cfq, sqq, b_sqq)
            for g in range(4):
                ps, bps = K.ps()
                for k in range(8):
                    mm(K, ps[:, :], bps, win[:, k, 672 + g * 128:672 + (g + 1) * 128], hT[:, k, :], k == 0, k == 7,
                       [b_win, b_hT])
                cp(K, "act", uf[:, g, :], ps[:, :], [bps], [b_uf])
        if gi + 1 < len(groups):
            pre = p1(gi + 1)
        lat_finish(n, 2, 256, cfk, b_cfk, sqk, b_sqk, rsk, b_rsk, cnk, b_cnk)
        if not is_ctx:
            lat_finish(n, 3, 384, cfq, b_cfq, sqq, b_sqq, rsq, b_rsq, cnq, b_cnq)
            for tb in range(4):
                for gp in range(2):
                    ps, bps = K.ps()
                    for g2 in range(2):
                        g = gp * 2 + g2
                        mm(K, ps[:, g2 * 256:(g2 + 1) * 256], bps, uf[:, g, tb * 128:(tb + 1) * 128], csm[:, :], True,
                           True, [b_uf, b_csm])
                    pv = ps[:, :].rearrange("p (g r c) -> p g r c", g=2, r=2)
                    cp(K, "act", xrt[:, tb, gp * 256:(gp + 1) * 256].rearrange("p (g c) -> p g c", g=2),
                       pv[:, :, 0, :], [bps], [b_xrt])
                    cp(K, "dve", xit[:, tb, gp * 256:(gp + 1) * 256].rearrange("p (g c) -> p g c", g=2),
                       pv[:, :, 1, :], [bps], [b_xit])
            S.dma("pool", K.XR_d[t0:t0 + 512, :].rearrange("(b p) f -> p b f", p=128), xrt[:, :, :], reads=[b_xrt],
                  writes=[K.b_XR])
            S.dma("pool", K.XI_d[t0:t0 + 512, :].rearrange("(b p) f -> p b f", p=128), xit[:, :, :], reads=[b_xit],
                  writes=[K.b_XI])
        for tb in range(nb):
            ps, bps = K.ps()
            for c in range(2):
                mm(K, ps[:, :], bps, cnk[:, c, tb * 128:(tb + 1) * 128], wuv[:, c, :], c == 0, c == 1, [b_cnk, b_wuv])
            cp(K, "act", vt[:, tb, :, 0:64], ps[:, :].rearrange("p (h d) -> p h d", h=8), [bps], [b_vt])
        S.dma("pool", K.V_d[koff:koff + n, :].rearrange("(b p) f -> p b f", p=128),
              vt[:, 0:nb, :, :].rearrange("p b h d -> p b (h d)"), reads=[b_vt], writes=[K.b_V])
        if not is_ctx:
            rc, brc = ropec.next()
            S.dma("sp", rc[:], K.c_ropec[:, t0:t0 + 512], writes=[brc])
            rsn, brsn = ropes.next()
            S.dma("sp", rsn[:], K.c_ropes[:, t0:t0 + 512], writes=[brsn])
        else:
            rc = brc = rsn = brsn = None
        kt, bkt = ktt.next()
        jobs = [("k", h) for h in range(H)]
        if not is_ctx:
            qt, bqt = qtt.next()
            jobs += [("q", h) for h in range(H)]
        st_ = {}

        def stage_a(j):
            kind_, h = jobs[j]
            ps, bps = K.ps()
            if kind_ == "k":
                for c in range(2):
                    mm(K, ps[0:96, 0:n], bps, wk[:, c, h, :], cnk[:, c, 0:n], c == 0, False, [b_wk, b_cnk])
                mm(K, ps[0:96, 0:n], bps, e32[:, :], kr[:, 0:n], False, True, [b_e32, b_kr])
            else:
                for c in range(3):
                    mm(K, ps[0:96, 0:n], bps, wuq[:, c, h * 96:(h + 1) * 96], cnq[:, c, 0:n], c == 0, c == 2,
                       [b_wuq, b_cnq])
            s1, bs1 = sqh.next()
            act(K, s1[:, 0:n], ps[0:96, 0:n], AF.Square, [bps], [bs1])
            st_[j] = dict(ps=ps, bps=bps, s1=s1, bs1=bs1)

        def stage_b(j):
            kind_, h = jobs[j]
            d_ = st_[j]
            gcol = K.CV_KN if kind_ == "k" else K.CV_QN
            dst, b_dst = (kt[:, h, 0:n], bkt) if kind_ == "k" else (qt[:, h, 0:n], bqt)
            psN, bN = K.ps()
            mm(K, psN[0:96, 0:n], bN, ones[0:96, 0:96], d_["s1"][:, 0:n], True, True, [b_ones, d_["bs1"]])
            r1, br1 = rsh.next()
            rstd_act(K, r1[:, 0:n], psN[0:96, 0:n], 1.0 / 96, [bN], br1)
            if rc is None:
                stt(K, "dve", dst, d_["ps"][0:96, 0:n], cv[0:96, gcol:gcol + 1], r1[:, 0:n], ALU.mult, ALU.mult,
                    [d_["bps"], b_cv, br1], [b_dst])
                return
            qn, bqn = qnb.next()
            stt(K, "dve", qn[:, 0:n], d_["ps"][0:96, 0:n], cv[0:96, gcol:gcol + 1], r1[:, 0:n], ALU.mult, ALU.mult,
                [d_["bps"], b_cv, br1], [bqn])
            d_["qn"], d_["bqn"] = qn, bqn

        def stage_c(j):
            kind_, h = jobs[j]
            d_ = st_.pop(j)
            if rc is None:
                return
            dst, b_dst = (kt[:, h, 0:n], bkt) if kind_ == "k" else (qt[:, h, 0:n], bqt)
            qn, bqn = d_["qn"], d_["bqn"]
            psR, bR = K.ps()
            mm(K, psR[0:96, 0:n], bR, rm[:, :], qn[:, 0:n], True, True, [b_rm, bqn])
            t1, bt1 = t1r.next()
            tt(K, "pool", t1[:, 0:n], qn[:, 0:n], rc[:, 0:n], ALU.mult, [bqn, brc], [bt1])
            t2, bt2 = t2r.next()
            tt(K, "dve", t2[:, 0:n], psR[0:96, 0:n], rsn[:, 0:n], ALU.mult, [bR, brsn], [bt2])
            tt(K, "pool", dst, t1[:, 0:n], t2[:, 0:n], ALU.add, [bt1, bt2], [b_dst])

        nj = len(jobs)
        for step in range(nj + 2):
            if step < nj:
                stage_a(step)
            if 0 <= step - 1 < nj:
                stage_b(step - 1)
            if 0 <= step - 2 < nj:
                stage_c(step - 2)
        S.dma("pool", K.KT_d[:, :, koff:koff + n], kt[:, :, 0:n], reads=[bkt], writes=[K.b_KT])
        if not is_ctx:
            S.dma("pool", K.QT_d[:, :, t0:t0 + n], qt[:, :, :], reads=[bqt], writes=[K.b_QT])
    ph.close()


def phase_fft(K):
    S = K.S
    ph = Phase(K, "fft")
    wa, b_wa = ph.sb("wa", [128, 64, 128], BF16)
    for q in range(4):
        S.dma("pool", wa[:, q * 16:(q + 1) * 16, :], K.c_wa[q * 16:(q + 1) * 16].rearrange("n k m -> k n m"),
              writes=[b_wa])
    fb, b_fb = ph.sb("fb", [128, 64], BF16)
    S.dma("pool", fb[:], K.c_fb, writes=[b_fb])
    xg, b_xg = ph.sb("xg", [128, 64, 256], BF16)
    gp_, b_gp = ph.sb("gp", [128, 64, 256], BF16)
    yt, b_yt = ph.sb("yt", [128, 2, 4096], BF16)
    for hf in range(2):
        cs = slice(hf * 256, (hf + 1) * 256)
        S.dma("sp", xg[0:64, :, :], K.XR_d[:, cs].rearrange("(a b) c -> a b c", a=64), reads=[K.b_XR], writes=[b_xg])
        S.dma("sp", xg[64:128, :, :], K.XI_d[:, cs].rearrange("(a b) c -> a b c", a=64), reads=[K.b_XI],
              writes=[b_xg])
        for n2 in range(0, 64, 2):
            ps, bps = K.ps()
            for j in range(2):
                mm(K, ps[:, j * 256:(j + 1) * 256], bps, wa[:, n2 + j, :], xg[:, n2 + j, :], True, True, [b_wa, b_xg])
            cp(K, "act" if (n2 // 2) % 2 == 0 else "dve", gp_[:, n2:n2 + 2, :],
               ps[:, :].rearrange("p (j c) -> p j c", j=2), [bps], [b_gp])
        S.dma("sp", K.GD_d[hf].rearrange("r k n c -> (r k) n c"), gp_[:, :, :], reads=[b_gp], writes=[K.b_GD])
        for r in range(2):
            S.dma("sp", xg[r * 64:(r + 1) * 64, :, :], K.GD_d[hf, r].rearrange("k n c -> n k c"), reads=[K.b_GD],
                  writes=[b_xg])
        for gl in range(2):
            for kb in range(8):
                ps, bps = K.ps()
                for j in range(8):
                    k1 = kb * 8 + j
                    mm(K, ps[:, j * 64:(j + 1) * 64], bps, xg[:, k1, gl * 128:(gl + 1) * 128], fb[:, :], True, True,
                       [b_xg, b_fb])
                dst = yt[:, gl, :].rearrange("p (a b) -> p a b", b=64)[:, :, kb * 8:(kb + 1) * 8]
                cp(K, "act" if kb % 2 == 0 else "dve", dst, ps[:, :].rearrange("p (j a) -> p a j", j=8), [bps],
                   [b_yt])
        S.dma("sp", K.YT_d[:, hf * 2:(hf + 1) * 2, :], yt[:, :, :], reads=[b_yt], writes=[K.b_YT])
    ph.close()


def phase_att(K):
    S = K.S
    ph = Phase(K, "att")
    G1, bG1 = load_mod(K, ph, "G1", 0, 2 * D)
    ktall, b_kt = ph.sb("kt", [96, 8, NK], BF16)
    for h in range(H):
        S.dma("sp", ktall[:, h, :], K.KT_d[:, h, :], reads=[K.b_KT], writes=[b_kt])
    vall, b_v = ph.sb("v", [128, 34, 8 * VS], BF16)
    for q in range(2):
        S.dma("sp", vall[:, q * 17:(q + 1) * 17, :],
              K.V_d[q * 17 * 128:(q + 1) * 17 * 128, :].rearrange("(b p) f -> p b f", p=128), reads=[K.b_V],
              writes=[b_v])
    wo_a, b_woa = ph.sb("woa", [128, 4, D], BF16)
    S.dma("pool", wo_a[:], K.a_w_out[0:512, :].rearrange("(j p) n -> p j n", p=128), writes=[b_woa])
    wo_f, b_wof = ph.sb("wof", [128, 4, D], BF16)
    S.dma("pool", wo_f[:], K.a_w_out[512:1024, :].rearrange("(g p) n -> p g n", p=128), writes=[b_wof])
    onesb, b_onesb = ph.sb("onesb", [128, 64], BF16)
    S.op("dve", lambda e: e.memset(onesb[:], 1.0), writes=[b_onesb])
    hlr = Rot(ph, "hl", [65, 1024], BF16, 2)
    qtr = Rot(ph, "qt", [96, 8, 512], BF16, 2)
    ytr = Rot(ph, "ytq", [128, 4, 512], BF16, 2)
    ptr = Rot(ph, "pT", [128, 512], BF16, 3)
    oT, b_oT = ph.sb("oT", [128, 4, 512], BF16)
    oddr = Rot(ph, "oodd", [64, 512], BF16, 2)
    our = Rot(ph, "ou", [64, 512], F32, 2)
    rcp = Rot(ph, "rcp", [65, 512], F32, 2)
    xr = Rot(ph, "x", [128, D], F32, 2)
    tmr = Rot(ph, "tm", [128, 512], F32, 2)
    scale = float(96 ** -0.5)
    nps = [0, 0]

    def ps_o():
        it = K.psl[nps[0] % 2]
        nps[0] += 1
        return it

    def ps_s():
        it = K.psl[2 + nps[1] % 3]
        nps[1] += 1
        return it

    ps_x = K.psl[5]
    LA = 2
    state = {}

    def load_q(qg):
        t0 = qg * 512
        qt, bqt = qtr.next()
        S.dma("sp", qt[:], K.QT_d[:, :, t0:t0 + 512], reads=[K.b_QT], writes=[bqt])
        ytq, bytq = ytr.next()
        S.dma("sp", ytq[:], K.YT_d[:, :, t0:t0 + 512], reads=[K.b_YT], writes=[bytq])
        state[qg] = (qt, bqt, ytq, bytq)

    items = [(qg, h, kc) for qg in range(8) for h in range(H) for kc in range(34)]
    pos = {}
    spend = {}

    def emit_s(i):
        qg, h, kc = items[i]
        if (qg, h) not in pos:
            pos[(qg, h)] = ps_o()
        qt, bqt = state[qg][0], state[qg][1]
        pss, bpss = ps_s()
        mm(K, pss[:, :], bpss, ktall[:, h, kc * 128:(kc + 1) * 128], qt[:, h, :], True, True, [b_kt, bqt])
        spend[i] = (pss, bpss)

    fin = {}

    def finalize_a(qg, h):
        po, bpo = pos.pop((qg, h))
        ou, bou = our.next()
        cp(K, "act", ou[:], po[0:64, :], [bpo], [bou])
        rc, brc = rcp.next()
        S.op("dve", lambda e, rc=rc, po=po: e.reciprocal(out=rc[64:65, :], in_=po[64:65, :]), reads=[bpo],
             writes=[brc])
        hl, bhl = hlr.next()
        cp(K, "dve", hl[64:65, 0:512], rc[64:65, :], [brc], [bhl])
        tt(K, "dve", hl[64:65, 512:1024], rc[64:65, :], hl[64:65, 0:512], ALU.subtract, [brc, bhl], [bhl])
        fin[(qg, h)] = (ou, bou, hl, bhl)

    def finalize(qg, h):
        ou, bou, hl, bhl = fin.pop((qg, h))
        pb, bpb = ps_x
        mm(K, pb[0:64, :], bpb, onesb[64:65, 0:64], hl[64:65, 0:512], True, False, [b_onesb, bhl])
        mm(K, pb[0:64, :], bpb, onesb[64:65, 0:64], hl[64:65, 512:1024], False, True, [b_onesb, bhl])
        if h % 2 == 0:
            tt(K, "dve", oT[0:64, h // 2, :], ou[:], pb[0:64, :], ALU.mult, [bou, bpb], [b_oT])
        else:
            od, bod = oddr.next()
            tt(K, "dve", od[:], ou[:], pb[0:64, :], ALU.mult, [bou, bpb], [bod])
            S.dma("sp", oT[64:128, h // 2, :], od[:], reads=[bod], writes=[b_oT])

    def outproj(qg):
        t0 = qg * 512
        ytq, bytq = state[qg][2], state[qg][3]
        for tb in range(4):
            xt, bx = xr.next()
            S.dma("sp", xt[:], K.x[t0 + tb * 128:t0 + (tb + 1) * 128, :], writes=[bx])
            for dh in range(2):
                ps, bps = ps_x
                for j in range(4):
                    mm(K, ps[:, :], bps, oT[:, j, tb * 128:(tb + 1) * 128], wo_a[:, j, dh * 512:(dh + 1) * 512],
                       j == 0, False, [b_oT, b_woa])
                for g in range(4):
                    mm(K, ps[:, :], bps, ytq[:, g, tb * 128:(tb + 1) * 128], wo_f[:, g, dh * 512:(dh + 1) * 512],
                       False, g == 3, [bytq, b_wof])
                sl = slice(dh * 512, (dh + 1) * 512)
                tm, btm = tmr.next()
                tt(K, "dve", tm[:], ps[:, :], G1[:, sl], ALU.mult, [bps, bG1], [btm])
                tt(K, "pool", xt[:, sl], xt[:, sl], tm[:], ALU.add, [bx, btm], [bx])
            S.dma("pool", K.X1_d[t0 + tb * 128:t0 + (tb + 1) * 128, :], xt[:], reads=[bx], writes=[K.b_X1])

    load_q(0)
    pending = []
    n_items = len(items)
    for i in range(min(LA, n_items)):
        emit_s(i)
    for i in range(n_items):
        qg, h, kc = items[i]
        if i + LA < n_items:
            q2, h2, k2 = items[i + LA]
            if q2 not in state:
                load_q(q2)
            emit_s(i + LA)
        pss, bpss = spend.pop(i)
        po, bpo = pos[(qg, h)]
        pT, bpT = ptr.next()
        act(K, pT[:], pss[:, :], AF.Exp, [bpss], [bpT], scale=scale)
        mm(K, po[0:65, :], bpo, vall[:, kc, h * VS:h * VS + 65], pT[:], kc == 0, kc == 33, [b_v, bpT])
        if kc == 33:
            pending.append((qg, h))
        if kc == 1:
            for (pq, phh) in pending:
                if (pq, phh) not in fin:
                    finalize_a(pq, phh)
        if kc == 20 and pending:
            for (pq, phh) in pending:
                finalize(pq, phh)
                if phh == H - 1:
                    outproj(pq)
            pending = []
    for (pq, phh) in pending:
        if (pq, phh) not in fin:
            finalize_a(pq, phh)
        finalize(pq, phh)
        if phh == H - 1:
            outproj(pq)
    ph.close()


def phase_ffn(K, L, src, b_src, dst, b_dst):
    S = K.S
    ph = Phase(K, "ffn%d" % L)
    ident, b_ident, ones, b_ones = load_consts(K, ph)
    nrm = Normer(K, ph, ident, b_ident)
    A, bA = load_mod(K, ph, "A", L, 4 * D, gain=K.norm2_g[L, :], tmp=nrm.xr.items[0])
    SH, bSH = load_mod(K, ph, "SH", L, 3 * D)
    G2, bG2 = load_mod(K, ph, "G2", L, 5 * D)
    hT, b_hT = ph.sb("hT", [128, 8, 512], BF16)
    gT, b_gT = ph.sb("gT", [128, NHC, 512], BF16)
    sar = Rot(ph, "sa", [128, 512], F32, 2)
    tmr = Rot(ph, "tm", [128, 512], F32, 2)
    def p1(tg):
        t0 = tg * 512
        return [nrm.part1(src[t0 + tb * 128:t0 + (tb + 1) * 128, :], A, bA, SH, bSH, rd=[b_src]) for tb in range(4)]

    pre = p1(0)
    w1, b_w1 = ph.sb("w1", [128, 8, DFF], BF16)
    w3, b_w3 = ph.sb("w3", [128, 8, DFF], BF16)
    w2, b_w2 = ph.sb("w2", [128, NHC, D], BF16)
    b_w1s = [Buf("w1s%d" % i) for i in range(4)]
    b_w3s = [Buf("w3s%d" % i) for i in range(4)]
    b_w2s = [Buf("w2s%d" % i) for i in range(NHC)]
    for q in range(4):
        cs = slice(q * 704, (q + 1) * 704)
        for k in range(8):
            S.dma("pool", w1[:, k, cs], K.ffn_w1[L, k * 128:(k + 1) * 128, cs], writes=[b_w1s[q]])
            S.dma("pool", w3[:, k, cs], K.ffn_w3[L, k * 128:(k + 1) * 128, cs], writes=[b_w3s[q]])
    for hc in range(NHC):
        S.dma("pool", w2[:, hc, :], K.ffn_w2[L, hc * 128:(hc + 1) * 128, :], writes=[b_w2s[hc]])
    for tg in range(8):
        t0 = tg * 512
        for tb in range(4):
            nrm.part2(pre[tb][0], pre[tb][1], hT[:, :, tb * 128:(tb + 1) * 128], b_hT)
        for hc in range(NHC):
            q = (hc * 128) // 704
            q2 = (hc * 128 + 127) // 704
            pa, bpa = K.ps()
            for k in range(8):
                mm(K, pa[:, :], bpa, w1[:, k, hc * 128:(hc + 1) * 128], hT[:, k, :], k == 0, k == 7,
                   [b_w1s[q], b_w1s[q2], b_hT])
            pb, bpb = K.ps()
            for k in range(8):
                mm(K, pb[:, :], bpb, w3[:, k, hc * 128:(hc + 1) * 128], hT[:, k, :], k == 0, k == 7,
                   [b_w3s[q], b_w3s[q2], b_hT])
            sa, bsa = sar.next()
            act(K, sa[:], pa[:, :], AF.Silu, [bpa], [bsa])
            tt(K, "dve", gT[:, hc, :], sa[:], pb[:, :], ALU.mult, [bsa, bpb], [b_gT])
        if tg + 1 < 8:
            pre = p1(tg + 1)
        for tb in range(4):
            xt, bx = nrm.xr.next()
            S.dma("sp", xt[:], src[t0 + tb * 128:t0 + (tb + 1) * 128, :], reads=[b_src], writes=[bx])
            for dh in range(2):
                ps, bps = K.ps()
                for hc in range(NHC):
                    mm(K, ps[:, :], bps, gT[:, hc, tb * 128:(tb + 1) * 128], w2[:, hc, dh * 512:(dh + 1) * 512],
                       hc == 0, hc == NHC - 1, [b_gT, b_w2s[hc]])
                sl = slice(dh * 512, (dh + 1) * 512)
                tm, btm = tmr.next()
                tt(K, "dve", tm[:], ps[:, :], G2[:, sl], ALU.mult, [bps, bG2], [btm])
                tt(K, "pool", xt[:, sl], xt[:, sl], tm[:], ALU.add, [bx, btm], [bx])
            S.dma("pool", dst[t0 + tb * 128:t0 + (tb + 1) * 128, :], xt[:], reads=[bx], writes=[b_dst])
    ph.close()


def phase_mix1(K, src, b_src, dst, b_dst):
    S = K.S
    ph = Phase(K, "mix1")
    ident, b_ident, ones, b_ones = load_consts(K, ph)
    cv, b_cv = ph.sb("cv", [128, K.NCV], F32)
    S.dma("sp", cv[:], K.colvecs, writes=[b_cv])
    conf, b_conf = ph.sb("conf", [128, 4, T + 30], BF16)
    scx, b_scx = ph.sb("scx", [128, 4, T + 2], BF16)
    sbt, b_sbt = ph.sb("sbt", [128, 4, T], BF16)
    S.op("dve", lambda e: e.memset(conf[:, :, 0:15], 0.0), writes=[b_conf])
    S.op("dve", lambda e: e.memset(conf[:, :, T + 15:T + 30], 0.0), writes=[b_conf])
    S.op("dve", lambda e: e.memset(scx[:, :, 0:1], 0.0), writes=[b_scx])
    S.op("dve", lambda e: e.memset(scx[:, :, T + 1:T + 2], 0.0), writes=[b_scx])
    p1 = Phase(K, "mix1a")
    nrm = Normer(K, p1, ident, b_ident)
    A, bA = load_mod(K, p1, "A", 1, 1 * D, gain=K.norm1_g[1, :], tmp=nrm.xr.items[0])
    SH, bSH = load_mod(K, p1, "SH", 1, 0)
    win, b_win = p1.sb("win", [128, 8, ODD_IN], BF16)
    for k in range(8):
        for q in range(2):
            S.dma("pool", win[:, k, q * 1280:(q + 1) * 1280], K.b_w_in[k * 128:(k + 1) * 128, q * 1280:(q + 1) * 1280],
                  writes=[b_win])
    hT, b_hT = p1.sb("hT", [128, 8, 512], BF16)
    sgr = Rot(p1, "sg", [128, 512], F32, 2)
    def pp1(tg):
        t0 = tg * 512
        return [nrm.part1(src[t0 + tb * 128:t0 + (tb + 1) * 128, :], A, bA, SH, bSH, rd=[b_src]) for tb in range(4)]

    pre = pp1(0)
    for tg in range(8):
        t0 = tg * 512
        for tb in range(4):
            nrm.part2(pre[tb][0], pre[tb][1], hT[:, :, tb * 128:(tb + 1) * 128], b_hT)
        if tg + 1 < 8:
            pre = pp1(tg + 1)

        def proj(fc):
            ps, bps = K.ps()
            for k in range(8):
                mm(K, ps[:, :], bps, win[:, k, fc * 128:(fc + 1) * 128], hT[:, k, :], k == 0, k == 7, [b_win, b_hT])
            return ps, bps

        for c in range(4):
            pa, bpa = proj(c)
            pg, bpg = proj(4 + c)
            sg, bsg = sgr.next()
            act(K, sg[:], pg[:, :], AF.Sigmoid, [bpg], [bsg])
            tt(K, "dve", conf[:, c, 15 + t0:15 + t0 + 512], sg[:], pa[:, :], ALU.mult, [bsg, bpa], [b_conf])
            pB, bpB = proj(8 + c)
            cp(K, "act", sbt[:, c, t0:t0 + 512], pB[:, :], [bpB], [b_sbt])
            pC, bpC = proj(12 + c)
            pX, bpX = proj(16 + c)
            sg, bsg = sgr.next()
            cp(K, "act", sg[:], pC[:, :], [bpC], [bsg])
            tt(K, "dve", scx[:, c, 1 + t0:1 + t0 + 512], sg[:], pX[:, :], ALU.mult, [bsg, bpX], [b_scx])
    p1.close()
    p2 = Phase(K, "mix1b")
    G1, bG1 = load_mod(K, p2, "G1", 1, 2 * D)
    identf, b_identf = p2.sb("identf", [128, 128], F32)
    S.dma("sp", identf[:], K.c_ident, writes=[b_identf])
    dg, b_dg = p2.sb("dg", [128, 4, 31, 128], BF16)
    dg2, b_dg2 = p2.sb("dg2", [128, 4, 3, 128], BF16)
    for c in range(4):
        for j in range(31):
            col = K.CV_DW + c * 31 + j
            ts(K, "dve" if j % 2 == 0 else "pool", dg[:, c, j, :], identf[:], cv[:, col:col + 1], None, ALU.mult, None,
               [b_identf, b_cv], [b_dg])
        for j in range(3):
            col = K.CV_SC + c * 3 + j
            ts(K, "dve", dg2[:, c, j, :], identf[:], cv[:, col:col + 1], None, ALU.mult, None, [b_identf, b_cv],
               [b_dg2])
    wo, b_wo = p2.sb("wo", [128, 8, D], BF16)
    for f in range(8):
        S.dma("pool", wo[:, f, :], K.b_w_out[f * 128:(f + 1) * 128, :], writes=[b_wo])
    oTr = Rot(p2, "oT", [128, 8, 512], BF16, 2)
    yr = Rot(p2, "y", [128, 512], F32, 3)
    ybr = Rot(p2, "yb", [128, 512], BF16, 2)
    dr = Rot(p2, "d", [128, 512], F32, 2)
    sqr = Rot(p2, "sq", [128, 512], BF16, 2)
    rsr = Rot(p2, "rs", [128, 512], F32, 2)
    xr = Rot(p2, "x", [128, D], F32, 2)
    tmr = Rot(p2, "tm", [128, 512], F32, 2)
    items = [(tg, c) for tg in range(8) for c in range(4)]
    oTs = {}
    stt_ = {}

    def st_a(i):
        tg, c = items[i]
        t0 = tg * 512
        if tg not in oTs:
            oTs[tg] = oTr.next()
        oT, b_oT = oTs[tg]
        ps, bps = K.ps()
        for j in range(31):
            mm(K, ps[:, :], bps, dg[:, c, j, :], conf[:, c, t0 + j:t0 + j + 512], j == 0, j == 30, [b_dg, b_conf])
        y, by = yr.next()
        act(K, y[:], ps[:, :], AF.Identity, [bps, b_cv], [by], bias=cv[:, K.CV_DWB + c:K.CV_DWB + c + 1])
        yb, byb = ybr.next()
        cp(K, "pool", yb[:], y[:], [by], [byb])
        ps2, bps2 = K.ps()
        for j in range(3):
            mm(K, ps2[:, :], bps2, dg2[:, c, j, :], scx[:, c, t0 + j:t0 + j + 512], j == 0, j == 2, [b_dg2, b_scx])
        tt(K, "dve", oT[:, 4 + c, :], sbt[:, c, t0:t0 + 512], ps2[:, :], ALU.mult, [b_sbt, bps2], [b_oT])
        stt_[i] = dict(y=y, by=by, yb=yb, byb=byb)

    def st_b(i):
        d_ = stt_[i]
        pm, bpm = K.ps()
        mm(K, pm[:, :], bpm, ones[:, :], d_["yb"][:], True, True, [b_ones, d_["byb"]])
        d, bd = dr.next()
        stt(K, "dve", d[:], pm[:, :], -1.0 / 128, d_["y"][:], ALU.mult, ALU.add, [bpm, d_["by"]], [bd])
        sq, bsq = sqr.next()
        act(K, sq[:], d[:], AF.Square, [bd], [bsq])
        d_.update(d=d, bd=bd, sq=sq, bsq=bsq)

    def st_c(i):
        tg, c = items[i]
        oT, b_oT = oTs[tg]
        d_ = stt_.pop(i)
        d, bd = d_["d"], d_["bd"]
        pv, bpv = K.ps()
        mm(K, pv[:, :], bpv, ones[:, :], d_["sq"][:], True, True, [b_ones, d_["bsq"]])
        rs, brs = rsr.next()
        rstd_act(K, rs[:], pv[:, :], 1.0 / 128, [bpv], brs)
        tt(K, "dve", d[:], d[:], rs[:], ALU.mult, [bd, brs], [bd])
        act(K, oT[:, c, :], d[:], AF.Silu, [bd, b_cv], [b_oT], scale=cv[:, K.CV_LNG + c:K.CV_LNG + c + 1],
            bias=cv[:, K.CV_LNB + c:K.CV_LNB + c + 1])

    def outproj(tg):
        t0 = tg * 512
        oT, b_oT = oTs.pop(tg)
        for tb in range(4):
            xt, bx = xr.next()
            S.dma("sp", xt[:], src[t0 + tb * 128:t0 + (tb + 1) * 128, :], reads=[b_src], writes=[bx])
            for dh in range(2):
                ps, bps = K.ps()
                for f in range(8):
                    mm(K, ps[:, :], bps, oT[:, f, tb * 128:(tb + 1) * 128], wo[:, f, dh * 512:(dh + 1) * 512], f == 0,
                       f == 7, [b_oT, b_wo])
                sl = slice(dh * 512, (dh + 1) * 512)
                tm, btm = tmr.next()
                tt(K, "dve", tm[:], ps[:, :], G1[:, sl], ALU.mult, [bps, bG1], [btm])
                tt(K, "pool", xt[:, sl], xt[:, sl], tm[:], ALU.add, [bx, btm], [bx])
            S.dma("pool", dst[t0 + tb * 128:t0 + (tb + 1) * 128, :], xt[:], reads=[bx], writes=[b_dst])

    ni = len(items)
    for step in range(ni + 4):
        if step < ni:
            st_a(step)
        if 0 <= step - 1 < ni:
            st_b(step - 1)
        if 0 <= step - 2 < ni:
            st_c(step - 2)
        j = step - 4
        if 0 <= j < ni and items[j][1] == 3:
            outproj(items[j][0])
    p2.close()
    ph.close()


CV_LAYOUT = {}


def colvec_layout():
    off = 0
    lay = {}
    for name, n in (("QLN", 3), ("KVLN", 2), ("QN", 1), ("KN", 1), ("DW", 4 * 31), ("DWB", 4), ("LNG", 4),
                    ("LNB", 4), ("SC", 4 * 3)):
        lay[name] = off
        off += n
    return lay, off


def build(phases=("ada", "pre", "fft", "att", "ffn0", "mix1", "ffn1"), dbg=False):
    nc = bass.Bass("TRN2", target_bir_lowering=False)
    K = Ctx()
    K.nc = nc

    def din(name, shape):
        return nc.dram_tensor(name, list(shape), F32, kind="ExternalInput").ap()

    def dscr(name, shape, dt):
        kind = "ExternalOutput" if dbg else "Internal"
        return nc.dram_tensor(name, list(shape), dt, kind=kind).ap()

    K.x = din("x", [T, D])
    K.ctx = din("ctx", [CTX, D])
    K.cc = din("cc", [128, 8, 2])
    K.ada_w = din("ada_w", [2, D, 6 * D])
    K.ada_b = din("ada_b", [2, 6 * D])
    K.norm1_g = din("norm1_g", [2, D])
    K.norm2_g = din("norm2_g", [2, D])
    K.ffn_w1 = din("ffn_w1", [2, D, DFF])
    K.ffn_w3 = din("ffn_w3", [2, D, DFF])
    K.ffn_w2 = din("ffn_w2", [2, DFF, D])
    K.a_w_in = din("a_w_in", [D, EVEN_IN])
    K.a_w_uq = din("a_w_uq", [384, 768])
    K.a_w_uk = din("a_w_uk", [256, 512])
    K.a_w_uv = din("a_w_uv", [256, 512])
    K.a_w_out = din("a_w_out", [D, D])
    K.b_w_in = din("b_w_in", [D, ODD_IN])
    K.b_w_out = din("b_w_out", [D, D])
    lay, ncv = colvec_layout()
    K.NCV = ncv
    for k_, v_ in lay.items():
        setattr(K, "CV_" + k_, v_)
    K.colvecs = din("colvecs", [128, ncv])
    K.c_ident = din("c_ident", [128, 128])
    K.c_e32 = din("c_e32", [128, 96])
    K.c_rm = din("c_rm", [96, 96])
    K.c_csm = din("c_csm", [128, 256])
    K.c_ropec = din("c_ropec", [96, T])
    K.c_ropes = din("c_ropes", [96, T])
    K.c_wa = din("c_wa", [64, 128, 128])
    K.c_fb = din("c_fb", [128, 64])
    K.out = nc.dram_tensor("out", [T, D], F32, kind="ExternalOutput").ap()
    K.b_out = Buf("out")

    K.mod_d = dscr("mod_d", [3, 6 * D], F32)
    K.b_mod = Buf("mod_d")
    K.QT_d = dscr("QT_d", [96, 8, T], BF16)
    K.b_QT = Buf("QT_d")
    K.KT_d = dscr("KT_d", [96, 8, NK], BF16)
    K.b_KT = Buf("KT_d")
    K.V_d = dscr("V_d", [NK, 8 * VS], BF16)
    K.b_V = Buf("V_d")
    K.XR_d = dscr("XR_d", [T, 512], BF16)
    K.b_XR = Buf("XR_d")
    K.XI_d = dscr("XI_d", [T, 512], BF16)
    K.b_XI = Buf("XI_d")
    K.GD_d = dscr("GD_d", [2, 2, 64, 64, 256], BF16)
    K.b_GD = Buf("GD_d")
    K.YT_d = dscr("YT_d", [128, 4, T], BF16)
    K.b_YT = Buf("YT_d")
    K.X1_d = dscr("X1_d", [T, D], F32)
    K.b_X1 = Buf("X1_d")
    K.X2_d = dscr("X2_d", [T, D], F32)
    K.b_X2 = Buf("X2_d")
    K.X3_d = dscr("X3_d", [T, D], F32)
    K.b_X3 = Buf("X3_d")

    with contextlib.ExitStack() as st:
        sems = {e: st.enter_context(nc.semaphore("s_" + e)) for e in Sched.ENGS}
        dsems = [st.enter_context(nc.semaphore("d%d" % i)) for i in range(N_DMA_SEMS)]
        K.S = Sched(nc, sems, dsems)
        psl = []
        for i in range(6):
            psl.append((st.enter_context(nc.psum_tensor("ps%d" % i, [128, 512], F32)), Buf("ps%d" % i, True)))
        K.pt = []
        for i in range(2):
            K.pt.append((st.enter_context(nc.psum_tensor("pt%d" % i, [128, 1024], BF16)), Buf("pt%d" % i, True)))
        K.psi = 0

        def ps():
            it = psl[K.psi % 6]
            K.psi += 1
            return it

        K.ps = ps
        K.psl = psl
        if "ada" in phases:
            phase_ada(K)
        if "pre" in phases:
            phase_pre(K)
        if "fft" in phases:
            phase_fft(K)
        if "att" in phases:
            phase_att(K)
        if "ffn0" in phases:
            phase_ffn(K, 0, K.X1_d, K.b_X1, K.X2_d, K.b_X2)
        if "mix1" in phases:
            phase_mix1(K, K.X2_d, K.b_X2, K.X3_d, K.b_X3)
        if "ffn1" in phases:
            phase_ffn(K, 1, K.X3_d, K.b_X3, K.out, K.b_out)
        K.S.barrier()
        K.S.emit()
    return nc


def host_consts():
    c = {}
    c["c_ident"] = np.eye(128, dtype=np.float32)
    e32 = np.zeros((128, 96), np.float32)
    e32[np.arange(32), 64 + np.arange(32)] = 1.0
    c["c_e32"] = e32
    rm = np.zeros((96, 96), np.float32)
    for i in range(16):
        rm[80 + i, 64 + i] = -1.0
        rm[64 + i, 80 + i] = 1.0
    c["c_rm"] = rm
    m = np.arange(128, dtype=np.float64)
    ang = 2 * np.pi * np.outer(m, m) / 128.0
    c["c_csm"] = np.concatenate([np.cos(ang), -np.sin(ang)], axis=1).astype(np.float32)
    rows = (np.arange(T) // 64).astype(np.float32)
    cols = (np.arange(T) % 64).astype(np.float32)
    inv = (np.float32(10000.0) ** (-np.arange(8, dtype=np.float32) / np.float32(8))).astype(np.float32)
    angp = np.concatenate([rows[:, None] * inv, cols[:, None] * inv], axis=-1).astype(np.float32)
    cos = np.cos(angp.astype(np.float64)).T
    sin = np.sin(angp.astype(np.float64)).T
    rc = np.ones((96, T), np.float64)
    rs = np.zeros((96, T), np.float64)
    rc[64:80] = cos
    rc[80:96] = cos
    rs[64:80] = sin
    rs[80:96] = sin
    c["c_ropec"] = rc.astype(np.float32)
    c["c_ropes"] = rs.astype(np.float32)
    n1 = np.arange(64, dtype=np.float64)
    k1 = np.arange(64, dtype=np.float64)
    wa = np.zeros((64, 128, 128), np.float64)
    for n2 in range(64):
        th = 2 * np.pi * (np.outer(n1, k1) / 64.0 + (n2 * k1)[None, :] / 4096.0)
        wa[n2, 0:64, 0:64] = np.cos(th)
        wa[n2, 64:128, 0:64] = np.sin(th)
        wa[n2, 0:64, 64:128] = -np.sin(th)
        wa[n2, 64:128, 64:128] = np.cos(th)
    c["c_wa"] = wa.astype(np.float32)
    n2 = np.arange(64, dtype=np.float64)
    k2 = np.arange(64, dtype=np.float64)
    ph = 2 * np.pi * np.outer(n2, k2) / 64.0
    s = 1.0 / np.sqrt(4096.0 * 128.0)
    c["c_fb"] = (np.concatenate([np.cos(ph), np.sin(ph)], axis=0) * s).astype(np.float32)
    return c


def make_in_maps(inputs):
    f = lambda a: np.ascontiguousarray(np.asarray(a, dtype=np.float32))
    lay, ncv = colvec_layout()
    cvs = np.zeros((128, ncv), np.float32)
    cvs[:, lay["QLN"]:lay["QLN"] + 3] = f(inputs["a_q_ln_g"]).reshape(3, 128).T
    cvs[:, lay["KVLN"]:lay["KVLN"] + 2] = f(inputs["a_kv_ln_g"]).reshape(2, 128).T
    cvs[0:96, lay["QN"]] = f(inputs["a_q_norm_g"]).reshape(96)
    cvs[0:96, lay["KN"]] = f(inputs["a_k_norm_g"]).reshape(96)
    dw = f(inputs["b_conf_dw"]).reshape(31, 4, 128)
    cvs[:, lay["DW"]:lay["DW"] + 124] = dw.transpose(2, 1, 0).reshape(128, 124)
    cvs[:, lay["DWB"]:lay["DWB"] + 4] = f(inputs["b_conf_dw_b"]).reshape(4, 128).T
    cvs[:, lay["LNG"]:lay["LNG"] + 4] = f(inputs["b_conf_ln_g"]).reshape(4, 128).T
    cvs[:, lay["LNB"]:lay["LNB"] + 4] = f(inputs["b_conf_ln_b"]).reshape(4, 128).T
    sc = f(inputs["b_sc_dw"]).reshape(3, 4, 128)
    cvs[:, lay["SC"]:lay["SC"] + 12] = sc.transpose(2, 1, 0).reshape(128, 12)
    shared = {
        "ada_w": f(inputs["ada_w"]), "ada_b": f(inputs["ada_b"]),
        "norm1_g": f(inputs["norm1_g"]), "norm2_g": f(inputs["norm2_g"]),
        "ffn_w1": f(inputs["ffn_w1"]), "ffn_w3": f(inputs["ffn_w3"]), "ffn_w2": f(inputs["ffn_w2"]),
        "a_w_in": f(inputs["a_w_in"]).reshape(D, EVEN_IN),
        "a_w_uq": f(inputs["a_w_uq"]).reshape(384, 768),
        "a_w_uk": f(inputs["a_w_uk"]).reshape(256, 512),
        "a_w_uv": f(inputs["a_w_uv"]).reshape(256, 512),
        "a_w_out": f(inputs["a_w_out"]).reshape(D, D),
        "b_w_in": f(inputs["b_w_in"]).reshape(D, ODD_IN),
        "b_w_out": f(inputs["b_w_out"]).reshape(D, D),
        "colvecs": cvs,
    }
    shared.update(host_consts())
    x = f(inputs["x"])
    ctx = f(inputs["ctx"])
    c = f(inputs["c"])
    c_ctx = f(inputs["c_ctx"]).reshape(D)
    maps = []
    for b in range(x.shape[0]):
        cc = np.stack([c[b].reshape(8, 128).T, c_ctx.reshape(8, 128).T], axis=-1)
        m = dict(shared)
        m["x"] = np.ascontiguousarray(x[b])
        m["ctx"] = np.ascontiguousarray(ctx[b])
        m["cc"] = np.ascontiguousarray(cc)
        maps.append(m)
    return maps


def kernel(**inputs):
    maps = make_in_maps(inputs)
    nc = build()
    res = run_bass_kernel_spmd(nc, maps, core_ids=list(range(len(maps))))
    return np.stack([np.asarray(r["out"], dtype=np.float32) for r in res.results], axis=0)
```

```python
import contextlib
import numpy as np
import concourse.bass as bass
import concourse.mybir as mybir
from concourse.bass_utils import run_bass_kernel_spmd

F32 = mybir.dt.float32
BF16 = mybir.dt.bfloat16
AF = mybir.ActivationFunctionType
ALU = mybir.AluOpType
AX = mybir.AxisListType

N_DMA_SEMS = 40
T = 4096
D = 1024
CTX = 256
NK = CTX + T
H = 8
DFF = 2816
NHC = DFF // 128
EPS = 1e-6
EVEN_IN = 1184
VS = 80
ODD_IN = 2560


class Buf:
    __slots__ = ("name", "w", "r", "excl")

    def __init__(self, name, excl=False):
        self.name = name
        self.w = None
        self.r = {}
        self.excl = excl


class Sched:
    ENGS = ("pe", "act", "dve", "pool", "sp")

    def __init__(self, nc, sems, dsems):
        self.nc = nc
        self.sems = sems
        self.dsems = dsems
        self.q = {e: [] for e in self.ENGS}
        self.cnt = {e: 0 for e in self.ENGS}
        self.seen = {e: {} for e in self.ENGS}
        self.dcnt = [0] * N_DMA_SEMS
        self.dnext = 0
        self.n_ops = 0

    def _deps(self, e, reads, writes, extra=()):
        deps = {}

        def need(ev, same_ok):
            if ev is None:
                return
            k, v = ev
            if k == e and not same_ok:
                return
            if deps.get(k, 0) < v:
                deps[k] = v

        same = e != "pe"
        for b in reads:
            need(b.w, same)
            if b.excl:
                for k, v in b.r.items():
                    need((k, v), False)
        for b in writes:
            need(b.w, same)
            for k, v in b.r.items():
                need((k, v), False)
        for ev in extra:
            need(ev, True)
        waits = []
        sn = self.seen[e]
        for k, v in deps.items():
            if sn.get(k, 0) < v:
                sn[k] = v
                waits.append((k, v))
        return waits

    def op(self, e, fn, reads=(), writes=()):
        waits = self._deps(e, reads, writes)
        self.cnt[e] += 1
        ev = (e, self.cnt[e])
        self.q[e].append((waits, fn, ev))
        for b in reads:
            if b.r.get(e, 0) < ev[1]:
                b.r[e] = ev[1]
        for b in writes:
            b.w = ev
            b.r = {}
        self.n_ops += 1
        return ev

    def dma(self, e, out_ap, in_ap, reads=(), writes=()):
        i = self.dnext
        self.dnext = (self.dnext + 1) % N_DMA_SEMS
        k = ("d", i)
        extra = []
        if self.dcnt[i] > 0:
            extra.append((k, self.dcnt[i]))
        waits = self._deps(e, reads, writes, extra)
        self.dcnt[i] += 16
        ev = (k, self.dcnt[i])

        def fn(eng, out_ap=out_ap, in_ap=in_ap):
            return eng.dma_start(out=out_ap, in_=in_ap)

        self.q[e].append((waits, fn, ev))
        for b in reads:
            b.r[k] = ev[1]
        for b in writes:
            b.w = ev
            b.r = {}
        self.n_ops += 1
        return ev

    def barrier(self):
        for e in self.ENGS:
            waits = []
            sn = self.seen[e]
            for k in self.ENGS:
                if k != e and self.cnt[k] > sn.get(k, 0):
                    sn[k] = self.cnt[k]
                    waits.append((k, self.cnt[k]))
            for i in range(N_DMA_SEMS):
                k = ("d", i)
                if self.dcnt[i] > sn.get(k, 0):
                    sn[k] = self.dcnt[i]
                    waits.append((k, self.dcnt[i]))
            self.q[e].append((waits, None, None))

    def emit(self):
        nc, sems, dsems = self.nc, self.sems, self.dsems

        def semof(k):
            return dsems[k[1]] if isinstance(k, tuple) else sems[k]

        def run(e):
            def body(eng):
                for waits, fn, ev in self.q[e]:
                    for k, v in waits:
                        eng.wait_ge(semof(k), v)
                    if fn is None:
                        continue
                    ins = fn(eng)
                    if isinstance(ev[0], tuple):
                        ins.then_inc(dsems[ev[0][1]], 16)
                    else:
                        ins.then_inc(sems[e], 1)
            return body

        with nc.Block() as block:
            block.tensor(run("pe"))
            block.scalar(run("act"))
            block.vector(run("dve"))
            block.gpsimd(run("pool"))
            block.sync(run("sp"))
        self.q = {e: [] for e in self.ENGS}


class Rot:
    def __init__(self, ph, name, shape, dt, n=2):
        self.items = [ph.sb("%s%d" % (name, i), shape, dt) for i in range(n)]
        self.i = 0

    def next(self):
        it = self.items[self.i % len(self.items)]
        self.i += 1
        return it


class Phase:
    def __init__(self, K, name):
        self.K = K
        self.name = name
        self.st = contextlib.ExitStack()
        self.n = 0

    def sb(self, name, shape, dt):
        self.n += 1
        nm = "%s_%s" % (self.name, name)
        t = self.st.enter_context(self.K.nc.sbuf_tensor(nm, list(shape), dt))
        return t, Buf(nm)

    def close(self):
        self.K.S.barrier()
        self.K.S.emit()
        self.st.close()


class Ctx:
    pass


def mm(K, out_ap, out_b, lhsT, rhs, start, stop, reads):
    K.S.op("pe", lambda e: e.matmul(out_ap, lhsT=lhsT, rhs=rhs, start=start, stop=stop),
           reads=reads, writes=[out_b])


def act(K, out_ap, in_ap, func, reads, writes, **kw):
    K.S.op("act", lambda e: e.activation(out=out_ap, in_=in_ap, func=func, **kw), reads=reads, writes=writes)


def tt(K, eng, out_ap, in0, in1, op, reads, writes):
    K.S.op(eng, lambda e: e.tensor_tensor(out=out_ap, in0=in0, in1=in1, op=op), reads=reads, writes=writes)


def stt(K, eng, out_ap, in0, scalar, in1, op0, op1, reads, writes):
    K.S.op(eng, lambda e: e.scalar_tensor_tensor(out=out_ap, in0=in0, scalar=scalar, in1=in1, op0=op0, op1=op1),
           reads=reads, writes=writes)


def ts(K, eng, out_ap, in0, s1, s2, op0, op1, reads, writes):
    if s2 is None:
        K.S.op(eng, lambda e: e.tensor_scalar(out=out_ap, in0=in0, scalar1=s1, scalar2=None, op0=op0),
               reads=reads, writes=writes)
    else:
        K.S.op(eng, lambda e: e.tensor_scalar(out=out_ap, in0=in0, scalar1=s1, scalar2=s2, op0=op0, op1=op1),
               reads=reads, writes=writes)


def cp(K, eng, out_ap, in_ap, reads, writes):
    if eng == "act":
        K.S.op("act", lambda e: e.copy(out=out_ap, in_=in_ap), reads=reads, writes=writes)
    else:
        K.S.op(eng, lambda e: e.tensor_copy(out=out_ap, in_=in_ap), reads=reads, writes=writes)


def rstd_act(K, out_ap, in_ap, scale, reads, b_out):
    act(K, out_ap, in_ap, AF.Ln, reads, [b_out], scale=scale, bias=EPS)
    act(K, out_ap, out_ap, AF.Exp, [b_out], [b_out], scale=-0.5)


def recip(K, ap, b):
    K.S.op("dve", lambda e: e.reciprocal(out=ap, in_=ap), reads=[b], writes=[b])


def load_consts(K, ph):
    S = K.S
    ident, b_ident = ph.sb("ident", [128, 128], BF16)
    S.dma("pool", ident[:], K.c_ident, writes=[b_ident])
    ones, b_ones = ph.sb("ones", [128, 128], BF16)
    S.op("dve", lambda e: e.memset(ones[:], 1.0), writes=[b_ones])
    return ident, b_ident, ones, b_ones


def load_mod(K, ph, name, row, lo, gain=None, tmp=None):
    t, b = ph.sb(name, [128, D], F32)
    reload_mod(K, t, b, row, lo, gain, tmp)
    return t, b


def reload_mod(K, t, b, row, lo, gain=None, tmp=None):
    S = K.S
    S.dma("sp", t[:], K.mod_d[row, lo:lo + D].partition_broadcast(128), reads=[K.b_mod], writes=[b])
    if gain is not None:
        g, bg = tmp
        S.dma("sp", g[:], gain.partition_broadcast(128), writes=[bg])
        stt(K, "dve", t[:], t[:], 1.0, g[:], ALU.add, ALU.mult, [b, bg], [b])


class Normer:
    def __init__(self, K, ph, ident, b_ident, nhb=4):
        self.K = K
        self.ident, self.b_ident = ident, b_ident
        self.xr = Rot(ph, "nx", [128, D], F32, 2)
        self.hb = Rot(ph, "nhb", [128, D], BF16, nhb)
        self.ss = Rot(ph, "nss", [128, 1], F32, nhb)
        self.ptn = 0

    def part1(self, src_rows, A, bA, SH, bSH, rd=()):
        K, S = self.K, self.K.S
        xt, bx = self.xr.next()
        S.dma("sp", xt[:], src_rows, reads=list(rd), writes=[bx])
        hb, bhb = self.hb.next()
        ss, bs = self.ss.next()
        act(K, hb[:], xt[:], AF.Square, [bx], [bhb, bs], accum_out=ss[:])
        act(K, ss[:], ss[:], AF.Sqrt, [bs], [bs], scale=1.0 / D, bias=EPS)
        recip(K, ss[:], bs)
        stt(K, "dve", xt[:], xt[:], ss[:, 0:1], A[:], ALU.mult, ALU.mult, [bx, bs, bA], [bx])
        tt(K, "pool", hb[:], xt[:], SH[:], ALU.add, [bx, bSH], [bhb])
        return hb, bhb

    def part2(self, hb, bhb, hT_dst, b_hT):
        K, S = self.K, self.K.S
        pt, bpt = K.pt[self.ptn % 2]
        self.ptn += 1
        for c in range(8):
            S.op("pe", lambda e, c=c, pt=pt, hb=hb: e.transpose(out=pt[:, c * 128:(c + 1) * 128],
                                                                in_=hb[:, c * 128:(c + 1) * 128],
                                                                identity=self.ident[:]),
                 reads=[bhb, self.b_ident], writes=[bpt])
        cp(K, "act", hT_dst, pt[:].rearrange("p (c n) -> p c n", c=8), [bpt], [b_hT])

    def block(self, src_rows, A, bA, SH, bSH, hT_dst, b_hT, rd=()):
        hb, bhb = self.part1(src_rows, A, bA, SH, bSH, rd)
        self.part2(hb, bhb, hT_dst, b_hT)


def phase_ada(K):
    S = K.S
    ph = Phase(K, "ada")
    cc, b_cc = ph.sb("cc", [128, 8, 2], F32)
    sl, b_sl = ph.sb("sl", [128, 8, 2], BF16)
    S.dma("sp", cc[:], K.cc, writes=[b_cc])
    act(K, sl[:], cc[:], AF.Silu, [b_cc], [b_sl])
    wr = Rot(ph, "w", [128, 8, 512], BF16, 3)
    for L in range(2):
        bias, b_bias = ph.sb("bias%d" % L, [2, 6144], F32)
        S.dma("sp", bias[:], K.ada_b[L, :].partition_broadcast(2), writes=[b_bias])
        rows, b_rows = ph.sb("rows%d" % L, [2, 6144], F32)
        for j in range(12):
            w, bw = wr.next()
            S.dma("pool", w[:], K.ada_w[L, :, j * 512:(j + 1) * 512].rearrange("(k p) n -> p k n", p=128),
                  writes=[bw])
            ps, bps = K.ps()
            for k in range(8):
                mm(K, ps[0:2, :], bps, sl[:, k, :], w[:, k, :], k == 0, k == 7, [b_sl, bw])
            tt(K, "dve", rows[:, j * 512:(j + 1) * 512], ps[0:2, :], bias[:, j * 512:(j + 1) * 512], ALU.add,
               [bps, b_bias], [b_rows])
        S.dma("sp", K.mod_d[L:L + 1, :], rows[0:1, :], reads=[b_rows], writes=[K.b_mod])
        if L == 0:
            S.dma("sp", K.mod_d[2:3, :], rows[1:2, :], reads=[b_rows], writes=[K.b_mod])
    ph.close()


def phase_pre(K):
    S = K.S
    ph = Phase(K, "pre")
    ident, b_ident, ones, b_ones = load_consts(K, ph)
    cv, b_cv = ph.sb("cv", [128, K.NCV], F32)
    S.dma("sp", cv[:], K.colvecs, writes=[b_cv])
    win, b_win = ph.sb("win", [128, 8, EVEN_IN], BF16)
    for k in range(8):
        S.dma("pool", win[:, k, :], K.a_w_in[k * 128:(k + 1) * 128, :], writes=[b_win])
    wtmp, b_wtmp = ph.sb("wtmp", [128, 3, 768], F32)
    wuq, b_wuq = ph.sb("wuq", [128, 3, 768], BF16)
    S.dma("sp", wtmp[:], K.a_w_uq.rearrange("(c p) n -> p c n", p=128), writes=[b_wtmp])
    for c in range(3):
        ts(K, "dve", wuq[:, c, :], wtmp[:, c, :], cv[:, K.CV_QLN + c:K.CV_QLN + c + 1], None, ALU.mult, None,
           [b_wtmp, b_cv], [b_wuq])
    wk, b_wk = ph.sb("wk", [128, 2, 8, 96], BF16)
    S.op("dve", lambda e: e.memset(wk[:], 0.0), writes=[b_wk])
    S.dma("sp", wtmp[:, 0:2, 0:512], K.a_w_uk.rearrange("(c p) n -> p c n", p=128), reads=[], writes=[b_wtmp])
    for c in range(2):
        ts(K, "dve", wk[:, c, :, 0:64], wtmp[:, c, 0:512].rearrange("p (h d) -> p h d", h=8),
           cv[:, K.CV_KVLN + c:K.CV_KVLN + c + 1], None, ALU.mult, None, [b_wtmp, b_cv], [b_wk])
    wuv, b_wuv = ph.sb("wuv", [128, 2, 512], BF16)
    S.dma("sp", wtmp[:, 0:2, 0:512], K.a_w_uv.rearrange("(c p) n -> p c n", p=128), reads=[], writes=[b_wtmp])
    for c in range(2):
        ts(K, "dve", wuv[:, c, :], wtmp[:, c, 0:512], cv[:, K.CV_KVLN + c:K.CV_KVLN + c + 1], None, ALU.mult, None,
           [b_wtmp, b_cv], [b_wuv])
    e32, b_e32 = ph.sb("e32", [128, 96], BF16)
    S.dma("pool", e32[:], K.c_e32, writes=[b_e32])
    rm, b_rm = ph.sb("rm", [96, 96], BF16)
    S.dma("pool", rm[:], K.c_rm, writes=[b_rm])
    csm, b_csm = ph.sb("csm", [128, 256], BF16)
    S.dma("pool", csm[:], K.c_csm, writes=[b_csm])

    A, bA = ph.sb("A", [128, D], F32)
    SH, bSH = ph.sb("SH", [128, D], F32)
    nrm = Normer(K, ph, ident, b_ident)
    hT, b_hT = ph.sb("hT", [128, 8, 512], BF16)
    cfk, b_cfk = ph.sb("cfk", [128, 2, 512], F32)
    cfq, b_cfq = ph.sb("cfq", [128, 3, 512], F32)
    sqk, b_sqk = ph.sb("sqk", [128, 2, 512], BF16)
    sqq, b_sqq = ph.sb("sqq", [128, 3, 512], BF16)
    rsk, b_rsk = ph.sb("rsk", [128, 512], F32)
    rsq, b_rsq = ph.sb("rsq", [128, 512], F32)
    cnk, b_cnk = ph.sb("cnk", [128, 2, 512], BF16)
    cnq, b_cnq = ph.sb("cnq", [128, 3, 512], BF16)
    kr, b_kr = ph.sb("kr", [128, 512], BF16)
    S.op("dve", lambda e: e.memset(kr[:], 0.0), writes=[b_kr])
    vt, b_vt = ph.sb("vt", [128, 4, 8, VS], BF16)
    S.op("dve", lambda e: e.memset(vt[:], 1.0), writes=[b_vt])
    ktt = Rot(ph, "ktt", [96, 8, 512], BF16, 1)
    qtt = Rot(ph, "qtt", [96, 8, 512], BF16, 1)
    ropec = Rot(ph, "ropec", [96, 512], F32, 1)
    ropes = Rot(ph, "ropes", [96, 512], F32, 1)
    sqh = Rot(ph, "sqh", [96, 512], BF16, 3)
    rsh = Rot(ph, "rsh", [96, 512], F32, 2)
    qnb = Rot(ph, "qnb", [96, 512], BF16, 3)
    t1r = Rot(ph, "t1r", [96, 512], F32, 2)
    t2r = Rot(ph, "t2r", [96, 512], F32, 2)
    uf, b_uf = ph.sb("uf", [128, 4, 512], BF16)
    xrt, b_xrt = ph.sb("xrt", [128, 4, 512], BF16)
    xit, b_xit = ph.sb("xit", [128, 4, 512], BF16)

    def proj_chunks(n, col0, nch, cf, b_cf, sq, b_sq):
        for c in range(nch):
            ps, bps = K.ps()
            for k in range(8):
                mm(K, ps[:, 0:n], bps, win[:, k, col0 + c * 128:col0 + (c + 1) * 128], hT[:, k, 0:n], k == 0, k == 7,
                   [b_win, b_hT])
            cp(K, "act", cf[:, c, 0:n], ps[:, 0:n], [bps], [b_cf])
            act(K, sq[:, c, 0:n], ps[:, 0:n], AF.Square, [bps], [b_sq])

    def lat_finish(n, nch, dim, cf, b_cf, sq, b_sq, rs, b_rs, cn, b_cn):
        ps, bps = K.ps()
        for c in range(nch):
            mm(K, ps[:, 0:n], bps, ones[:, :], sq[:, c, 0:n], c == 0, c == nch - 1, [b_ones, b_sq])
        rstd_act(K, rs[:, 0:n], ps[:, 0:n], 1.0 / dim, [bps], b_rs)
        for c in range(nch):
            tt(K, "dve" if c % 2 == 0 else "pool", cn[:, c, 0:n], cf[:, c, 0:n], rs[:, 0:n], ALU.mult, [b_cf, b_rs],
               [b_cn])

    groups = [("ctx", K.ctx, 0, CTX, 0)] + [("lat", K.x, g * 512, 512, CTX + g * 512) for g in range(8)]

    def p1(gi):
        kind, src, t0, n, koff = groups[gi]
        is_ctx = kind == "ctx"
        if gi <= 1:
            row = 2 if is_ctx else 0
            reload_mod(K, A, bA, row, 1 * D, gain=K.norm1_g[0, :], tmp=nrm.xr.items[0])
            reload_mod(K, SH, bSH, row, 0)
        return [nrm.part1(src[t0 + tb * 128:t0 + (tb + 1) * 128, :], A, bA, SH, bSH) for tb in range(n // 128)]

    pre = p1(0)
    for gi, (kind, src, t0, n, koff) in enumerate(groups):
        nb = n // 128
        is_ctx = kind == "ctx"
        for tb in range(nb):
            nrm.part2(pre[tb][0], pre[tb][1], hT[:, :, tb * 128:(tb + 1) * 128], b_hT)
        proj_chunks(n, 384, 2, cfk, b_cfk, sqk, b_sqk)
        ps, bps = K.ps()
        for k in range(8):
            mm(K, ps[0:32, 0:n], bps, win[:, k, 640:672], hT[:, k, 0:n], k == 0, k == 7, [b_win, b_hT])
        cp(K, "act", kr[0:32, 0:n], ps[0:32, 0:n], [bps], [b_kr])
        if not is_ctx:
            proj_chunks(n, 0, 3, cfq, b_cfq, sqq, b_sqq)
            for g in range(4):
                ps, bps = K.ps()
                for k in range(8):
                    mm(K, ps[:, :], bps, win[:, k, 672 + g * 128:672 + (g + 1) * 128], hT[:, k, :], k == 0, k == 7,
                       [b_win, b_hT])
                cp(K, "act", uf[:, g, :], ps[:, :], [bps], [b_uf])
        if gi + 1 < len(groups):
            pre = p1(gi + 1)
        lat_finish(n, 2, 256, cfk, b_cfk, sqk, b_sqk, rsk, b_rsk, cnk, b_cnk)
        if not is_ctx:
            lat_finish(n, 3, 384, cfq, b_cfq, sqq, b_sqq, rsq, b_rsq, cnq, b_cnq)
            for tb in range(4):
                for gp in range(2):
                    ps, bps = K.ps()
                    for g2 in range(2):
                        g = gp * 2 + g2
                        mm(K, ps[:, g2 * 256:(g2 + 1) * 256], bps, uf[:, g, tb * 128:(tb + 1) * 128], csm[:, :], True,
                           True, [b_uf, b_csm])
                    pv = ps[:, :].rearrange("p (g r c) -> p g r c", g=2, r=2)
                    cp(K, "act", xrt[:, tb, gp * 256:(gp + 1) * 256].rearrange("p (g c) -> p g c", g=2),
                       pv[:, :, 0, :], [bps], [b_xrt])
                    cp(K, "dve", xit[:, tb, gp * 256:(gp + 1) * 256].rearrange("p (g c) -> p g c", g=2),
                       pv[:, :, 1, :], [bps], [b_xit])
            S.dma("pool", K.XR_d[t0:t0 + 512, :].rearrange("(b p) f -> p b f", p=128), xrt[:, :, :], reads=[b_xrt],
                  writes=[K.b_XR])
            S.dma("pool", K.XI_d[t0:t0 + 512, :].rearrange("(b p) f -> p b f", p=128), xit[:, :, :], reads=[b_xit],
                  writes=[K.b_XI])
        for tb in range(nb):
            ps, bps = K.ps()
            for c in range(2):
                mm(K, ps[:, :], bps, cnk[:, c, tb * 128:(tb + 1) * 128], wuv[:, c, :], c == 0, c == 1, [b_cnk, b_wuv])
            cp(K, "act", vt[:, tb, :, 0:64], ps[:, :].rearrange("p (h d) -> p h d", h=8), [bps], [b_vt])
        S.dma("pool", K.V_d[koff:koff + n, :].rearrange("(b p) f -> p b f", p=128),
              vt[:, 0:nb, :, :].rearrange("p b h d -> p b (h d)"), reads=[b_vt], writes=[K.b_V])
        if not is_ctx:
            rc, brc = ropec.next()
            S.dma("sp", rc[:], K.c_ropec[:, t0:t0 + 512], writes=[brc])
            rsn, brsn = ropes.next()
            S.dma("sp", rsn[:], K.c_ropes[:, t0:t0 + 512], writes=[brsn])
        else:
            rc = brc = rsn = brsn = None
        kt, bkt = ktt.next()
        jobs = [("k", h) for h in range(H)]
        if not is_ctx:
            qt, bqt = qtt.next()
            jobs += [("q", h) for h in range(H)]
        st_ = {}

        def stage_a(j):
            kind_, h = jobs[j]
            ps, bps = K.ps()
            if kind_ == "k":
                for c in range(2):
                    mm(K, ps[0:96, 0:n], bps, wk[:, c, h, :], cnk[:, c, 0:n], c == 0, False, [b_wk, b_cnk])
                mm(K, ps[0:96, 0:n], bps, e32[:, :], kr[:, 0:n], False, True, [b_e32, b_kr])
            else:
                for c in range(3):
                    mm(K, ps[0:96, 0:n], bps, wuq[:, c, h * 96:(h + 1) * 96], cnq[:, c, 0:n], c == 0, c == 2,
                       [b_wuq, b_cnq])
            s1, bs1 = sqh.next()
            act(K, s1[:, 0:n], ps[0:96, 0:n], AF.Square, [bps], [bs1])
            st_[j] = dict(ps=ps, bps=bps, s1=s1, bs1=bs1)

        def stage_b(j):
            kind_, h = jobs[j]
            d_ = st_[j]
            gcol = K.CV_KN if kind_ == "k" else K.CV_QN
            dst, b_dst = (kt[:, h, 0:n], bkt) if kind_ == "k" else (qt[:, h, 0:n], bqt)
            psN, bN = K.ps()
            mm(K, psN[0:96, 0:n], bN, ones[0:96, 0:96], d_["s1"][:, 0:n], True, True, [b_ones, d_["bs1"]])
            r1, br1 = rsh.next()
            rstd_act(K, r1[:, 0:n], psN[0:96, 0:n], 1.0 / 96, [bN], br1)
            if rc is None:
                stt(K, "dve", dst, d_["ps"][0:96, 0:n], cv[0:96, gcol:gcol + 1], r1[:, 0:n], ALU.mult, ALU.mult,
                    [d_["bps"], b_cv, br1], [b_dst])
                return
            qn, bqn = qnb.next()
            stt(K, "dve", qn[:, 0:n], d_["ps"][0:96, 0:n], cv[0:96, gcol:gcol + 1], r1[:, 0:n], ALU.mult, ALU.mult,
                [d_["bps"], b_cv, br1], [bqn])
            d_["qn"], d_["bqn"] = qn, bqn

        def stage_c(j):
            kind_, h = jobs[j]
            d_ = st_.pop(j)
            if rc is None:
                return
            dst, b_dst = (kt[:, h, 0:n], bkt) if kind_ == "k" else (qt[:, h, 0:n], bqt)
            qn, bqn = d_["qn"], d_["bqn"]
            psR, bR = K.ps()
            mm(K, psR[0:96, 0:n], bR, rm[:, :], qn[:, 0:n], True, True, [b_rm, bqn])
            t1, bt1 = t1r.next()
            tt(K, "pool", t1[:, 0:n], qn[:, 0:n], rc[:, 0:n], ALU.mult, [bqn, brc], [bt1])
            t2, bt2 = t2r.next()
            tt(K, "dve", t2[:, 0:n], psR[0:96, 0:n], rsn[:, 0:n], ALU.mult, [bR, brsn], [bt2])
            tt(K, "pool", dst, t1[:, 0:n], t2[:, 0:n], ALU.add, [bt1, bt2], [b_dst])

        nj = len(jobs)
        for step in range(nj + 2):
            if step < nj:
                stage_a(step)
            if 0 <= step - 1 < nj:
                stage_b(step - 1)
            if 0 <= step - 2 < nj:
                stage_c(step - 2)
        S.dma("pool", K.KT_d[:, :, koff:koff + n], kt[:, :, 0:n], reads=[bkt], writes=[K.b_KT])
        if not is_ctx:
            S.dma("pool", K.QT_d[:, :, t0:t0 + n], qt[:, :, :], reads=[bqt], writes=[K.b_QT])
    ph.close()


def phase_fft(K):
    S = K.S
    ph = Phase(K, "fft")
    wa, b_wa = ph.sb("wa", [128, 64, 128], BF16)
    for q in range(4):
        S.dma("pool", wa[:, q * 16:(q + 1) * 16, :], K.c_wa[q * 16:(q + 1) * 16].rearrange("n k m -> k n m"),
              writes=[b_wa])
    fb, b_fb = ph.sb("fb", [128, 64], BF16)
    S.dma("pool", fb[:], K.c_fb, writes=[b_fb])
    xg, b_xg = ph.sb("xg", [128, 64, 256], BF16)
    gp_, b_gp = ph.sb("gp", [128, 64, 256], BF16)
    yt, b_yt = ph.sb("yt", [128, 2, 4096], BF16)
    for hf in range(2):
        cs = slice(hf * 256, (hf + 1) * 256)
        S.dma("sp", xg[0:64, :, :], K.XR_d[:, cs].rearrange("(a b) c -> a b c", a=64), reads=[K.b_XR], writes=[b_xg])
        S.dma("sp", xg[64:128, :, :], K.XI_d[:, cs].rearrange("(a b) c -> a b c", a=64), reads=[K.b_XI],
              writes=[b_xg])
        for n2 in range(0, 64, 2):
            ps, bps = K.ps()
            for j in range(2):
                mm(K, ps[:, j * 256:(j + 1) * 256], bps, wa[:, n2 + j, :], xg[:, n2 + j, :], True, True, [b_wa, b_xg])
            cp(K, "act" if (n2 // 2) % 2 == 0 else "dve", gp_[:, n2:n2 + 2, :],
               ps[:, :].rearrange("p (j c) -> p j c", j=2), [bps], [b_gp])
        S.dma("sp", K.GD_d[hf].rearrange("r k n c -> (r k) n c"), gp_[:, :, :], reads=[b_gp], writes=[K.b_GD])
        for r in range(2):
            S.dma("sp", xg[r * 64:(r + 1) * 64, :, :], K.GD_d[hf, r].rearrange("k n c -> n k c"), reads=[K.b_GD],
                  writes=[b_xg])
        for gl in range(2):
            for kb in range(8):
                ps, bps = K.ps()
                for j in range(8):
                    k1 = kb * 8 + j
                    mm(K, ps[:, j * 64:(j + 1) * 64], bps, xg[:, k1, gl * 128:(gl + 1) * 128], fb[:, :], True, True,
                       [b_xg, b_fb])
                dst = yt[:, gl, :].rearrange("p (a b) -> p a b", b=64)[:, :, kb * 8:(kb + 1) * 8]
                cp(K, "act" if kb % 2 == 0 else "dve", dst, ps[:, :].rearrange("p (j a) -> p a j", j=8), [bps],
                   [b_yt])
        S.dma("sp", K.YT_d[:, hf * 2:(hf + 1) * 2, :], yt[:, :, :], reads=[b_yt], writes=[K.b_YT])
    ph.close()


def phase_att(K):
    S = K.S
    ph = Phase(K, "att")
    G1, bG1 = load_mod(K, ph, "G1", 0, 2 * D)
    ktall, b_kt = ph.sb("kt", [96, 8, NK], BF16)
    for h in range(H):
        S.dma("sp", ktall[:, h, :], K.KT_d[:, h, :], reads=[K.b_KT], writes=[b_kt])
    vall, b_v = ph.sb("v", [128, 34, 8 * VS], BF16)
    for q in range(2):
        S.dma("sp", vall[:, q * 17:(q + 1) * 17, :],
              K.V_d[q * 17 * 128:(q + 1) * 17 * 128, :].rearrange("(b p) f -> p b f", p=128), reads=[K.b_V],
              writes=[b_v])
    wo_a, b_woa = ph.sb("woa", [128, 4, D], BF16)
    S.dma("pool", wo_a[:], K.a_w_out[0:512, :].rearrange("(j p) n -> p j n", p=128), writes=[b_woa])
    wo_f, b_wof = ph.sb("wof", [128, 4, D], BF16)
    S.dma("pool", wo_f[:], K.a_w_out[512:1024, :].rearrange("(g p) n -> p g n", p=128), writes=[b_wof])
    onesb, b_onesb = ph.sb("onesb", [128, 64], BF16)
    S.op("dve", lambda e: e.memset(onesb[:], 1.0), writes=[b_onesb])
    hlr = Rot(ph, "hl", [65, 1024], BF16, 2)
    qtr = Rot(ph, "qt", [96, 8, 512], BF16, 2)
    ytr = Rot(ph, "ytq", [128, 4, 512], BF16, 2)
    ptr = Rot(ph, "pT", [128, 512], BF16, 3)
    oT, b_oT = ph.sb("oT", [128, 4, 512], BF16)
    oddr = Rot(ph, "oodd", [64, 512], BF16, 2)
    our = Rot(ph, "ou", [64, 512], F32, 2)
    rcp = Rot(ph, "rcp", [65, 512], F32, 2)
    xr = Rot(ph, "x", [128, D], F32, 2)
    tmr = Rot(ph, "tm", [128, 512], F32, 2)
    scale = float(96 ** -0.5)
    nps = [0, 0]

    def ps_o():
        it = K.psl[nps[0] % 2]
        nps[0] += 1
        return it

    sbanks = [K.psl[2], K.psl[3], K.psl[4]] + [(K.pt[i][0][:].bitcast(F32), K.pt[i][1]) for i in range(2)]

    def ps_s():
        it = sbanks[nps[1] % 5]
        nps[1] += 1
        return it

    ps_x = K.psl[5]
    LA = 4
    state = {}

    def load_q(qg):
        t0 = qg * 512
        qt, bqt = qtr.next()
        S.dma("sp", qt[:], K.QT_d[:, :, t0:t0 + 512], reads=[K.b_QT], writes=[bqt])
        ytq, bytq = ytr.next()
        S.dma("sp", ytq[:], K.YT_d[:, :, t0:t0 + 512], reads=[K.b_YT], writes=[bytq])
        state[qg] = (qt, bqt, ytq, bytq)

    items = [(qg, h, kc) for qg in range(8) for h in range(H) for kc in range(34)]
    pos = {}
    spend = {}

    def emit_s(i):
        qg, h, kc = items[i]
        if (qg, h) not in pos:
            pos[(qg, h)] = ps_o()
        qt, bqt = state[qg][0], state[qg][1]
        pss, bpss = ps_s()
        mm(K, pss[:, :], bpss, ktall[:, h, kc * 128:(kc + 1) * 128], qt[:, h, :], True, True, [b_kt, bqt])
        spend[i] = (pss, bpss)

    fin = {}

    def finalize_a(qg, h):
        po, bpo = pos.pop((qg, h))
        ou, bou = our.next()
        cp(K, "act", ou[:], po[0:64, :], [bpo], [bou])
        rc, brc = rcp.next()
        S.op("dve", lambda e, rc=rc, po=po: e.reciprocal(out=rc[64:65, :], in_=po[64:65, :]), reads=[bpo],
             writes=[brc])
        hl, bhl = hlr.next()
        cp(K, "dve", hl[64:65, 0:512], rc[64:65, :], [brc], [bhl])
        tt(K, "dve", hl[64:65, 512:1024], rc[64:65, :], hl[64:65, 0:512], ALU.subtract, [brc, bhl], [bhl])
        fin[(qg, h)] = (ou, bou, hl, bhl)

    def finalize(qg, h):
        ou, bou, hl, bhl = fin.pop((qg, h))
        pb, bpb = ps_x
        mm(K, pb[0:64, :], bpb, onesb[64:65, 0:64], hl[64:65, 0:512], True, False, [b_onesb, bhl])
        mm(K, pb[0:64, :], bpb, onesb[64:65, 0:64], hl[64:65, 512:1024], False, True, [b_onesb, bhl])
        if h % 2 == 0:
            tt(K, "dve", oT[0:64, h // 2, :], ou[:], pb[0:64, :], ALU.mult, [bou, bpb], [b_oT])
        else:
            od, bod = oddr.next()
            tt(K, "dve", od[:], ou[:], pb[0:64, :], ALU.mult, [bou, bpb], [bod])
            S.dma("sp", oT[64:128, h // 2, :], od[:], reads=[bod], writes=[b_oT])

    def outproj(qg):
        t0 = qg * 512
        ytq, bytq = state[qg][2], state[qg][3]
        for tb in range(4):
            xt, bx = xr.next()
            S.dma("sp", xt[:], K.x[t0 + tb * 128:t0 + (tb + 1) * 128, :], writes=[bx])
            for dh in range(2):
                ps, bps = ps_x
                for j in range(4):
                    mm(K, ps[:, :], bps, oT[:, j, tb * 128:(tb + 1) * 128], wo_a[:, j, dh * 512:(dh + 1) * 512],
                       j == 0, False, [b_oT, b_woa])
                for g in range(4):
                    mm(K, ps[:, :], bps, ytq[:, g, tb * 128:(tb + 1) * 128], wo_f[:, g, dh * 512:(dh + 1) * 512],
                       False, g == 3, [bytq, b_wof])
                sl = slice(dh * 512, (dh + 1) * 512)
                tm, btm = tmr.next()
                tt(K, "dve", tm[:], ps[:, :], G1[:, sl], ALU.mult, [bps, bG1], [btm])
                tt(K, "pool", xt[:, sl], xt[:, sl], tm[:], ALU.add, [bx, btm], [bx])
            S.dma("pool", K.X1_d[t0 + tb * 128:t0 + (tb + 1) * 128, :], xt[:], reads=[bx], writes=[K.b_X1])

    load_q(0)
    pending = []
    n_items = len(items)
    for i in range(min(LA, n_items)):
        emit_s(i)
    for i in range(n_items):
        qg, h, kc = items[i]
        if i + LA < n_items:
            q2, h2, k2 = items[i + LA]
            if q2 not in state:
                load_q(q2)
            emit_s(i + LA)
        pss, bpss = spend.pop(i)
        po, bpo = pos[(qg, h)]
        pT, bpT = ptr.next()
        act(K, pT[:], pss[:, :], AF.Exp, [bpss], [bpT], scale=scale)
        mm(K, po[0:65, :], bpo, vall[:, kc, h * VS:h * VS + 65], pT[:], kc == 0, kc == 33, [b_v, bpT])
        if kc == 33:
            pending.append((qg, h))
        if kc == 1:
            for (pq, phh) in pending:
                if (pq, phh) not in fin:
                    finalize_a(pq, phh)
        if kc == 20 and pending:
            for (pq, phh) in pending:
                finalize(pq, phh)
                if phh == H - 1:
                    outproj(pq)
            pending = []
    for (pq, phh) in pending:
        if (pq, phh) not in fin:
            finalize_a(pq, phh)
        finalize(pq, phh)
        if phh == H - 1:
            outproj(pq)
    ph.close()


def phase_ffn(K, L, src, b_src, dst, b_dst):
    S = K.S
    ph = Phase(K, "ffn%d" % L)
    ident, b_ident, ones, b_ones = load_consts(K, ph)
    nrm = Normer(K, ph, ident, b_ident)
    A, bA = load_mod(K, ph, "A", L, 4 * D, gain=K.norm2_g[L, :], tmp=nrm.xr.items[0])
    SH, bSH = load_mod(K, ph, "SH", L, 3 * D)
    G2, bG2 = load_mod(K, ph, "G2", L, 5 * D)
    hT, b_hT = ph.sb("hT", [128, 8, 512], BF16)
    gT, b_gT = ph.sb("gT", [128, NHC, 512], BF16)
    sar = Rot(ph, "sa", [128, 512], F32, 2)
    tmr = Rot(ph, "tm", [128, 512], F32, 2)
    def p1(tg):
        t0 = tg * 512
        return [nrm.part1(src[t0 + tb * 128:t0 + (tb + 1) * 128, :], A, bA, SH, bSH, rd=[b_src]) for tb in range(4)]

    pre = p1(0)
    w1, b_w1 = ph.sb("w1", [128, 8, DFF], BF16)
    w3, b_w3 = ph.sb("w3", [128, 8, DFF], BF16)
    w2, b_w2 = ph.sb("w2", [128, NHC, D], BF16)
    b_w1s = [Buf("w1s%d" % i) for i in range(4)]
    b_w3s = [Buf("w3s%d" % i) for i in range(4)]
    b_w2s = [Buf("w2s%d" % i) for i in range(NHC)]
    for q in range(4):
        cs = slice(q * 704, (q + 1) * 704)
        for k in range(8):
            S.dma("pool", w1[:, k, cs], K.ffn_w1[L, k * 128:(k + 1) * 128, cs], writes=[b_w1s[q]])
            S.dma("pool", w3[:, k, cs], K.ffn_w3[L, k * 128:(k + 1) * 128, cs], writes=[b_w3s[q]])
    for hc in range(NHC):
        S.dma("pool", w2[:, hc, :], K.ffn_w2[L, hc * 128:(hc + 1) * 128, :], writes=[b_w2s[hc]])
    for tg in range(8):
        t0 = tg * 512
        for tb in range(4):
            nrm.part2(pre[tb][0], pre[tb][1], hT[:, :, tb * 128:(tb + 1) * 128], b_hT)
        for hc in range(NHC):
            q = (hc * 128) // 704
            q2 = (hc * 128 + 127) // 704
            pa, bpa = K.ps()
            for k in range(8):
                mm(K, pa[:, :], bpa, w1[:, k, hc * 128:(hc + 1) * 128], hT[:, k, :], k == 0, k == 7,
                   [b_w1s[q], b_w1s[q2], b_hT])
            pb, bpb = K.ps()
            for k in range(8):
                mm(K, pb[:, :], bpb, w3[:, k, hc * 128:(hc + 1) * 128], hT[:, k, :], k == 0, k == 7,
                   [b_w3s[q], b_w3s[q2], b_hT])
            sa, bsa = sar.next()
            act(K, sa[:], pa[:, :], AF.Silu, [bpa], [bsa])
            tt(K, "dve", gT[:, hc, :], sa[:], pb[:, :], ALU.mult, [bsa, bpb], [b_gT])
        if tg + 1 < 8:
            pre = p1(tg + 1)
        for tb in range(4):
            xt, bx = nrm.xr.next()
            S.dma("sp", xt[:], src[t0 + tb * 128:t0 + (tb + 1) * 128, :], reads=[b_src], writes=[bx])
            for dh in range(2):
                ps, bps = K.ps()
                for hc in range(NHC):
                    mm(K, ps[:, :], bps, gT[:, hc, tb * 128:(tb + 1) * 128], w2[:, hc, dh * 512:(dh + 1) * 512],
                       hc == 0, hc == NHC - 1, [b_gT, b_w2s[hc]])
                sl = slice(dh * 512, (dh + 1) * 512)
                tm, btm = tmr.next()
                tt(K, "dve", tm[:], ps[:, :], G2[:, sl], ALU.mult, [bps, bG2], [btm])
                tt(K, "pool", xt[:, sl], xt[:, sl], tm[:], ALU.add, [bx, btm], [bx])
            S.dma("pool", dst[t0 + tb * 128:t0 + (tb + 1) * 128, :], xt[:], reads=[bx], writes=[b_dst])
    ph.close()


def phase_mix1(K, src, b_src, dst, b_dst):
    S = K.S
    ph = Phase(K, "mix1")
    ident, b_ident, ones, b_ones = load_consts(K, ph)
    cv, b_cv = ph.sb("cv", [128, K.NCV], F32)
    S.dma("sp", cv[:], K.colvecs, writes=[b_cv])
    conf, b_conf = ph.sb("conf", [128, 4, T + 30], BF16)
    scx, b_scx = ph.sb("scx", [128, 4, T + 2], BF16)
    sbt, b_sbt = ph.sb("sbt", [128, 4, T], BF16)
    S.op("dve", lambda e: e.memset(conf[:, :, 0:15], 0.0), writes=[b_conf])
    S.op("dve", lambda e: e.memset(conf[:, :, T + 15:T + 30], 0.0), writes=[b_conf])
    S.op("dve", lambda e: e.memset(scx[:, :, 0:1], 0.0), writes=[b_scx])
    S.op("dve", lambda e: e.memset(scx[:, :, T + 1:T + 2], 0.0), writes=[b_scx])
    p1 = Phase(K, "mix1a")
    nrm = Normer(K, p1, ident, b_ident)
    A, bA = load_mod(K, p1, "A", 1, 1 * D, gain=K.norm1_g[1, :], tmp=nrm.xr.items[0])
    SH, bSH = load_mod(K, p1, "SH", 1, 0)
    win, b_win = p1.sb("win", [128, 8, ODD_IN], BF16)
    for k in range(8):
        for q in range(2):
            S.dma("pool", win[:, k, q * 1280:(q + 1) * 1280], K.b_w_in[k * 128:(k + 1) * 128, q * 1280:(q + 1) * 1280],
                  writes=[b_win])
    hT, b_hT = p1.sb("hT", [128, 8, 512], BF16)
    sgr = Rot(p1, "sg", [128, 512], F32, 2)
    def pp1(tg):
        t0 = tg * 512
        return [nrm.part1(src[t0 + tb * 128:t0 + (tb + 1) * 128, :], A, bA, SH, bSH, rd=[b_src]) for tb in range(4)]

    pre = pp1(0)
    for tg in range(8):
        t0 = tg * 512
        for tb in range(4):
            nrm.part2(pre[tb][0], pre[tb][1], hT[:, :, tb * 128:(tb + 1) * 128], b_hT)
        if tg + 1 < 8:
            pre = pp1(tg + 1)

        def proj(fc):
            ps, bps = K.ps()
            for k in range(8):
                mm(K, ps[:, :], bps, win[:, k, fc * 128:(fc + 1) * 128], hT[:, k, :], k == 0, k == 7, [b_win, b_hT])
            return ps, bps

        for c in range(4):
            pa, bpa = proj(c)
            pg, bpg = proj(4 + c)
            sg, bsg = sgr.next()
            act(K, sg[:], pg[:, :], AF.Sigmoid, [bpg], [bsg])
            tt(K, "dve", conf[:, c, 15 + t0:15 + t0 + 512], sg[:], pa[:, :], ALU.mult, [bsg, bpa], [b_conf])
            pB, bpB = proj(8 + c)
            cp(K, "act", sbt[:, c, t0:t0 + 512], pB[:, :], [bpB], [b_sbt])
            pC, bpC = proj(12 + c)
            pX, bpX = proj(16 + c)
            sg, bsg = sgr.next()
            cp(K, "act", sg[:], pC[:, :], [bpC], [bsg])
            tt(K, "dve", scx[:, c, 1 + t0:1 + t0 + 512], sg[:], pX[:, :], ALU.mult, [bsg, bpX], [b_scx])
    p1.close()
    p2 = Phase(K, "mix1b")
    G1, bG1 = load_mod(K, p2, "G1", 1, 2 * D)
    identf, b_identf = p2.sb("identf", [128, 128], F32)
    S.dma("sp", identf[:], K.c_ident, writes=[b_identf])
    dg, b_dg = p2.sb("dg", [128, 4, 31, 128], BF16)
    dg2, b_dg2 = p2.sb("dg2", [128, 4, 3, 128], BF16)
    for c in range(4):
        for j in range(31):
            col = K.CV_DW + c * 31 + j
            ts(K, "dve" if j % 2 == 0 else "pool", dg[:, c, j, :], identf[:], cv[:, col:col + 1], None, ALU.mult, None,
               [b_identf, b_cv], [b_dg])
        for j in range(3):
            col = K.CV_SC + c * 3 + j
            ts(K, "dve", dg2[:, c, j, :], identf[:], cv[:, col:col + 1], None, ALU.mult, None, [b_identf, b_cv],
               [b_dg2])
    wo, b_wo = p2.sb("wo", [128, 8, D], BF16)
    for f in range(8):
        S.dma("pool", wo[:, f, :], K.b_w_out[f * 128:(f + 1) * 128, :], writes=[b_wo])
    oTr = Rot(p2, "oT", [128, 8, 512], BF16, 2)
    yr = Rot(p2, "y", [128, 512], F32, 3)
    ybr = Rot(p2, "yb", [128, 512], BF16, 2)
    dr = Rot(p2, "d", [128, 512], F32, 2)
    sqr = Rot(p2, "sq", [128, 512], BF16, 2)
    rsr = Rot(p2, "rs", [128, 512], F32, 2)
    xr = Rot(p2, "x", [128, D], F32, 2)
    tmr = Rot(p2, "tm", [128, 512], F32, 2)
    items = [(tg, c) for tg in range(8) for c in range(4)]
    oTs = {}
    stt_ = {}

    def st_a(i):
        tg, c = items[i]
        t0 = tg * 512
        if tg not in oTs:
            oTs[tg] = oTr.next()
        oT, b_oT = oTs[tg]
        ps, bps = K.ps()
        for j in range(31):
            mm(K, ps[:, :], bps, dg[:, c, j, :], conf[:, c, t0 + j:t0 + j + 512], j == 0, j == 30, [b_dg, b_conf])
        y, by = yr.next()
        act(K, y[:], ps[:, :], AF.Identity, [bps, b_cv], [by], bias=cv[:, K.CV_DWB + c:K.CV_DWB + c + 1])
        yb, byb = ybr.next()
        cp(K, "pool", yb[:], y[:], [by], [byb])
        ps2, bps2 = K.ps()
        for j in range(3):
            mm(K, ps2[:, :], bps2, dg2[:, c, j, :], scx[:, c, t0 + j:t0 + j + 512], j == 0, j == 2, [b_dg2, b_scx])
        tt(K, "dve", oT[:, 4 + c, :], sbt[:, c, t0:t0 + 512], ps2[:, :], ALU.mult, [b_sbt, bps2], [b_oT])
        stt_[i] = dict(y=y, by=by, yb=yb, byb=byb)

    def st_b(i):
        d_ = stt_[i]
        pm, bpm = K.ps()
        mm(K, pm[:, :], bpm, ones[:, :], d_["yb"][:], True, True, [b_ones, d_["byb"]])
        d, bd = dr.next()
        stt(K, "dve", d[:], pm[:, :], -1.0 / 128, d_["y"][:], ALU.mult, ALU.add, [bpm, d_["by"]], [bd])
        sq, bsq = sqr.next()
        act(K, sq[:], d[:], AF.Square, [bd], [bsq])
        d_.update(d=d, bd=bd, sq=sq, bsq=bsq)

    def st_c(i):
        tg, c = items[i]
        oT, b_oT = oTs[tg]
        d_ = stt_.pop(i)
        d, bd = d_["d"], d_["bd"]
        pv, bpv = K.ps()
        mm(K, pv[:, :], bpv, ones[:, :], d_["sq"][:], True, True, [b_ones, d_["bsq"]])
        rs, brs = rsr.next()
        rstd_act(K, rs[:], pv[:, :], 1.0 / 128, [bpv], brs)
        tt(K, "dve", d[:], d[:], rs[:], ALU.mult, [bd, brs], [bd])
        act(K, oT[:, c, :], d[:], AF.Silu, [bd, b_cv], [b_oT], scale=cv[:, K.CV_LNG + c:K.CV_LNG + c + 1],
            bias=cv[:, K.CV_LNB + c:K.CV_LNB + c + 1])

    def outproj(tg):
        t0 = tg * 512
        oT, b_oT = oTs.pop(tg)
        for tb in range(4):
            xt, bx = xr.next()
            S.dma("sp", xt[:], src[t0 + tb * 128:t0 + (tb + 1) * 128, :], reads=[b_src], writes=[bx])
            for dh in range(2):
                ps, bps = K.ps()
                for f in range(8):
                    mm(K, ps[:, :], bps, oT[:, f, tb * 128:(tb + 1) * 128], wo[:, f, dh * 512:(dh + 1) * 512], f == 0,
                       f == 7, [b_oT, b_wo])
                sl = slice(dh * 512, (dh + 1) * 512)
                tm, btm = tmr.next()
                tt(K, "dve", tm[:], ps[:, :], G1[:, sl], ALU.mult, [bps, bG1], [btm])
                tt(K, "pool", xt[:, sl], xt[:, sl], tm[:], ALU.add, [bx, btm], [bx])
            S.dma("pool", dst[t0 + tb * 128:t0 + (tb + 1) * 128, :], xt[:], reads=[bx], writes=[b_dst])

    ni = len(items)
    for step in range(ni + 4):
        if step < ni:
            st_a(step)
        if 0 <= step - 1 < ni:
            st_b(step - 1)
        if 0 <= step - 2 < ni:
            st_c(step - 2)
        j = step - 4
        if 0 <= j < ni and items[j][1] == 3:
            outproj(items[j][0])
    p2.close()
    ph.close()


CV_LAYOUT = {}


def colvec_layout():
    off = 0
    lay = {}
    for name, n in (("QLN", 3), ("KVLN", 2), ("QN", 1), ("KN", 1), ("DW", 4 * 31), ("DWB", 4), ("LNG", 4),
                    ("LNB", 4), ("SC", 4 * 3)):
        lay[name] = off
        off += n
    return lay, off


def build(phases=("ada", "pre", "fft", "att", "ffn0", "mix1", "ffn1"), dbg=False):
    nc = bass.Bass("TRN2", target_bir_lowering=False)
    K = Ctx()
    K.nc = nc

    def din(name, shape):
        return nc.dram_tensor(name, list(shape), F32, kind="ExternalInput").ap()

    def dscr(name, shape, dt):
        kind = "ExternalOutput" if dbg else "Internal"
        return nc.dram_tensor(name, list(shape), dt, kind=kind).ap()

    K.x = din("x", [T, D])
    K.ctx = din("ctx", [CTX, D])
    K.cc = din("cc", [128, 8, 2])
    K.ada_w = din("ada_w", [2, D, 6 * D])
    K.ada_b = din("ada_b", [2, 6 * D])
    K.norm1_g = din("norm1_g", [2, D])
    K.norm2_g = din("norm2_g", [2, D])
    K.ffn_w1 = din("ffn_w1", [2, D, DFF])
    K.ffn_w3 = din("ffn_w3", [2, D, DFF])
    K.ffn_w2 = din("ffn_w2", [2, DFF, D])
    K.a_w_in = din("a_w_in", [D, EVEN_IN])
    K.a_w_uq = din("a_w_uq", [384, 768])
    K.a_w_uk = din("a_w_uk", [256, 512])
    K.a_w_uv = din("a_w_uv", [256, 512])
    K.a_w_out = din("a_w_out", [D, D])
    K.b_w_in = din("b_w_in", [D, ODD_IN])
    K.b_w_out = din("b_w_out", [D, D])
    lay, ncv = colvec_layout()
    K.NCV = ncv
    for k_, v_ in lay.items():
        setattr(K, "CV_" + k_, v_)
    K.colvecs = din("colvecs", [128, ncv])
    K.c_ident = din("c_ident", [128, 128])
    K.c_e32 = din("c_e32", [128, 96])
    K.c_rm = din("c_rm", [96, 96])
    K.c_csm = din("c_csm", [128, 256])
    K.c_ropec = din("c_ropec", [96, T])
    K.c_ropes = din("c_ropes", [96, T])
    K.c_wa = din("c_wa", [64, 128, 128])
    K.c_fb = din("c_fb", [128, 64])
    K.out = nc.dram_tensor("out", [T, D], F32, kind="ExternalOutput").ap()
    K.b_out = Buf("out")

    K.mod_d = dscr("mod_d", [3, 6 * D], F32)
    K.b_mod = Buf("mod_d")
    K.QT_d = dscr("QT_d", [96, 8, T], BF16)
    K.b_QT = Buf("QT_d")
    K.KT_d = dscr("KT_d", [96, 8, NK], BF16)
    K.b_KT = Buf("KT_d")
    K.V_d = dscr("V_d", [NK, 8 * VS], BF16)
    K.b_V = Buf("V_d")
    K.XR_d = dscr("XR_d", [T, 512], BF16)
    K.b_XR = Buf("XR_d")
    K.XI_d = dscr("XI_d", [T, 512], BF16)
    K.b_XI = Buf("XI_d")
    K.GD_d = dscr("GD_d", [2, 2, 64, 64, 256], BF16)
    K.b_GD = Buf("GD_d")
    K.YT_d = dscr("YT_d", [128, 4, T], BF16)
    K.b_YT = Buf("YT_d")
    K.X1_d = dscr("X1_d", [T, D], F32)
    K.b_X1 = Buf("X1_d")
    K.X2_d = dscr("X2_d", [T, D], F32)
    K.b_X2 = Buf("X2_d")
    K.X3_d = dscr("X3_d", [T, D], F32)
    K.b_X3 = Buf("X3_d")

    with contextlib.ExitStack() as st:
        sems = {e: st.enter_context(nc.semaphore("s_" + e)) for e in Sched.ENGS}
        dsems = [st.enter_context(nc.semaphore("d%d" % i)) for i in range(N_DMA_SEMS)]
        K.S = Sched(nc, sems, dsems)
        psl = []
        for i in range(6):
            psl.append((st.enter_context(nc.psum_tensor("ps%d" % i, [128, 512], F32)), Buf("ps%d" % i, True)))
        K.pt = []
        for i in range(2):
            K.pt.append((st.enter_context(nc.psum_tensor("pt%d" % i, [128, 1024], BF16)), Buf("pt%d" % i, True)))
        K.psi = 0

        def ps():
            it = psl[K.psi % 6]
            K.psi += 1
            return it

        K.ps = ps
        K.psl = psl
        if "ada" in phases:
            phase_ada(K)
        if "pre" in phases:
            phase_pre(K)
        if "fft" in phases:
            phase_fft(K)
        if "att" in phases:
            phase_att(K)
        if "ffn0" in phases:
            phase_ffn(K, 0, K.X1_d, K.b_X1, K.X2_d, K.b_X2)
        if "mix1" in phases:
            phase_mix1(K, K.X2_d, K.b_X2, K.X3_d, K.b_X3)
        if "ffn1" in phases:
            phase_ffn(K, 1, K.X3_d, K.b_X3, K.out, K.b_out)
        K.S.barrier()
        K.S.emit()
    return nc


def host_consts():
    c = {}
    c["c_ident"] = np.eye(128, dtype=np.float32)
    e32 = np.zeros((128, 96), np.float32)
    e32[np.arange(32), 64 + np.arange(32)] = 1.0
    c["c_e32"] = e32
    rm = np.zeros((96, 96), np.float32)
    for i in range(16):
        rm[80 + i, 64 + i] = -1.0
        rm[64 + i, 80 + i] = 1.0
    c["c_rm"] = rm
    m = np.arange(128, dtype=np.float64)
    ang = 2 * np.pi * np.outer(m, m) / 128.0
    c["c_csm"] = np.concatenate([np.cos(ang), -np.sin(ang)], axis=1).astype(np.float32)
    rows = (np.arange(T) // 64).astype(np.float32)
    cols = (np.arange(T) % 64).astype(np.float32)
    inv = (np.float32(10000.0) ** (-np.arange(8, dtype=np.float32) / np.float32(8))).astype(np.float32)
    angp = np.concatenate([rows[:, None] * inv, cols[:, None] * inv], axis=-1).astype(np.float32)
    cos = np.cos(angp.astype(np.float64)).T
    sin = np.sin(angp.astype(np.float64)).T
    rc = np.ones((96, T), np.float64)
    rs = np.zeros((96, T), np.float64)
    rc[64:80] = cos
    rc[80:96] = cos
    rs[64:80] = sin
    rs[80:96] = sin
    c["c_ropec"] = rc.astype(np.float32)
    c["c_ropes"] = rs.astype(np.float32)
    n1 = np.arange(64, dtype=np.float64)
    k1 = np.arange(64, dtype=np.float64)
    wa = np.zeros((64, 128, 128), np.float64)
    for n2 in range(64):
        th = 2 * np.pi * (np.outer(n1, k1) / 64.0 + (n2 * k1)[None, :] / 4096.0)
        wa[n2, 0:64, 0:64] = np.cos(th)
        wa[n2, 64:128, 0:64] = np.sin(th)
        wa[n2, 0:64, 64:128] = -np.sin(th)
        wa[n2, 64:128, 64:128] = np.cos(th)
    c["c_wa"] = wa.astype(np.float32)
    n2 = np.arange(64, dtype=np.float64)
    k2 = np.arange(64, dtype=np.float64)
    ph = 2 * np.pi * np.outer(n2, k2) / 64.0
    s = 1.0 / np.sqrt(4096.0 * 128.0)
    c["c_fb"] = (np.concatenate([np.cos(ph), np.sin(ph)], axis=0) * s).astype(np.float32)
    return c


def make_in_maps(inputs):
    f = lambda a: np.ascontiguousarray(np.asarray(a, dtype=np.float32))
    lay, ncv = colvec_layout()
    cvs = np.zeros((128, ncv), np.float32)
    cvs[:, lay["QLN"]:lay["QLN"] + 3] = f(inputs["a_q_ln_g"]).reshape(3, 128).T
    cvs[:, lay["KVLN"]:lay["KVLN"] + 2] = f(inputs["a_kv_ln_g"]).reshape(2, 128).T
    cvs[0:96, lay["QN"]] = f(inputs["a_q_norm_g"]).reshape(96)
    cvs[0:96, lay["KN"]] = f(inputs["a_k_norm_g"]).reshape(96)
    dw = f(inputs["b_conf_dw"]).reshape(31, 4, 128)
    cvs[:, lay["DW"]:lay["DW"] + 124] = dw.transpose(2, 1, 0).reshape(128, 124)
    cvs[:, lay["DWB"]:lay["DWB"] + 4] = f(inputs["b_conf_dw_b"]).reshape(4, 128).T
    cvs[:, lay["LNG"]:lay["LNG"] + 4] = f(inputs["b_conf_ln_g"]).reshape(4, 128).T
    cvs[:, lay["LNB"]:lay["LNB"] + 4] = f(inputs["b_conf_ln_b"]).reshape(4, 128).T
    sc = f(inputs["b_sc_dw"]).reshape(3, 4, 128)
    cvs[:, lay["SC"]:lay["SC"] + 12] = sc.transpose(2, 1, 0).reshape(128, 12)
    shared = {
        "ada_w": f(inputs["ada_w"]), "ada_b": f(inputs["ada_b"]),
        "norm1_g": f(inputs["norm1_g"]), "norm2_g": f(inputs["norm2_g"]),
        "ffn_w1": f(inputs["ffn_w1"]), "ffn_w3": f(inputs["ffn_w3"]), "ffn_w2": f(inputs["ffn_w2"]),
        "a_w_in": f(inputs["a_w_in"]).reshape(D, EVEN_IN),
        "a_w_uq": f(inputs["a_w_uq"]).reshape(384, 768),
        "a_w_uk": f(inputs["a_w_uk"]).reshape(256, 512),
        "a_w_uv": f(inputs["a_w_uv"]).reshape(256, 512),
        "a_w_out": f(inputs["a_w_out"]).reshape(D, D),
        "b_w_in": f(inputs["b_w_in"]).reshape(D, ODD_IN),
        "b_w_out": f(inputs["b_w_out"]).reshape(D, D),
        "colvecs": cvs,
    }
    shared.update(host_consts())
    x = f(inputs["x"])
    ctx = f(inputs["ctx"])
    c = f(inputs["c"])
    c_ctx = f(inputs["c_ctx"]).reshape(D)
    maps = []
    for b in range(x.shape[0]):
        cc = np.stack([c[b].reshape(8, 128).T, c_ctx.reshape(8, 128).T], axis=-1)
        m = dict(shared)
        m["x"] = np.ascontiguousarray(x[b])
        m["ctx"] = np.ascontiguousarray(ctx[b])
        m["cc"] = np.ascontiguousarray(cc)
        maps.append(m)
    return maps


def kernel(**inputs):
    maps = make_in_maps(inputs)
    nc = build()
    res = run_bass_kernel_spmd(nc, maps, core_ids=list(range(len(maps))))
    return np.stack([np.asarray(r["out"], dtype=np.float32) for r in res.results], axis=0)
```
